# Optimizing a Trainium2 kernel written in Bass

```python
import jax, jax.numpy as jnp
from jax import lax
import numpy as np

D_MODEL = 1024
BATCH = 8
SEQ = 4096
DEPTH = 1

N_RET_HEADS = 4
RET_QK_DIM = 128
RET_V_DIM = 2 * RET_QK_DIM
RET_QK_W = N_RET_HEADS * RET_QK_DIM
RET_V_W = N_RET_HEADS * RET_V_DIM
CHUNK = 128
ROPE_BASE = 10000.0
POOL_WINDOWS = (2, 4, 8, 16)
POOL_GROUPS = len(POOL_WINDOWS)
POOL_GROUP_DIM = 128
POOL_W = POOL_GROUPS * POOL_GROUP_DIM
D_FF = 4 * D_MODEL
IN_COLS = 2 * RET_QK_W + 2 * RET_V_W + POOL_W + 2 * D_MODEL
EPS = 1e-6

kernel_name = "hybrid_retention_pool_gated_block"


def rmsnorm(x, g):
    xf = x.astype(jnp.float32)
    y = xf * lax.rsqrt(jnp.mean(xf * xf, axis=-1, keepdims=True) + EPS)
    return (y * g.astype(jnp.float32)).astype(x.dtype)


def rotary(x, pos):
    half = x.shape[-1] // 2
    inv_freq = ROPE_BASE ** (-jnp.arange(half, dtype=jnp.float32) * 2.0 / x.shape[-1])
    ang = pos.astype(jnp.float32)[:, None] * inv_freq[None, :]
    cos = jnp.cos(ang)[None, :, None, :]
    sin = jnp.sin(ang)[None, :, None, :]
    x1, x2 = x[..., :half], x[..., half:]
    return jnp.concatenate([x1 * cos - x2 * sin, x1 * sin + x2 * cos], axis=-1)


def retention_chunkwise(q, k, v):
    B, S, H, dk = q.shape
    dv = v.shape[-1]
    N = S // CHUNK
    log_gamma = jnp.log(1.0 - jnp.exp2(-5.0 - jnp.arange(H, dtype=jnp.float32)))
    qc = q.reshape(B, N, CHUNK, H, dk)
    kc = k.reshape(B, N, CHUNK, H, dk)
    vc = v.reshape(B, N, CHUNK, H, dv)
    idx = jnp.arange(CHUNK, dtype=jnp.float32)
    diff = idx[:, None] - idx[None, :]
    inner_decay = jnp.where(diff[None] >= 0,
                            jnp.exp(log_gamma[:, None, None] * jnp.maximum(diff, 0.0)[None]), 0.0)
    scores = jnp.einsum('bnchd,bnmhd->bnhcm', qc, kc) * inner_decay[None, None]
    inner = jnp.einsum('bnhcm,bnmhe->bnche', scores, vc)
    zeta = jnp.exp(log_gamma[:, None] * (CHUNK - 1.0 - idx)[None, :])
    xi = jnp.exp(log_gamma[:, None] * (idx + 1.0)[None, :])
    chunk_kv = jnp.einsum('bnmhd,hm,bnmhe->nbhde', kc, zeta, vc)
    chunk_decay = jnp.exp(log_gamma * CHUNK)[None, :, None, None]

    def step(state, kv):
        return state * chunk_decay + kv, state

    init = jnp.zeros((B, H, dk, dv), jnp.float32)
    _, prev_states = lax.scan(step, init, chunk_kv)
    cross = jnp.einsum('bnchd,hc,nbhde->bnche', qc, xi, prev_states)
    return (inner + cross).reshape(B, S, H, dv)


def multiscale_pool(u, w_grp, scale):
    B, S, _ = u.shape
    uf = u.astype(jnp.float32)
    cs = jnp.cumsum(uf, axis=1)
    t = jnp.arange(S)
    outs = []
    for g, w in enumerate(POOL_WINDOWS):
        sl = slice(g * POOL_GROUP_DIM, (g + 1) * POOL_GROUP_DIM)
        csg = cs[..., sl]
        shifted = jnp.pad(csg, ((0, 0), (w, 0), (0, 0)))[:, :S]
        cnt = jnp.minimum(t + 1, w).astype(jnp.float32)[None, :, None]
        outs.append((csg - shifted) / cnt - uf[..., sl])
    p = jnp.stack(outs, axis=2)
    y = jnp.einsum('bsgc,gcd->bsgd', p, w_grp.astype(jnp.float32)).reshape(B, S, POOL_W)
    return (y * scale.astype(jnp.float32)).astype(u.dtype)


def hybrid_layer(x, ln_mix, w_in, b_gate, gn_gain, w_ret_up, pool_w, pool_scale,
                 w_pool_up, w_o, ln_mlp, w_up, w_down):
    B, S, D = x.shape
    h = rmsnorm(x, ln_mix)
    z = h @ w_in
    o = 0
    q = z[..., o:o + RET_QK_W]; o += RET_QK_W
    k = z[..., o:o + RET_QK_W]; o += RET_QK_W
    v = z[..., o:o + RET_V_W]; o += RET_V_W
    g = z[..., o:o + RET_V_W]; o += RET_V_W
    pu = z[..., o:o + POOL_W]; o += POOL_W
    gate_logits = z[..., o:o + 2 * D]

    pos = jnp.arange(S)
    qh = rotary(q.astype(jnp.float32).reshape(B, S, N_RET_HEADS, RET_QK_DIM), pos) * (RET_QK_DIM ** -0.5)
    kh = rotary(k.astype(jnp.float32).reshape(B, S, N_RET_HEADS, RET_QK_DIM), pos)
    vh = v.astype(jnp.float32).reshape(B, S, N_RET_HEADS, RET_V_DIM)
    ret = retention_chunkwise(qh, kh, vh)
    mu = jnp.mean(ret, axis=-1, keepdims=True)
    var = jnp.mean(jnp.square(ret - mu), axis=-1, keepdims=True)
    ret_n = ((ret - mu) * lax.rsqrt(var + EPS)).reshape(B, S, RET_V_W) * gn_gain.astype(jnp.float32)
    y_ret = (jax.nn.silu(g.astype(jnp.float32)) * ret_n).astype(x.dtype) @ w_ret_up

    y_pool = multiscale_pool(pu, pool_w, pool_scale) @ w_pool_up

    gates = jax.nn.sigmoid(gate_logits + b_gate)
    mixed = gates[..., :D] * y_ret + gates[..., D:] * y_pool
    x = x + mixed @ w_o

    h2 = rmsnorm(x, ln_mlp)
    x = x + jnp.square(jax.nn.relu(h2 @ w_up)) @ w_down
    return x


def setup_inputs(seed: int = 0) -> dict:
    key = jax.random.key(seed)
    ks = jax.random.split(key, 16)
    L = DEPTH
    f32 = jnp.float32

    def nrm(k, shape, fan_in):
        return jax.random.normal(k, shape, f32) * (fan_in ** -0.5)

    def gain(k, shape):
        return 1.0 + 0.02 * jax.random.normal(k, shape, f32)

    return {
        "x": jax.random.normal(ks[0], (BATCH, SEQ, D_MODEL), f32),
        "ln_mix": gain(ks[1], (L, D_MODEL)),
        "w_in": nrm(ks[2], (L, D_MODEL, IN_COLS), D_MODEL),
        "b_gate": 0.02 * jax.random.normal(ks[3], (L, 2 * D_MODEL), f32),
        "gn_gain": gain(ks[4], (L, RET_V_W)),
        "w_ret_up": nrm(ks[5], (L, RET_V_W, D_MODEL), RET_V_W),
        "pool_w": nrm(ks[6], (L, POOL_GROUPS, POOL_GROUP_DIM, POOL_GROUP_DIM), POOL_GROUP_DIM),
        "pool_scale": gain(ks[7], (L, POOL_W)),
        "w_pool_up": nrm(ks[8], (L, POOL_W, D_MODEL), POOL_W),
        "w_o": nrm(ks[9], (L, D_MODEL, D_MODEL), D_MODEL),
        "ln_mlp": gain(ks[10], (L, D_MODEL)),
        "w_up": nrm(ks[11], (L, D_MODEL, D_FF), D_MODEL),
        "w_down": nrm(ks[12], (L, D_FF, D_MODEL), D_FF),
        "ln_final": gain(ks[13], (D_MODEL,)),
    }


def reference(x, ln_mix, w_in, b_gate, gn_gain, w_ret_up, pool_w, pool_scale,
              w_pool_up, w_o, ln_mlp, w_up, w_down, ln_final):
    for l in range(DEPTH):
        x = hybrid_layer(x, ln_mix[l], w_in[l], b_gate[l], gn_gain[l], w_ret_up[l],
                         pool_w[l], pool_scale[l], w_pool_up[l], w_o[l], ln_mlp[l],
                         w_up[l], w_down[l])
    return rmsnorm(x, ln_final)
```

```python
import numpy as np
import ml_dtypes
from contextlib import ExitStack
import concourse.bass as bass
import concourse.mybir as mybir
from concourse.bass_utils import run_bass_kernel_spmd

F32 = mybir.dt.float32
BF16 = mybir.dt.bfloat16
I32 = mybir.dt.int32
AF = mybir.ActivationFunctionType
ALU = mybir.AluOpType

_ESZ = {F32: 4, BF16: 2, I32: 4}
_G = 256


def ap_keys(ap):
    if isinstance(ap, (tuple, str)):
        return [ap]
    t = ap.tensor
    row = 1
    for s in list(t.shape)[1:]:
        row *= int(s)
    esz = _ESZ[ap.dtype]
    off = int(ap.offset) % row
    hi = off
    for st, cnt in list(ap.ap)[1:]:
        hi += (int(cnt) - 1) * int(st)
    hi += 1
    name = t.name
    if name.startswith("ps"):
        return [(name, 0)]
    return [(name, b) for b in range(off * esz // _G, (hi * esz - 1) // _G + 1)]


class Chan:
    def __init__(self, sem):
        self.sem = sem
        self.count = 0


class Op:
    __slots__ = ("eng", "fn", "deps", "chan", "chanval", "semval", "needs_inc", "idx")


class Sched:
    ENGS = ("pe", "act", "dve", "pool", "sp")

    def __init__(self):
        self.ops = []
        self.lastw = {}
        self.readers = {}

    def add(self, eng, fn, reads=(), writes=(), chan=None):
        op = Op()
        op.eng = eng
        op.fn = fn
        op.chan = chan
        op.chanval = None
        op.semval = None
        op.needs_inc = False
        op.idx = len(self.ops)
        if chan is not None:
            chan.count += 16
            op.chanval = chan.count
        deps = {}
        rk = []
        for a in reads:
            rk.extend(ap_keys(a))
        wk = []
        for a in writes:
            wk.extend(ap_keys(a))
        psr = [k for k in rk if isinstance(k[0], str) and k[0].startswith("ps") and k[1] == 0 and len(k) == 2]
        if psr:
            rk = [k for k in rk if k not in psr]
            wk = wk + [k for k in psr if k not in wk]
        for k in rk:
            w = self.lastw.get(k)
            if w is not None:
                deps[w.idx] = w
        for k in wk:
            w = self.lastw.get(k)
            if w is not None:
                deps[w.idx] = w
            r = self.readers.get(k)
            if r:
                for o in r.values():
                    deps[o.idx] = o
        rkey = eng if chan is None else ("dma", id(chan))
        for k in rk:
            self.readers.setdefault(k, {})[rkey] = op
        for k in wk:
            self.lastw[k] = op
            self.readers[k] = {}
        deps.pop(op.idx, None)
        best = {}
        for d in deps.values():
            if d.eng == "pe" and eng == "pe" and d.chan is None and chan is None:
                continue
            k = d.eng if d.chan is None else ("dma", id(d.chan))
            if k not in best or best[k].idx < d.idx:
                best[k] = d
        op.deps = list(best.values())
        for d in op.deps:
            if d.chan is None:
                d.needs_inc = True
        self.ops.append(op)
        return op

    def mark(self, name):
        if not hasattr(self, "marks"):
            self.marks = []
        self.marks.append((name, len(self.ops)))

    def emit(self, nc, sems, final_chans=(), limit=None):
        if limit is not None:
            self.ops = self.ops[:limit]
            chmax = {}
            for op in self.ops:
                if op.chan is not None:
                    chmax[id(op.chan)] = (op.chan, op.chanval)
            final_chans = []
            for ch, v in chmax.values():
                c = Chan(ch.sem)
                c.count = v
                final_chans.append(c)
        cnt = {e: 0 for e in self.ENGS}
        for op in self.ops:
            if op.chan is None and op.needs_inc:
                cnt[op.eng] += 1
                op.semval = cnt[op.eng]
        per_eng = {e: [o for o in self.ops if o.eng == e] for e in self.ENGS}
        nwaits = {e: 0 for e in self.ENGS}

        def run(e, handle):
            waited = {}
            for op in per_eng[e]:
                need = {}
                for d in op.deps:
                    if d.chan is not None:
                        s, v = d.chan.sem, d.chanval
                    else:
                        s, v = sems[d.eng], d.semval
                    key = id(s)
                    if waited.get(key, 0) >= v:
                        continue
                    if key not in need or need[key][1] < v:
                        need[key] = (s, v)
                for key, (s, v) in need.items():
                    handle.wait_ge(s, v)
                    waited[key] = v
                    nwaits[e] += 1
                ins = op.fn(handle)
                if op.chan is not None:
                    ins.then_inc(op.chan.sem, 16)
                elif op.needs_inc:
                    ins.then_inc(sems[e], 1)
            if e == "sp":
                for ch in final_chans:
                    if ch.count:
                        handle.wait_ge(ch.sem, ch.count)

        with nc.Block() as block:
            @block.tensor
            def _(h):
                run("pe", h)

            @block.scalar
            def _(h):
                run("act", h)

            @block.vector
            def _(h):
                run("dve", h)

            @block.gpsimd
            def _(h):
                run("pool", h)

            @block.sync
            def _(h):
                run("sp", h)
        self.stats = {e: (len(per_eng[e]), cnt[e], nwaits[e]) for e in self.ENGS}


D = 1024
SEQ = 4096
NB = 8
H = 4
DK = 128
DV = 256
TT = 512
NTB = TT // 128
NT = SEQ // TT
NCH = 32
NRING = 4
EPS = 1e-6
GAM = [1.0 - 2.0 ** (-5.0 - h) for h in range(H)]
MAGIC = 1597463007.0


def build_program(SEQ=SEQ, limit=None):
    NT = SEQ // TT
    nc = bass.Bass("TRN2", target_bir_lowering=False)
    x_d = nc.dram_tensor("x", [SEQ, D], F32, kind="ExternalInput").ap()
    wall_d = nc.dram_tensor("wall", [NCH, 128, 4096], F32, kind="ExternalInput").ap()
    cos_d = nc.dram_tensor("cosT", [SEQ, 64], F32, kind="ExternalInput").ap()
    sin_d = nc.dram_tensor("sinT", [SEQ, 64], F32, kind="ExternalInput").ap()
    cf_d = nc.dram_tensor("cf", [128, 1536], F32, kind="ExternalInput").ap()
    cb_d = nc.dram_tensor("cb", [128, 1152], BF16, kind="ExternalInput").ap()
    pw_d = nc.dram_tensor("pw", [128, 512], F32, kind="ExternalInput").ap()
    out_d = nc.dram_tensor("out", [SEQ, D], F32, kind="ExternalOutput").ap()
    wscr_d = nc.dram_tensor("wscr", [NCH, 128, 4096], BF16, kind="Internal").ap()

    S = Sched()
    with ExitStack() as es:
        def sb(name, shape, dt):
            return es.enter_context(nc.sbuf_tensor(name, shape, dt))

        def sem(name):
            return es.enter_context(nc.semaphore(name))

        ring = [sb("ring%d" % i, [128, 4096], BF16) for i in range(NRING)]
        xs = [sb("xs%d" % i, [128, NTB, D], F32) for i in range(2)]
        big = sb("big", [128, 16384], BF16)
        hT = sb("hT", [128, 8, TT], BF16)
        gat = sb("gat", [128, 16, TT], BF16)
        scr = sb("scr", [128, 2048], F32)
        xn = [sb("xn%d" % i, [128, D], BF16) for i in range(NTB)]
        junk = sb("junk", [128, D], BF16)
        qrot = [sb("qrot%d" % i, [128, 512], BF16) for i in range(NTB)]
        krot = [sb("krot%d" % i, [128, 512], BF16) for i in range(NTB)]
        qT = [sb("qT%d" % i, [128, H, 128], BF16) for i in range(NTB)]
        kT = [sb("kT%d" % i, [128, H, 128], BF16) for i in range(NTB)]
        sT = [sb("sT%d" % i, [128, H, 128], BF16) for i in range(2)]
        St = sb("St", [128, H, DV], F32)
        Sbf = [sb("Sbf%d" % i, [128, H, DV], BF16) for i in range(2)]
        gated = [sb("gated%d" % i, [128, D], BF16) for i in range(2)]
        uh = sb("uh", [128, 4, 528], F32)
        ptmp = [sb("ptmp%d" % i, [128, 528], F32) for i in range(2)]
        pbf = sb("pbf", [128, 4, TT], BF16)
        ypT = sb("ypT", [128, 4, TT], BF16)
        cst = [sb("cst%d" % i, [128, 2, NTB, 64], F32) for i in range(2)]
        cf = sb("cfs", [128, 1536], F32)
        cbt = sb("cbs", [128, 1152], BF16)
        pwf = sb("pwf", [128, 512], F32)
        pwb = sb("pwb", [128, 4, 128], BF16)
        st_rms = sb("st_rms", [128, 64], F32)
        st_gn = sb("st_gn", [128, 64], F32)
        st_gn2 = sb("st_gn2", [128, 64], F32)
        st_fin = sb("st_fin", [128, 64], F32)
        st_fx = sb("st_fx", [128, 64], F32)
        st_eps = sb("st_eps", [128, 64], F32)
        ps = [es.enter_context(nc.psum_tensor("ps%d" % i, [128, 512], F32)) for i in range(8)]

        sems = {e: sem("s_" + e) for e in Sched.ENGS}
        ch_ring = [Chan(sem("c_ring%d" % i)) for i in range(NRING)]
        ch_ringc = [Chan(sem("c_ringc%d" % i)) for i in range(NRING)]
        ch_x = [Chan(sem("c_x%d" % i)) for i in range(2)]
        ch_cs = [Chan(sem("c_cs%d" % i)) for i in range(2)]
        ch_sn = [Chan(sem("c_sn%d" % i)) for i in range(2)]
        ch_out = [Chan(sem("c_out%d" % i)) for i in range(2)]
        ch_stage = [Chan(sem("c_stage%d" % i)) for i in range(4)]
        ch_scr = [Chan(sem("c_scr%d" % i)) for i in range(NRING)]
        ch_const = [Chan(sem("c_const%d" % i)) for i in range(3)]

        aT = big[:, :].rearrange("p (j t) -> p j t", j=32)
        vv = big[:, 0:4096].rearrange("p (b c) -> p b c", b=NTB)
        vz = big[:, 4096:8192].rearrange("p (b c) -> p b c", b=NTB)
        sg = big[:, 8192:12288].rearrange("p (b c) -> p b c", b=NTB)
        gatedT = big[:, 12288:16384].rearrange("p (k t) -> p k t", k=8)
        stage = [big[:, i * 8192:(i + 1) * 8192].bitcast(F32) for i in range(2)]
        mixedT = hT
        h2T = gat[:, 0:8, :]
        cmask = cf[:, 0:128]
        vzs = cf[:, 128:132]
        bgT = cf[:, 132:148]
        g_mix = cf[:, 148:156]
        g_gn = cf[:, 156:164]
        g_pool = cf[:, 164:168]
        g_mlp = cf[:, 168:176]
        invcnt = cf[:, 176:240].rearrange("p (g j) -> p g j", g=4)
        lnf = cf[:, 512:1536]
        ident = cbt[:, 0:128]
        diagq = cbt[:, 128:640].rearrange("p (h c) -> p h c", h=H)
        diagk = cbt[:, 640:1152].rearrange("p (h c) -> p h c", h=H)
        ssq = st_rms[:, 0:4]
        rv = st_rms[:, 4:8]
        ry = st_rms[:, 8:12]
        rt = st_rms[:, 12:16]
        rh = st_rms[:, 16:20]
        bst = st_gn[:, 0:24].rearrange("p (h s) -> p h s", h=H)
        mv = st_gn[:, 24:32].rearrange("p (h s) -> p h s", h=H)
        nmr = st_gn2[:, 0:4]
        epsc = st_eps[:, 0:1]

        bank_ctr = [0]

        def bank():
            b = ps[bank_ctr[0] % 8]
            bank_ctr[0] += 1
            return b

        def A(eng, fn, reads, writes, chan=None):
            return S.add(eng, fn, reads=reads, writes=writes, chan=chan)

        def mm(out, lhsT, rhs, start, stop):
            A("pe", lambda e: e.matmul(out, lhsT=lhsT, rhs=rhs, start=start, stop=stop), [lhsT, rhs], [out])

        def rsqrt_chain(v, y, t, hh):
            A("dve", lambda e: e.tensor_single_scalar(out=t.bitcast(I32), in_=v.bitcast(I32), scalar=1,
                                                      op=ALU.arith_shift_right), [v], [t])
            A("dve", lambda e: e.tensor_scalar(out=y.bitcast(I32), in0=t.bitcast(I32), scalar1=-1.0, scalar2=MAGIC,
                                               op0=ALU.mult, op1=ALU.add), [t], [y])
            A("dve", lambda e: e.tensor_scalar(out=hh, in0=v, scalar1=-0.5, scalar2=None, op0=ALU.mult), [v], [hh])
            for _ in range(2):
                A("dve", lambda e: e.tensor_tensor(out=t, in0=y, in1=y, op=ALU.mult), [y], [t])
                A("dve", lambda e: e.scalar_tensor_tensor(out=t, in0=t, scalar=1.0, in1=hh, op0=ALU.mult, op1=ALU.mult),
                  [t, hh], [t])
                A("dve", lambda e: e.scalar_tensor_tensor(out=y, in0=t, scalar=1.5, in1=y, op0=ALU.add, op1=ALU.mult),
                  [t, y], [y])

        A("pool", lambda e: e.dma_start(out=cf[:], in_=cf_d), [], [cf[:]], ch_const[0])
        A("pool", lambda e: e.dma_start(out=cbt[:], in_=cb_d), [], [cbt[:]], ch_const[1])
        A("pool", lambda e: e.dma_start(out=pwf[:], in_=pw_d), [], [pwf[:]], ch_const[2])
        A("dve", lambda e: e.tensor_copy(out=pwb[:].rearrange("p g d -> p (g d)"), in_=pwf[:]), [pwf[:]], [pwb[:]])
        A("dve", lambda e: e.memset(St[:], 0.0), [], [St[:]])
        A("dve", lambda e: e.memset(Sbf[0][:], 0.0), [], [Sbf[0][:]])
        A("dve", lambda e: e.memset(uh[:], 0.0), [], [uh[:]])
        A("dve", lambda e: e.memset(epsc, EPS), [], [epsc])

        S.mark('consts_done')
        def chunk_gain(j):
            if j <= 10:
                return g_mix, 8
            if j in (11, 13):
                return g_gn, 8
            if j == 12:
                return g_pool, 4
            if 16 <= j <= 23:
                return g_mlp, 8
            return None, 1

        S.mark('prologue_done')
        issued = [0]
        TOTAL = NT * NCH

        stg = [xs[1][:, p, :] for p in range(4)]

        def store_chunk(j):
            sl = j % NRING
            A("sp", lambda e: e.dma_start(out=wscr_d[j], in_=ring[sl][:]), [ring[sl][:]], [("dram_wscr", j)], ch_scr[sl])

        def stage_chunk(j):
            sl = j % NRING
            dst = ring[sl]
            gain, nk = chunk_gain(j)
            if gain is None:
                A("pool", lambda e: e.dma_start(out=dst[:], in_=wall_d[j]), [], [dst[:]], ch_ringc[sl])
                if j > 0:
                    store_chunk(j - 1)
                if j == NCH - 1:
                    store_chunk(j)
                return
            for p in range(4):
                A("sp", lambda e, p=p: e.dma_start(out=stg[p], in_=wall_d[j][:, p * 1024:(p + 1) * 1024]), [], [stg[p]], ch_stage[p])
            for p in range(4):
                eng = "act" if p % 2 == 0 else "dve"
                if gain is None:
                    pieces = [(stg[p], dst[:, p * 1024:(p + 1) * 1024], None)]
                else:
                    w = 4096 // nk
                    per = nk // 4
                    pieces = []
                    for i in range(per):
                        kc = p * per + i
                        pieces.append((stg[p][:, i * w:(i + 1) * w], dst[:, kc * w:(kc + 1) * w], gain[:, kc:kc + 1]))
                for src, d_, gcol in pieces:
                    if gcol is None:
                        if eng == "act":
                            A("act", lambda e, src=src, d_=d_: e.activation(out=d_, in_=src, func=AF.Copy), [src], [d_])
                        else:
                            A("dve", lambda e, src=src, d_=d_: e.tensor_copy(out=d_, in_=src), [src], [d_])
                    elif eng == "act":
                        A("act", lambda e, src=src, d_=d_, gcol=gcol: e.activation(out=d_, in_=src, func=AF.Copy, scale=gcol),
                          [src, gcol], [d_])
                    else:
                        A("dve", lambda e, src=src, d_=d_, gcol=gcol: e.tensor_scalar(out=d_, in0=src, scalar1=gcol, scalar2=None,
                                                                                     op0=ALU.mult), [src, gcol], [d_])
            if j > 0:
                store_chunk(j - 1)
            if j == NCH - 1:
                store_chunk(j)

        def prefetch_upto(n):
            while issued[0] <= min(n, TOTAL - 1):
                g = issued[0]
                j = g % NCH
                sl = g % NRING
                if g < NCH:
                    stage_chunk(j)
                else:
                    A("sp", lambda e, j=j, sl=sl: e.dma_start(out=ring[sl][:], in_=wscr_d[j]), [("dram_wscr", j)], [ring[sl][:]], ch_ring[sl])
                issued[0] += 1

        D0 = 4

        x1_loaded = [False]

        def pf(g):
            prefetch_upto(g + (NRING if g + NRING >= NCH else D0))
            if NT > 1 and issued[0] >= 24 and not x1_loaded[0]:
                x1_loaded[0] = True
                load_x(1)

        def wchunk(g):
            return ring[g % NRING][:, :].rearrange("p (k c) -> p k c", k=8)

        def load_x(t):
            sl = t % 2
            A("pool", lambda e: e.dma_start(out=xs[sl][:], in_=x_d[t * TT:(t + 1) * TT, :].rearrange("(c p) d -> p c d", p=128)),
              [], [xs[sl][:]], ch_x[sl])
            A("pool", lambda e: e.dma_start(out=cst[sl][:, 0], in_=cos_d[t * TT:(t + 1) * TT, :].rearrange("(c p) f -> p c f", p=128)),
              [], [cst[sl][:, 0]], ch_cs[sl])
            A("pool", lambda e: e.dma_start(out=cst[sl][:, 1], in_=sin_d[t * TT:(t + 1) * TT, :].rearrange("(c p) f -> p c f", p=128)),
              [], [cst[sl][:, 1]], ch_sn[sl])

        def norm_a(xsl, groups=None):
            if groups is None:
                groups = [list(range(NTB))]
            for grp in groups:
                lo, hi = grp[0], grp[-1] + 1
                for tb in grp:
                    A("act", lambda e, tb=tb: e.activation(out=junk[:], in_=xsl[:, tb, :], func=AF.Square, accum_out=ssq[:, tb:tb + 1]),
                      [xsl[:, tb, :]], [junk[:], ssq[:, tb:tb + 1]])
                A("act", lambda e, lo=lo, hi=hi: e.activation(out=rv[:, lo:hi], in_=ssq[:, lo:hi], func=AF.Sqrt, scale=1.0 / D, bias=epsc),
                  [ssq[:, lo:hi], epsc], [rv[:, lo:hi]])
                A("dve", lambda e, lo=lo, hi=hi: e.reciprocal(out=ry[:, lo:hi], in_=rv[:, lo:hi]), [rv[:, lo:hi]], [ry[:, lo:hi]])
                for tb in grp:
                    xb = xn[tb]
                    A("dve", lambda e, tb=tb, xb=xb: e.tensor_scalar(out=xb[:], in0=xsl[:, tb, :], scalar1=ry[:, tb:tb + 1], scalar2=None,
                                                                   op0=ALU.mult), [xsl[:, tb, :], ry[:, tb:tb + 1]], [xb[:]])

        def norm_b(dstT, tbs=None):
            for tb in (range(NTB) if tbs is None else tbs):
                xb = xn[tb]
                pb = bank()
                pbv = pb[:].bitcast(BF16).rearrange("p (k c) -> p k c", k=8)
                for kc in range(8):
                    A("pe", lambda e, kc=kc, xb=xb, pbv=pbv: e.transpose(out=pbv[:, kc, :], in_=xb[:, kc * 128:(kc + 1) * 128], identity=ident),
                      [xb[:, kc * 128:(kc + 1) * 128], ident], [pbv[:, kc, :]])
                dd = dstT[:, :, tb * 128:(tb + 1) * 128]
                if tb % 2 == 0:
                    A("dve", lambda e, dd=dd, pbv=pbv: e.tensor_copy(out=dd, in_=pbv), [pbv], [dd])
                else:
                    A("act", lambda e, dd=dd, pbv=pbv: e.activation(out=dd, in_=pbv, func=AF.Copy), [pbv], [dd])

        def tokmajor_chunk(g, srcT, evac):
            w = wchunk(g)
            for tb in range(NTB):
                pb = bank()
                for kc in range(8):
                    mm(pb[:], srcT[:, kc, tb * 128:(tb + 1) * 128], w[:, kc, :], kc == 0, kc == 7)
                evac(tb, pb)
            pf(g)

        def featmajor_chunk(g, srcT, evac, nk=8):
            w = wchunk(g)
            for cb in range(4):
                pb = bank()
                for kc in range(nk):
                    mm(pb[:], w[:, kc, cb * 128:(cb + 1) * 128], srcT[:, kc, :], kc == 0, kc == nk - 1)
                evac(cb, pb)
            pf(g)

        def mixer(t, hookA=None, hookB=None):
            g0 = t * NCH
            xsl = xs[t % 2]
            cs = cst[t % 2]

            def rotary(pb, tb, dst):
                pv = pb[:].rearrange("p (h t f) -> p h t f", h=H, t=2)
                t1 = scr[:, (tb % 2) * 1024:(tb % 2) * 1024 + 512]
                t2 = scr[:, (tb % 2) * 1024 + 512:(tb % 2) * 1024 + 1024]
                t1v = t1.rearrange("p (h t f) -> p h t f", h=H, t=2)
                t2v = t2.rearrange("p (h t f) -> p h t f", h=H, t=2)
                dv = dst[:].rearrange("p (h t f) -> p h t f", h=H, t=2)
                cosb = cs[:, 0, tb, :].unsqueeze(1).unsqueeze(1).broadcast_to([128, H, 2, 64])
                sinb = cs[:, 1, tb, :].unsqueeze(1).broadcast_to([128, H, 64])
                A("dve", lambda e: e.tensor_tensor(out=t1v, in0=pv, in1=cosb, op=ALU.mult), [pb[:], cs[:, 0, tb, :]], [t1])
                A("dve", lambda e: e.tensor_tensor(out=t2v[:, :, 0, :], in0=pv[:, :, 1, :], in1=sinb, op=ALU.mult),
                  [pb[:], cs[:, 1, tb, :]], [t2])
                A("dve", lambda e: e.tensor_tensor(out=t2v[:, :, 1, :], in0=pv[:, :, 0, :], in1=sinb, op=ALU.mult),
                  [pb[:], cs[:, 1, tb, :]], [t2])
                A("pool", lambda e: e.tensor_tensor(out=dv[:, :, 0, :], in0=t1v[:, :, 0, :], in1=t2v[:, :, 0, :], op=ALU.subtract),
                  [t1, t2], [dst[:]])
                A("pool", lambda e: e.tensor_tensor(out=dv[:, :, 1, :], in0=t1v[:, :, 1, :], in1=t2v[:, :, 1, :], op=ALU.add),
                  [t1, t2], [dst[:]])

            def diag_T(src, dg, dst):
                pb = bank()
                pv = pb[:].rearrange("p (h c) -> p h c", h=H)
                for h in range(H):
                    mm(pv[:, h, :], src[:, h * 128:(h + 1) * 128], dg[:, h, :], True, True)
                A("act", lambda e: e.activation(out=dst[:], in_=pv, func=AF.Copy), [pb[:]], [dst[:]])

            def ev_q(tb, pb):
                rotary(pb, tb, qrot[tb])

            def ev_k(tb, pb):
                rotary(pb, tb, krot[tb])

            def ev_v(i):
                def f(tb, pb):
                    A("act", lambda e: e.activation(out=vv[:, tb, i * 512:(i + 1) * 512], in_=pb[:], func=AF.Copy), [pb[:]],
                      [vv[:, tb, i * 512:(i + 1) * 512]])
                    for hl in range(2):
                        h = 2 * i + hl
                        o_ = vz[:, tb, h * 256:(h + 1) * 256]
                        A("act", lambda e, o_=o_, hl=hl, h=h: e.activation(out=o_, in_=pb[:, hl * 256:(hl + 1) * 256], func=AF.Copy,
                                                                         scale=vzs[:, h:h + 1]), [pb[:], vzs], [o_])
                return f

            def ev_g(i):
                def f(tb, pb):
                    o_ = sg[:, tb, i * 512:(i + 1) * 512]
                    A("act", lambda e: e.activation(out=o_, in_=pb[:], func=AF.Silu), [pb[:]], [o_])
                return f

            def ev_pu(cb, pb):
                o_ = uh[:, cb, 16:528]
                A("act", lambda e: e.activation(out=o_, in_=pb[:], func=AF.Copy), [pb[:]], [o_])

            def ev_gate(i):
                def f(cb, pb):
                    blk = i * 4 + cb
                    o_ = gat[:, blk, :]
                    A("act", lambda e: e.activation(out=o_, in_=pb[:], func=AF.Sigmoid, bias=bgT[:, blk:blk + 1]),
                      [pb[:], bgT], [o_])
                return f

            def R2(tb):
                pb = bank()
                pv = pb[:].rearrange("p (h c) -> p h c", h=H)
                for h in range(H):
                    mm(pv[:, h, :], kT[tb][:, h, :], qT[tb][:, h, :], True, True)
                sTb = sT[tb % 2]
                cmb = cmask.unsqueeze(1).broadcast_to([128, H, 128])
                A("dve", lambda e: e.tensor_tensor(out=sTb[:], in0=pv, in1=cmb, op=ALU.mult), [pb[:], cmask], [sTb[:]])
                for hp in range(2):
                    pk = bank()
                    for hl in range(2):
                        h = hp * 2 + hl
                        mm(pk[:, hl * 256:(hl + 1) * 256], krot[tb][:, h * 128:(h + 1) * 128], vz[:, tb, h * 256:(h + 1) * 256], True, True)
                    for hl in range(2):
                        h = hp * 2 + hl
                        A("dve", lambda e, h=h, hl=hl, pk=pk: e.scalar_tensor_tensor(out=St[:, h, :], in0=St[:, h, :], scalar=float(GAM[h] ** 128),
                                                                                    in1=pk[:, hl * 256:(hl + 1) * 256], op0=ALU.mult, op1=ALU.add),
                          [St[:, h, :], pk[:, hl * 256:(hl + 1) * 256]], [St[:, h, :]])
                nb = Sbf[(tb + 1) % 2]
                A("act", lambda e: e.activation(out=nb[:], in_=St[:], func=AF.Copy), [St[:]], [nb[:]])

            def R3a(tb):
                sTb = sT[tb % 2]
                sb_ = Sbf[tb % 2]
                pbs = []
                for hp in range(2):
                    po = bank()
                    pbs.append(po)
                    for hl in range(2):
                        h = hp * 2 + hl
                        o_ = po[:, hl * 256:(hl + 1) * 256]
                        mm(o_, sTb[:, h, :], vv[:, tb, h * 256:(h + 1) * 256], True, False)
                        mm(o_, qT[tb][:, h, :], sb_[:, h, :], False, True)
                for h in range(H):
                    src = pbs[h // 2][:, (h % 2) * 256:(h % 2 + 1) * 256]
                    A("dve", lambda e, h=h, src=src: e.bn_stats(out=bst[:, h, :], in_=src), [src], [bst[:, h, :]])
                    A("dve", lambda e, h=h: e.bn_aggr(out=mv[:, h, :], in_=bst[:, h, :]), [bst[:, h, :]], [mv[:, h, :]])
                gv = st_gn2[:, 4:8]
                gy = st_gn2[:, 8:12]
                gt = st_gn2[:, 12:16]
                gh = st_gn2[:, 16:20]
                A("dve", lambda e: e.tensor_scalar(out=gv, in0=mv[:, :, 1], scalar1=EPS, scalar2=None, op0=ALU.add), [mv], [gv])
                rsqrt_chain(gv, gy, gt, gh)
                A("dve", lambda e: e.scalar_tensor_tensor(out=nmr, in0=mv[:, :, 0], scalar=-1.0, in1=gy, op0=ALU.mult, op1=ALU.mult),
                  [mv, gy], [nmr])
                gb = gated[tb % 2]
                for h in range(H):
                    src = pbs[h // 2][:, (h % 2) * 256:(h % 2 + 1) * 256]
                    tmp = scr[:, 1024 + (h % 2) * 256:1024 + (h % 2 + 1) * 256]
                    A("act", lambda e, h=h, src=src, tmp=tmp: e.activation(out=tmp, in_=src, func=AF.Identity, scale=gy[:, h:h + 1],
                                                                         bias=nmr[:, h:h + 1]), [src, gy, nmr], [tmp])
                    o_ = gb[:, h * 256:(h + 1) * 256]
                    A("dve", lambda e, h=h, tmp=tmp, o_=o_: e.tensor_tensor(out=o_, in0=tmp, in1=sg[:, tb, h * 256:(h + 1) * 256], op=ALU.mult),
                      [tmp, sg[:, tb, h * 256:(h + 1) * 256]], [o_])

            def R3b(tb):
                gb = gated[tb % 2]
                pb = bank()
                pbv = pb[:].bitcast(BF16).rearrange("p (k c) -> p k c", k=8)
                for kc in range(8):
                    A("pe", lambda e, kc=kc, pbv=pbv: e.transpose(out=pbv[:, kc, :], in_=gb[:, kc * 128:(kc + 1) * 128], identity=ident),
                      [gb[:, kc * 128:(kc + 1) * 128], ident], [pbv[:, kc, :]])
                dd = gatedT[:, :, tb * 128:(tb + 1) * 128]
                A("dve", lambda e: e.tensor_copy(out=dd, in_=pbv), [pbv], [dd])

            def pooling():
                for g in range(4):
                    cur = uh[:, g, :]
                    src = cur
                    for lv in range(g + 1):
                        sh = 1 << lv
                        lo = 2 * sh - 1
                        dst = ptmp[lv % 2]
                        A("pool", lambda e, src=src, dst=dst, sh=sh, lo=lo: e.tensor_tensor(out=dst[:, lo:528], in0=src[:, lo:528],
                                                                                            in1=src[:, lo - sh:528 - sh], op=ALU.add),
                          [src[:, lo - sh:528]], [dst[:, lo:528]])
                        src = dst
                    wd = float(1 << (g + 1))
                    other = ptmp[(g + 1) % 2]
                    iw = invcnt[:, g, 15:16].broadcast_to([128, 512])
                    A("pool", lambda e, g=g, src=src, other=other, iw=iw: e.tensor_tensor(out=other[:, 16:528], in0=src[:, 16:528], in1=iw, op=ALU.mult),
                      [src[:, 16:528], invcnt[:, g, 15:16]], [other[:, 16:528]])
                    A("pool", lambda e, g=g, other=other: e.tensor_tensor(out=pbf[:, g, :], in0=other[:, 16:528], in1=uh[:, g, 16:528], op=ALU.subtract),
                      [other[:, 16:528], uh[:, g, 16:528]], [pbf[:, g, :]])
                    if t == 0:
                        fx = st_fx[:, 0:16]
                        A("pool", lambda e, g=g, src=src: e.tensor_tensor(out=fx, in0=src[:, 16:32], in1=invcnt[:, g, :], op=ALU.mult),
                          [src[:, 16:32], invcnt[:, g, :]], [fx])
                        A("pool", lambda e, g=g: e.tensor_tensor(out=pbf[:, g, 0:16], in0=fx, in1=uh[:, g, 16:32], op=ALU.subtract),
                          [fx, uh[:, g, 16:32]], [pbf[:, g, 0:16]])
                A("pool", lambda e: e.tensor_copy(out=uh[:, :, 0:16], in_=uh[:, :, 512:528]), [uh[:, :, 512:528]], [uh[:, :, 0:16]])

            tokmajor_chunk(g0 + 0, hT, ev_q)
            tokmajor_chunk(g0 + 1, hT, ev_k)
            for tb in range(NTB):
                diag_T(qrot[tb], diagq, qT[tb])
            tokmajor_chunk(g0 + 2, hT, ev_v(0))
            for tb in range(NTB):
                diag_T(krot[tb], diagk, kT[tb])
            tokmajor_chunk(g0 + 3, hT, ev_v(1))
            if hookA is not None:
                hookA()
            tokmajor_chunk(g0 + 4, hT, ev_g(0))
            R2(0)
            tokmajor_chunk(g0 + 5, hT, ev_g(1))
            R3a(0)
            R2(1)
            featmajor_chunk(g0 + 6, hT, ev_pu)
            pooling()
            R3a(1)
            R2(2)
            featmajor_chunk(g0 + 7, hT, ev_gate(0))
            R3b(0)
            R3a(2)
            R2(3)
            featmajor_chunk(g0 + 8, hT, ev_gate(1))
            R3b(1)
            R3a(3)
            featmajor_chunk(g0 + 9, hT, ev_gate(2))
            R3b(2)
            featmajor_chunk(g0 + 10, hT, ev_gate(3))
            R3b(3)

            S.mark('win_done%d' % t)
            for g in range(4):
                pb = bank()
                mm(pb[:], pwb[:, g, :], pbf[:, g, :], True, True)
                A("act", lambda e, g=g, pb=pb: e.activation(out=ypT[:, g, :], in_=pb[:], func=AF.Copy), [pb[:]], [ypT[:, g, :]])

            S.mark('poolbr_done%d' % t)
            wret0 = wchunk(g0 + 11)
            wpool = ring[(g0 + 12) % NRING][:, :].rearrange("p (k c) -> p k c", k=4)
            wret1 = wchunk(g0 + 13)
            pr = {}
            pp = {}

            def ret_blk(w, cb):
                pb = bank()
                pr[cb] = pb
                for kc in range(8):
                    mm(pb[:], w[:, kc, (cb % 4) * 128:(cb % 4 + 1) * 128], gatedT[:, kc, :], kc == 0, kc == 7)

            def pool_blk(cb):
                pb = bank()
                pp[cb] = pb
                for kc in range(4):
                    mm(pb[:], wpool[:, kc, cb * 128:(cb + 1) * 128], ypT[:, kc, :], kc == 0, kc == 3)

            def mix(cb):
                m1 = scr[:, (cb % 2) * 1024:(cb % 2) * 1024 + 512]
                m2 = scr[:, (cb % 2) * 1024 + 512:(cb % 2) * 1024 + 1024]
                A("dve", lambda e: e.tensor_tensor(out=m1, in0=pr[cb][:], in1=gat[:, cb, :], op=ALU.mult), [pr[cb][:], gat[:, cb, :]], [m1])
                A("dve", lambda e: e.tensor_tensor(out=m2, in0=pp[cb][:], in1=gat[:, 8 + cb, :], op=ALU.mult),
                  [pp[cb][:], gat[:, 8 + cb, :]], [m2])
                A("pool", lambda e: e.tensor_tensor(out=mixedT[:, cb, :], in0=m1, in1=m2, op=ALU.add), [m1, m2], [mixedT[:, cb, :]])

            for cb in range(4):
                ret_blk(wret0, cb)
                pool_blk(cb)
                mix(cb)
            pf(g0 + 11)
            for cb in range(4, 8):
                pool_blk(cb)
                if cb == 7:
                    pf(g0 + 12)
                ret_blk(wret1, cb)
                mix(cb)
            pf(g0 + 13)

            S.mark('merge_done%d' % t)
            if hookB is not None:
                hookB()
            wo = [wchunk(g0 + 14), wchunk(g0 + 15)]
            for tb in range(NTB):
                for i in range(2):
                    pb = bank()
                    for kc in range(8):
                        mm(pb[:], mixedT[:, kc, tb * 128:(tb + 1) * 128], wo[i][:, kc, :], kc == 0, kc == 7)
                    o_ = xsl[:, tb, i * 512:(i + 1) * 512]
                    A("dve", lambda e, o_=o_, pb=pb: e.tensor_tensor(out=o_, in0=pb[:], in1=o_, op=ALU.add), [pb[:], o_], [o_])
                if tb >= 2:
                    norm_b(h2T, [tb - 2])
                norm_a(xsl, [[tb]])
            pf(g0 + 14)
            pf(g0 + 15)

        def ffn_up(t):
            g0 = t * NCH
            xsl = xs[t % 2]

            def relu2(n, pb):
                rtmp = scr[:, (n % 2) * 512:(n % 2 + 1) * 512]
                A("act", lambda e: e.activation(out=rtmp, in_=pb[:], func=AF.Relu), [pb[:]], [rtmp])
                eng = "pool" if n % 2 == 0 else "dve"
                A(eng, lambda e: e.tensor_tensor(out=aT[:, n, :], in0=rtmp, in1=rtmp, op=ALU.mult), [rtmp], [aT[:, n, :]])

            w0, w1 = wchunk(g0 + 16), wchunk(g0 + 17)
            early = [(0, w0, cb) for cb in range(4)] + [(1, w1, cb) for cb in range(2)]
            banks = []
            for (j, w, cb) in early:
                pb = bank()
                banks.append(pb)
                for kc in range(8):
                    mm(pb[:, 0:256], w[:, kc, cb * 128:(cb + 1) * 128], h2T[:, kc, 0:256], kc == 0, kc == 7)
            norm_b(h2T, [2, 3])
            for (j, w, cb), pb in zip(early, banks):
                for kc in range(8):
                    mm(pb[:, 256:512], w[:, kc, cb * 128:(cb + 1) * 128], h2T[:, kc, 256:512], kc == 0, kc == 7)
                relu2(j * 4 + cb, pb)
                if (j, cb) == (0, 3):
                    pf(g0 + 16)
            for cb in range(2, 4):
                pb = bank()
                for kc in range(8):
                    mm(pb[:], w1[:, kc, cb * 128:(cb + 1) * 128], h2T[:, kc, :], kc == 0, kc == 7)
                relu2(4 + cb, pb)
            pf(g0 + 17)
            for j in range(2, 8):
                def ev_u(cb, pb, j=j):
                    relu2(j * 4 + cb, pb)
                featmajor_chunk(g0 + 16 + j, h2T, ev_u)
                if j == 3 and 1 <= t < NT - 1:
                    norm_a(xs[(t + 1) % 2])

        x1_normed = [False]

        def ffn_down(t):
            g0 = t * NCH
            xsl = xs[t % 2]
            for half in range(2):
                pbs = [bank() for _ in range(NTB)]
                for kg in range(4):
                    g = g0 + 24 + half * 4 + kg
                    w = wchunk(g)
                    for tb in range(NTB):
                        for kcl in range(8):
                            kc = kg * 8 + kcl
                            mm(pbs[tb][:], aT[:, kc, tb * 128:(tb + 1) * 128], w[:, kcl, :], kc == 0, kc == 31)
                    pf(g)
                    if t == 0 and NT > 1 and x1_loaded[0] and not x1_normed[0]:
                        x1_normed[0] = True
                        norm_a(xs[1])
                for tb in range(NTB):
                    o_ = xsl[:, tb, half * 512:(half + 1) * 512]
                    A("dve", lambda e, o_=o_, pb=pbs[tb]: e.tensor_tensor(out=o_, in0=pb[:], in1=o_, op=ALU.add), [pbs[tb][:], o_], [o_])

        def final(t):
            sl = t % 2
            xsl = xs[sl]
            fs = st_fin[:, 0:4]
            fv = st_fin[:, 4:8]
            fy = st_fin[:, 8:12]
            ft = st_fin[:, 12:16]
            fh = st_fin[:, 16:20]
            for tb in range(NTB):
                A("act", lambda e, tb=tb: e.activation(out=junk[:], in_=xsl[:, tb, :], func=AF.Square, accum_out=fs[:, tb:tb + 1]),
                  [xsl[:, tb, :]], [junk[:], fs[:, tb:tb + 1]])
            A("act", lambda e: e.activation(out=fv, in_=fs, func=AF.Sqrt, scale=1.0 / D, bias=epsc), [fs, epsc], [fv])
            A("dve", lambda e: e.reciprocal(out=fy, in_=fv), [fv], [fy])
            for tb in range(NTB):
                A("dve", lambda e, tb=tb: e.scalar_tensor_tensor(out=xsl[:, tb, :], in0=xsl[:, tb, :], scalar=fy[:, tb:tb + 1], in1=lnf,
                                                                 op0=ALU.mult, op1=ALU.mult), [xsl[:, tb, :], fy[:, tb:tb + 1], lnf], [xsl[:, tb, :]])
            A("pool", lambda e: e.dma_start(out=out_d[t * TT:(t + 1) * TT, :].rearrange("(c p) d -> p c d", p=128), in_=xsl[:]),
              [xsl[:]], [("out", t)], ch_out[sl])

        load_x(0)
        prefetch_upto(0)
        norm_a(xs[0])
        norm_b(hT)
        prefetch_upto(D0 - 1)
        for t in range(NT):
            hookA = (lambda t=t: final(t - 1)) if t >= 2 else None
            hookB = (lambda t=t: load_x(t + 1)) if 1 <= t < NT - 1 else None
            mixer(t, hookA, hookB)
            S.mark('mixer_done%d' % t)
            ffn_up(t)
            S.mark('ffn_up_done%d' % t)
            if 1 <= t < NT - 1:
                norm_b(hT)
            ffn_down(t)
            S.mark('ffn_down_done%d' % t)
            if t == 0 and NT > 1:
                assert x1_normed[0]
                norm_b(hT)
                final(0)
            elif t == NT - 1:
                final(t)
        S.emit(nc, sems, final_chans=ch_out, limit=limit)
    return nc, S


def _host_consts():
    pos = np.arange(SEQ, dtype=np.float32)
    inv_freq = (10000.0 ** (-np.arange(64, dtype=np.float32) * 2.0 / 128.0)).astype(np.float32)
    ang = (pos[:, None] * inv_freq[None, :]).astype(np.float32)
    cosT = np.cos(ang.astype(np.float64)).astype(np.float32)
    sinT = np.sin(ang.astype(np.float64)).astype(np.float32)
    idx = np.arange(128, dtype=np.float64)
    gam = np.array(GAM, dtype=np.float64)
    cb = np.zeros((128, 1152), dtype=np.float32)
    cb[:, 0:128] = np.eye(128)
    for h in range(H):
        cb[:, 128 + h * 128:128 + (h + 1) * 128] = np.diag(gam[h] ** idx * DK ** -0.5)
        cb[:, 640 + h * 128:640 + (h + 1) * 128] = np.diag(gam[h] ** (-idx))
    cb = cb.astype(ml_dtypes.bfloat16)
    return cosT, sinT, cb


def _prep(x, ln_mix, w_in, b_gate, gn_gain, w_ret_up, pool_w, pool_scale, w_pool_up, w_o, ln_mlp,
           w_up, w_down, ln_final):
    x = np.asarray(x, dtype=np.float32)
    f32 = lambda a: np.asarray(a, dtype=np.float32)
    w_in, w_ret_up, w_pool_up, w_o, w_up, w_down = map(f32, (w_in, w_ret_up, w_pool_up, w_o, w_up, w_down))

    def rows8(w):
        return np.ascontiguousarray(w.reshape(8, 128, 512).transpose(1, 0, 2)).reshape(128, 4096)

    wall = np.empty((NCH, 128, 4096), dtype=np.float32)
    for j in range(11):
        wall[j] = rows8(w_in[0][:, j * 512:(j + 1) * 512])
    wall[11] = rows8(w_ret_up[0][:, 0:512])
    wall[12] = np.ascontiguousarray(w_pool_up[0].reshape(4, 128, 1024).transpose(1, 0, 2)).reshape(128, 4096)
    wall[13] = rows8(w_ret_up[0][:, 512:1024])
    wall[14] = rows8(w_o[0][:, 0:512])
    wall[15] = rows8(w_o[0][:, 512:1024])
    for j in range(8):
        wall[16 + j] = rows8(w_up[0][:, j * 512:(j + 1) * 512])
    for half in range(2):
        for kg in range(4):
            wall[24 + half * 4 + kg] = rows8(w_down[0][kg * 1024:(kg + 1) * 1024, half * 512:(half + 1) * 512])

    cosT, sinT, cb = _host_consts()
    idx = np.arange(128, dtype=np.float64)
    gam = np.array(GAM, dtype=np.float64)
    cf = np.zeros((128, 1536), dtype=np.float32)
    cf[:, 0:128] = (idx[None, :] >= idx[:, None]).astype(np.float32)
    cf[:, 128:132] = (gam[None, :] ** (128.0 - idx[:, None])).astype(np.float32)
    cf[:, 132:148] = f32(b_gate)[0].reshape(16, 128).T
    cf[:, 148:156] = f32(ln_mix)[0].reshape(8, 128).T
    cf[:, 156:164] = f32(gn_gain)[0].reshape(8, 128).T
    cf[:, 164:168] = f32(pool_scale)[0].reshape(4, 128).T
    cf[:, 168:176] = f32(ln_mlp)[0].reshape(8, 128).T
    for g in range(4):
        cf[:, 176 + g * 16:176 + (g + 1) * 16] = (1.0 / np.minimum(np.arange(16) + 1.0, 2.0 ** (g + 1)))[None, :]
    cf[:, 512:1536] = f32(ln_final)[None, :]
    pw = np.ascontiguousarray(f32(pool_w)[0].transpose(1, 0, 2)).reshape(128, 512)

    return dict(wall=wall, cosT=cosT, sinT=sinT, cf=cf, cb=cb, pw=pw)


def kernel(x, ln_mix, w_in, b_gate, gn_gain, w_ret_up, pool_w, pool_scale, w_pool_up, w_o, ln_mlp,
           w_up, w_down, ln_final):
    x = np.asarray(x, dtype=np.float32)
    shared = _prep(x, ln_mix, w_in, b_gate, gn_gain, w_ret_up, pool_w, pool_scale, w_pool_up, w_o, ln_mlp,
                   w_up, w_down, ln_final)
    nc, _ = build_program()
    in_maps = [dict(x=np.ascontiguousarray(x[b]), **shared) for b in range(NB)]
    res = run_bass_kernel_spmd(nc, in_maps, core_ids=list(range(NB)))
    return np.stack([np.asarray(r["out"], dtype=np.float32) for r in res.results], axis=0)
```

```python
import numpy as np
import ml_dtypes
from contextlib import ExitStack
import concourse.bass as bass
import concourse.mybir as mybir
from concourse.bass_utils import run_bass_kernel_spmd

F32 = mybir.dt.float32
BF16 = mybir.dt.bfloat16
I32 = mybir.dt.int32
AF = mybir.ActivationFunctionType
ALU = mybir.AluOpType

_ESZ = {F32: 4, BF16: 2, I32: 4}
_G = 256


def ap_keys(ap):
    if isinstance(ap, (tuple, str)):
        return [ap]
    t = ap.tensor
    row = 1
    for s in list(t.shape)[1:]:
        row *= int(s)
    esz = _ESZ[ap.dtype]
    off = int(ap.offset) % row
    hi = off
    for st, cnt in list(ap.ap)[1:]:
        hi += (int(cnt) - 1) * int(st)
    hi += 1
    name = t.name
    if name.startswith("ps"):
        return [(name, 0)]
    return [(name, b) for b in range(off * esz // _G, (hi * esz - 1) // _G + 1)]


class Chan:
    def __init__(self, sem):
        self.sem = sem
        self.count = 0


class Op:
    __slots__ = ("eng", "fn", "deps", "chan", "chanval", "semval", "needs_inc", "idx")


class Sched:
    ENGS = ("pe", "act", "dve", "pool", "sp")

    def __init__(self):
        self.ops = []
        self.lastw = {}
        self.readers = {}

    def add(self, eng, fn, reads=(), writes=(), chan=None):
        op = Op()
        op.eng = eng
        op.fn = fn
        op.chan = chan
        op.chanval = None
        op.semval = None
        op.needs_inc = False
        op.idx = len(self.ops)
        if chan is not None:
            chan.count += 16
            op.chanval = chan.count
        deps = {}
        rk = []
        for a in reads:
            rk.extend(ap_keys(a))
        wk = []
        for a in writes:
            wk.extend(ap_keys(a))
        psr = [k for k in rk if isinstance(k[0], str) and k[0].startswith("ps") and k[1] == 0 and len(k) == 2]
        if psr:
            rk = [k for k in rk if k not in psr]
            wk = wk + [k for k in psr if k not in wk]
        for k in rk:
            w = self.lastw.get(k)
            if w is not None:
                deps[w.idx] = w
        for k in wk:
            w = self.lastw.get(k)
            if w is not None:
                deps[w.idx] = w
            r = self.readers.get(k)
            if r:
                for o in r.values():
                    deps[o.idx] = o
        rkey = eng if chan is None else ("dma", id(chan))
        for k in rk:
            self.readers.setdefault(k, {})[rkey] = op
        for k in wk:
            self.lastw[k] = op
            self.readers[k] = {}
        deps.pop(op.idx, None)
        best = {}
        for d in deps.values():
            if d.eng == "pe" and eng == "pe" and d.chan is None and chan is None:
                continue
            k = d.eng if d.chan is None else ("dma", id(d.chan))
            if k not in best or best[k].idx < d.idx:
                best[k] = d
        op.deps = list(best.values())
        for d in op.deps:
            if d.chan is None:
                d.needs_inc = True
        self.ops.append(op)
        return op

    def mark(self, name):
        if not hasattr(self, "marks"):
            self.marks = []
        self.marks.append((name, len(self.ops)))

    def emit(self, nc, sems, final_chans=(), limit=None):
        if limit is not None:
            self.ops = self.ops[:limit]
            chmax = {}
            for op in self.ops:
                if op.chan is not None:
                    chmax[id(op.chan)] = (op.chan, op.chanval)
            final_chans = []
            for ch, v in chmax.values():
                c = Chan(ch.sem)
                c.count = v
                final_chans.append(c)
        cnt = {e: 0 for e in self.ENGS}
        for op in self.ops:
            if op.chan is None and op.needs_inc:
                cnt[op.eng] += 1
                op.semval = cnt[op.eng]
        per_eng = {e: [o for o in self.ops if o.eng == e] for e in self.ENGS}
        nwaits = {e: 0 for e in self.ENGS}

        def run(e, handle):
            waited = {}
            for op in per_eng[e]:
                need = {}
                for d in op.deps:
                    if d.chan is not None:
                        s, v = d.chan.sem, d.chanval
                    else:
                        s, v = sems[d.eng], d.semval
                    key = id(s)
                    if waited.get(key, 0) >= v:
                        continue
                    if key not in need or need[key][1] < v:
                        need[key] = (s, v)
                for key, (s, v) in need.items():
                    handle.wait_ge(s, v)
                    waited[key] = v
                    nwaits[e] += 1
                ins = op.fn(handle)
                if op.chan is not None:
                    ins.then_inc(op.chan.sem, 16)
                elif op.needs_inc:
                    ins.then_inc(sems[e], 1)
            if e == "sp":
                for ch in final_chans:
                    if ch.count:
                        handle.wait_ge(ch.sem, ch.count)

        with nc.Block() as block:
            @block.tensor
            def _(h):
                run("pe", h)

            @block.scalar
            def _(h):
                run("act", h)

            @block.vector
            def _(h):
                run("dve", h)

            @block.gpsimd
            def _(h):
                run("pool", h)

            @block.sync
            def _(h):
                run("sp", h)
        self.stats = {e: (len(per_eng[e]), cnt[e], nwaits[e]) for e in self.ENGS}


D = 1024
SEQ = 4096
NB = 8
H = 4
DK = 128
DV = 256
TT = 512
NTB = TT // 128
NT = SEQ // TT
NCH = 32
NRING = 4
EPS = 1e-6
GAM = [1.0 - 2.0 ** (-5.0 - h) for h in range(H)]
MAGIC = 1597463007.0


def build_program(SEQ=SEQ, limit=None):
    NT = SEQ // TT
    nc = bass.Bass("TRN2", target_bir_lowering=False)
    x_d = nc.dram_tensor("x", [SEQ, D], F32, kind="ExternalInput").ap()
    wall_d = nc.dram_tensor("wall", [NCH, 128, 4096], F32, kind="ExternalInput").ap()
    cos_d = nc.dram_tensor("cosT", [SEQ, 64], F32, kind="ExternalInput").ap()
    sin_d = nc.dram_tensor("sinT", [SEQ, 64], F32, kind="ExternalInput").ap()
    cf_d = nc.dram_tensor("cf", [128, 1536], F32, kind="ExternalInput").ap()
    cb_d = nc.dram_tensor("cb", [128, 1152], BF16, kind="ExternalInput").ap()
    pw_d = nc.dram_tensor("pw", [128, 512], F32, kind="ExternalInput").ap()
    out_d = nc.dram_tensor("out", [SEQ, D], F32, kind="ExternalOutput").ap()
    wscr_d = nc.dram_tensor("wscr", [NCH, 128, 4096], BF16, kind="Internal").ap()

    S = Sched()
    with ExitStack() as es:
        def sb(name, shape, dt):
            return es.enter_context(nc.sbuf_tensor(name, shape, dt))

        def sem(name):
            return es.enter_context(nc.semaphore(name))

        ring = [sb("ring%d" % i, [128, 4096], BF16) for i in range(NRING)]
        xs = [sb("xs%d" % i, [128, NTB, D], F32) for i in range(2)]
        big = sb("big", [128, 16384], BF16)
        hT = sb("hT", [128, 8, TT], BF16)
        gat = sb("gat", [128, 16, TT], BF16)
        scr = sb("scr", [128, 2048], F32)
        xn = [sb("xn%d" % i, [128, D], BF16) for i in range(NTB)]
        junk = sb("junk", [128, D], BF16)
        qrot = [sb("qrot%d" % i, [128, 512], BF16) for i in range(NTB)]
        krot = [sb("krot%d" % i, [128, 512], BF16) for i in range(NTB)]
        qT = [sb("qT%d" % i, [128, H, 128], BF16) for i in range(NTB)]
        kT = [sb("kT%d" % i, [128, H, 128], BF16) for i in range(NTB)]
        sT = [sb("sT%d" % i, [128, H, 128], BF16) for i in range(2)]
        St = sb("St", [128, H, DV], F32)
        Sbf = [sb("Sbf%d" % i, [128, H, DV], BF16) for i in range(2)]
        gated = [sb("gated%d" % i, [128, D], BF16) for i in range(2)]
        uh = sb("uh", [128, 4, 528], F32)
        ptmp = [sb("ptmp%d" % i, [128, 528], F32) for i in range(2)]
        pbf = sb("pbf", [128, 4, TT], BF16)
        ypT = sb("ypT", [128, 4, TT], BF16)
        cst = [sb("cst%d" % i, [128, 2, NTB, 64], F32) for i in range(2)]
        cf = sb("cfs", [128, 1536], F32)
        cbt = sb("cbs", [128, 1152], BF16)
        pwf = sb("pwf", [128, 512], F32)
        pwb = sb("pwb", [128, 4, 128], BF16)
        st_rms = sb("st_rms", [128, 64], F32)
        st_gn = sb("st_gn", [128, 64], F32)
        st_gn2 = sb("st_gn2", [128, 64], F32)
        st_fin = sb("st_fin", [128, 64], F32)
        st_fx = sb("st_fx", [128, 64], F32)
        st_eps = sb("st_eps", [128, 64], F32)
        ps = [es.enter_context(nc.psum_tensor("ps%d" % i, [128, 512], F32)) for i in range(8)]

        sems = {e: sem("s_" + e) for e in Sched.ENGS}
        ch_ring = [Chan(sem("c_ring%d" % i)) for i in range(NRING)]
        ch_ringc = [Chan(sem("c_ringc%d" % i)) for i in range(NRING)]
        ch_x = [Chan(sem("c_x%d" % i)) for i in range(2)]
        ch_cs = [Chan(sem("c_cs%d" % i)) for i in range(2)]
        ch_sn = [Chan(sem("c_sn%d" % i)) for i in range(2)]
        ch_out = [Chan(sem("c_out%d" % i)) for i in range(2)]
        ch_outl = [Chan(sem("c_outl%d" % i)) for i in range(NTB)]
        ch_stage = [Chan(sem("c_stage%d" % i)) for i in range(4)]
        ch_scr = [Chan(sem("c_scr%d" % i)) for i in range(NRING)]
        ch_const = [Chan(sem("c_const%d" % i)) for i in range(4)]

        aT = big[:, :].rearrange("p (j t) -> p j t", j=32)
        vv = big[:, 0:4096].rearrange("p (b c) -> p b c", b=NTB)
        vz = big[:, 4096:8192].rearrange("p (b c) -> p b c", b=NTB)
        sg = big[:, 8192:12288].rearrange("p (b c) -> p b c", b=NTB)
        gatedT = big[:, 12288:16384].rearrange("p (k t) -> p k t", k=8)
        stage = [big[:, i * 8192:(i + 1) * 8192].bitcast(F32) for i in range(2)]
        mixedT = hT
        h2T = gat[:, 0:8, :]
        cmask = cf[:, 0:128]
        vzs = cf[:, 128:132]
        bgT = cf[:, 132:148]
        g_mix = cf[:, 148:156]
        g_gn = cf[:, 156:164]
        g_pool = cf[:, 164:168]
        g_mlp = cf[:, 168:176]
        invcnt = cf[:, 176:240].rearrange("p (g j) -> p g j", g=4)
        lnf = cf[:, 512:1536]
        ident = cbt[:, 0:128]
        diagq = cbt[:, 128:640].rearrange("p (h c) -> p h c", h=H)
        diagk = cbt[:, 640:1152].rearrange("p (h c) -> p h c", h=H)
        ssq = st_rms[:, 0:4]
        rv = st_rms[:, 4:8]
        ry = st_rms[:, 8:12]
        rt = st_rms[:, 12:16]
        rh = st_rms[:, 16:20]
        bst = st_gn[:, 0:24].rearrange("p (h s) -> p h s", h=H)
        mv = st_gn[:, 24:32].rearrange("p (h s) -> p h s", h=H)
        nmr = st_gn2[:, 0:4]
        epsc = st_eps[:, 0:1]

        bank_ctr = [0]

        def bank():
            b = ps[bank_ctr[0] % 8]
            bank_ctr[0] += 1
            return b

        def A(eng, fn, reads, writes, chan=None):
            return S.add(eng, fn, reads=reads, writes=writes, chan=chan)

        def mm(out, lhsT, rhs, start, stop):
            A("pe", lambda e: e.matmul(out, lhsT=lhsT, rhs=rhs, start=start, stop=stop), [lhsT, rhs], [out])

        def rsqrt_chain(v, y, t, hh):
            A("dve", lambda e: e.tensor_single_scalar(out=t.bitcast(I32), in_=v.bitcast(I32), scalar=1,
                                                      op=ALU.arith_shift_right), [v], [t])
            A("dve", lambda e: e.tensor_scalar(out=y.bitcast(I32), in0=t.bitcast(I32), scalar1=-1.0, scalar2=MAGIC,
                                               op0=ALU.mult, op1=ALU.add), [t], [y])
            A("dve", lambda e: e.tensor_scalar(out=hh, in0=v, scalar1=-0.5, scalar2=None, op0=ALU.mult), [v], [hh])
            for _ in range(2):
                A("dve", lambda e: e.tensor_tensor(out=t, in0=y, in1=y, op=ALU.mult), [y], [t])
                A("dve", lambda e: e.scalar_tensor_tensor(out=t, in0=t, scalar=1.0, in1=hh, op0=ALU.mult, op1=ALU.mult),
                  [t, hh], [t])
                A("dve", lambda e: e.scalar_tensor_tensor(out=y, in0=t, scalar=1.5, in1=y, op0=ALU.add, op1=ALU.mult),
                  [t, y], [y])

        load_x0_here = True
        A("dve", lambda e: e.memset(St[:], 0.0), [], [St[:]])
        A("dve", lambda e: e.memset(Sbf[0][:], 0.0), [], [Sbf[0][:]])
        A("dve", lambda e: e.memset(uh[:], 0.0), [], [uh[:]])
        A("dve", lambda e: e.memset(epsc, EPS), [], [epsc])

        S.mark('consts_done')
        def chunk_gain(j):
            if j <= 10:
                return g_mix, 8
            if j in (11, 13):
                return g_gn, 8
            if j == 12:
                return g_pool, 4
            if 16 <= j <= 23:
                return g_mlp, 8
            return None, 1

        S.mark('prologue_done')
        issued = [0]
        TOTAL = NT * NCH

        stg = [xs[1][:, p, :] for p in range(4)]

        def store_chunk(j):
            sl = j % NRING
            A("sp", lambda e: e.dma_start(out=wscr_d[j], in_=ring[sl][:]), [ring[sl][:]], [("dram_wscr", j)], ch_scr[sl])

        def stage_chunk(j):
            sl = j % NRING
            dst = ring[sl]
            gain, nk = chunk_gain(j)
            if gain is None:
                A("pool", lambda e: e.dma_start(out=dst[:], in_=wall_d[j]), [], [dst[:]], ch_ringc[sl])
                if j > 0:
                    store_chunk(j - 1)
                if j == NCH - 1:
                    store_chunk(j)
                return
            for p in range(4):
                A("sp", lambda e, p=p: e.dma_start(out=stg[p], in_=wall_d[j][:, p * 1024:(p + 1) * 1024]), [], [stg[p]], ch_stage[p])
            for p in range(4):
                eng = "act" if p % 2 == 0 else "dve"
                if gain is None:
                    pieces = [(stg[p], dst[:, p * 1024:(p + 1) * 1024], None)]
                else:
                    w = 4096 // nk
                    per = nk // 4
                    pieces = []
                    for i in range(per):
                        kc = p * per + i
                        pieces.append((stg[p][:, i * w:(i + 1) * w], dst[:, kc * w:(kc + 1) * w], gain[:, kc:kc + 1]))
                for src, d_, gcol in pieces:
                    if gcol is None:
                        if eng == "act":
                            A("act", lambda e, src=src, d_=d_: e.activation(out=d_, in_=src, func=AF.Copy), [src], [d_])
                        else:
                            A("dve", lambda e, src=src, d_=d_: e.tensor_copy(out=d_, in_=src), [src], [d_])
                    elif eng == "act":
                        A("act", lambda e, src=src, d_=d_, gcol=gcol: e.activation(out=d_, in_=src, func=AF.Copy, scale=gcol),
                          [src, gcol], [d_])
                    else:
                        A("dve", lambda e, src=src, d_=d_, gcol=gcol: e.tensor_scalar(out=d_, in0=src, scalar1=gcol, scalar2=None,
                                                                                     op0=ALU.mult), [src, gcol], [d_])
            if j > 0:
                store_chunk(j - 1)
            if j == NCH - 1:
                store_chunk(j)

        def prefetch_upto(n):
            while issued[0] <= min(n, TOTAL - 1):
                g = issued[0]
                j = g % NCH
                sl = g % NRING
                if g < NCH:
                    stage_chunk(j)
                else:
                    A("sp", lambda e, j=j, sl=sl: e.dma_start(out=ring[sl][:], in_=wscr_d[j]), [("dram_wscr", j)], [ring[sl][:]], ch_ring[sl])
                issued[0] += 1

        D0 = 4

        x1_loaded = [False]

        def pf(g):
            prefetch_upto(g + (NRING if g + NRING >= NCH else D0))
            if NT > 1 and issued[0] >= 24 and not x1_loaded[0]:
                x1_loaded[0] = True
                load_x(1)

        def wchunk(g):
            return ring[g % NRING][:, :].rearrange("p (k c) -> p k c", k=8)

        def load_x(t):
            sl = t % 2
            A("pool", lambda e: e.dma_start(out=xs[sl][:], in_=x_d[t * TT:(t + 1) * TT, :].rearrange("(c p) d -> p c d", p=128)),
              [], [xs[sl][:]], ch_x[sl])
            A("pool", lambda e: e.dma_start(out=cst[sl][:, 0], in_=cos_d[t * TT:(t + 1) * TT, :].rearrange("(c p) f -> p c f", p=128)),
              [], [cst[sl][:, 0]], ch_cs[sl])
            A("pool", lambda e: e.dma_start(out=cst[sl][:, 1], in_=sin_d[t * TT:(t + 1) * TT, :].rearrange("(c p) f -> p c f", p=128)),
              [], [cst[sl][:, 1]], ch_sn[sl])

        def norm_a(xsl, groups=None):
            if groups is None:
                groups = [list(range(NTB))]
            for grp in groups:
                lo, hi = grp[0], grp[-1] + 1
                for tb in grp:
                    A("act", lambda e, tb=tb: e.activation(out=junk[:], in_=xsl[:, tb, :], func=AF.Square, accum_out=ssq[:, tb:tb + 1]),
                      [xsl[:, tb, :]], [junk[:], ssq[:, tb:tb + 1]])
                A("act", lambda e, lo=lo, hi=hi: e.activation(out=rv[:, lo:hi], in_=ssq[:, lo:hi], func=AF.Sqrt, scale=1.0 / D, bias=epsc),
                  [ssq[:, lo:hi], epsc], [rv[:, lo:hi]])
                A("dve", lambda e, lo=lo, hi=hi: e.reciprocal(out=ry[:, lo:hi], in_=rv[:, lo:hi]), [rv[:, lo:hi]], [ry[:, lo:hi]])
                for tb in grp:
                    xb = xn[tb]
                    A("dve", lambda e, tb=tb, xb=xb: e.tensor_scalar(out=xb[:], in0=xsl[:, tb, :], scalar1=ry[:, tb:tb + 1], scalar2=None,
                                                                   op0=ALU.mult), [xsl[:, tb, :], ry[:, tb:tb + 1]], [xb[:]])

        def norm_b(dstT, tbs=None):
            for tb in (range(NTB) if tbs is None else tbs):
                xb = xn[tb]
                pb = bank()
                pbv = pb[:].bitcast(BF16).rearrange("p (k c) -> p k c", k=8)
                for kc in range(8):
                    A("pe", lambda e, kc=kc, xb=xb, pbv=pbv: e.transpose(out=pbv[:, kc, :], in_=xb[:, kc * 128:(kc + 1) * 128], identity=ident),
                      [xb[:, kc * 128:(kc + 1) * 128], ident], [pbv[:, kc, :]])
                dd = dstT[:, :, tb * 128:(tb + 1) * 128]
                if tb % 2 == 0:
                    A("dve", lambda e, dd=dd, pbv=pbv: e.tensor_copy(out=dd, in_=pbv), [pbv], [dd])
                else:
                    A("act", lambda e, dd=dd, pbv=pbv: e.activation(out=dd, in_=pbv, func=AF.Copy), [pbv], [dd])

        def tokmajor_chunk(g, srcT, evac):
            w = wchunk(g)
            for tb in range(NTB):
                pb = bank()
                for kc in range(8):
                    mm(pb[:], srcT[:, kc, tb * 128:(tb + 1) * 128], w[:, kc, :], kc == 0, kc == 7)
                evac(tb, pb)
            pf(g)

        def featmajor_chunk(g, srcT, evac, nk=8):
            w = wchunk(g)
            for cb in range(4):
                pb = bank()
                for kc in range(nk):
                    mm(pb[:], w[:, kc, cb * 128:(cb + 1) * 128], srcT[:, kc, :], kc == 0, kc == nk - 1)
                evac(cb, pb)
            pf(g)

        def mixer(t, hookA=None, hookB=None):
            g0 = t * NCH
            xsl = xs[t % 2]
            cs = cst[t % 2]

            def rotary(pb, tb, dst):
                pv = pb[:].rearrange("p (h t f) -> p h t f", h=H, t=2)
                t1 = scr[:, (tb % 2) * 1024:(tb % 2) * 1024 + 512]
                t2 = scr[:, (tb % 2) * 1024 + 512:(tb % 2) * 1024 + 1024]
                t1v = t1.rearrange("p (h t f) -> p h t f", h=H, t=2)
                t2v = t2.rearrange("p (h t f) -> p h t f", h=H, t=2)
                dv = dst[:].rearrange("p (h t f) -> p h t f", h=H, t=2)
                cosb = cs[:, 0, tb, :].unsqueeze(1).unsqueeze(1).broadcast_to([128, H, 2, 64])
                sinb = cs[:, 1, tb, :].unsqueeze(1).broadcast_to([128, H, 64])
                A("dve", lambda e: e.tensor_tensor(out=t1v, in0=pv, in1=cosb, op=ALU.mult), [pb[:], cs[:, 0, tb, :]], [t1])
                A("dve", lambda e: e.tensor_tensor(out=t2v[:, :, 0, :], in0=pv[:, :, 1, :], in1=sinb, op=ALU.mult),
                  [pb[:], cs[:, 1, tb, :]], [t2])
                A("dve", lambda e: e.tensor_tensor(out=t2v[:, :, 1, :], in0=pv[:, :, 0, :], in1=sinb, op=ALU.mult),
                  [pb[:], cs[:, 1, tb, :]], [t2])
                A("pool", lambda e: e.tensor_tensor(out=dv[:, :, 0, :], in0=t1v[:, :, 0, :], in1=t2v[:, :, 0, :], op=ALU.subtract),
                  [t1, t2], [dst[:]])
                A("pool", lambda e: e.tensor_tensor(out=dv[:, :, 1, :], in0=t1v[:, :, 1, :], in1=t2v[:, :, 1, :], op=ALU.add),
                  [t1, t2], [dst[:]])

            def diag_T(src, dg, dst):
                pb = bank()
                pv = pb[:].rearrange("p (h c) -> p h c", h=H)
                for h in range(H):
                    mm(pv[:, h, :], src[:, h * 128:(h + 1) * 128], dg[:, h, :], True, True)
                A("act", lambda e: e.activation(out=dst[:], in_=pv, func=AF.Copy), [pb[:]], [dst[:]])

            def ev_q(tb, pb):
                rotary(pb, tb, qrot[tb])

            def ev_k(tb, pb):
                rotary(pb, tb, krot[tb])

            def ev_v(i):
                def f(tb, pb):
                    A("act", lambda e: e.activation(out=vv[:, tb, i * 512:(i + 1) * 512], in_=pb[:], func=AF.Copy), [pb[:]],
                      [vv[:, tb, i * 512:(i + 1) * 512]])
                    for hl in range(2):
                        h = 2 * i + hl
                        o_ = vz[:, tb, h * 256:(h + 1) * 256]
                        A("act", lambda e, o_=o_, hl=hl, h=h: e.activation(out=o_, in_=pb[:, hl * 256:(hl + 1) * 256], func=AF.Copy,
                                                                         scale=vzs[:, h:h + 1]), [pb[:], vzs], [o_])
                return f

            def ev_g(i):
                def f(tb, pb):
                    o_ = sg[:, tb, i * 512:(i + 1) * 512]
                    A("act", lambda e: e.activation(out=o_, in_=pb[:], func=AF.Silu), [pb[:]], [o_])
                return f

            def ev_pu(cb, pb):
                o_ = uh[:, cb, 16:528]
                A("act", lambda e: e.activation(out=o_, in_=pb[:], func=AF.Copy), [pb[:]], [o_])

            def ev_gate(i):
                def f(cb, pb):
                    blk = i * 4 + cb
                    o_ = gat[:, blk, :]
                    A("act", lambda e: e.activation(out=o_, in_=pb[:], func=AF.Sigmoid, bias=bgT[:, blk:blk + 1]),
                      [pb[:], bgT], [o_])
                return f

            def R2(tb):
                pb = bank()
                pv = pb[:].rearrange("p (h c) -> p h c", h=H)
                for h in range(H):
                    mm(pv[:, h, :], kT[tb][:, h, :], qT[tb][:, h, :], True, True)
                sTb = sT[tb % 2]
                cmb = cmask.unsqueeze(1).broadcast_to([128, H, 128])
                A("dve", lambda e: e.tensor_tensor(out=sTb[:], in0=pv, in1=cmb, op=ALU.mult), [pb[:], cmask], [sTb[:]])
                for hp in range(2):
                    pk = bank()
                    for hl in range(2):
                        h = hp * 2 + hl
                        mm(pk[:, hl * 256:(hl + 1) * 256], krot[tb][:, h * 128:(h + 1) * 128], vz[:, tb, h * 256:(h + 1) * 256], True, True)
                    for hl in range(2):
                        h = hp * 2 + hl
                        A("dve", lambda e, h=h, hl=hl, pk=pk: e.scalar_tensor_tensor(out=St[:, h, :], in0=St[:, h, :], scalar=float(GAM[h] ** 128),
                                                                                    in1=pk[:, hl * 256:(hl + 1) * 256], op0=ALU.mult, op1=ALU.add),
                          [St[:, h, :], pk[:, hl * 256:(hl + 1) * 256]], [St[:, h, :]])
                nb = Sbf[(tb + 1) % 2]
                A("act", lambda e: e.activation(out=nb[:], in_=St[:], func=AF.Copy), [St[:]], [nb[:]])

            def R3a(tb):
                sTb = sT[tb % 2]
                sb_ = Sbf[tb % 2]
                pbs = []
                for hp in range(2):
                    po = bank()
                    pbs.append(po)
                    for hl in range(2):
                        h = hp * 2 + hl
                        o_ = po[:, hl * 256:(hl + 1) * 256]
                        mm(o_, sTb[:, h, :], vv[:, tb, h * 256:(h + 1) * 256], True, False)
                        mm(o_, qT[tb][:, h, :], sb_[:, h, :], False, True)
                for h in range(H):
                    src = pbs[h // 2][:, (h % 2) * 256:(h % 2 + 1) * 256]
                    A("dve", lambda e, h=h, src=src: e.bn_stats(out=bst[:, h, :], in_=src), [src], [bst[:, h, :]])
                    A("dve", lambda e, h=h: e.bn_aggr(out=mv[:, h, :], in_=bst[:, h, :]), [bst[:, h, :]], [mv[:, h, :]])
                gv = st_gn2[:, 4:8]
                gy = st_gn2[:, 8:12]
                gt = st_gn2[:, 12:16]
                gh = st_gn2[:, 16:20]
                A("dve", lambda e: e.tensor_scalar(out=gv, in0=mv[:, :, 1], scalar1=EPS, scalar2=None, op0=ALU.add), [mv], [gv])
                rsqrt_chain(gv, gy, gt, gh)
                A("dve", lambda e: e.scalar_tensor_tensor(out=nmr, in0=mv[:, :, 0], scalar=-1.0, in1=gy, op0=ALU.mult, op1=ALU.mult),
                  [mv, gy], [nmr])
                gb = gated[tb % 2]
                for h in range(H):
                    src = pbs[h // 2][:, (h % 2) * 256:(h % 2 + 1) * 256]
                    tmp = scr[:, 1024 + (h % 2) * 256:1024 + (h % 2 + 1) * 256]
                    A("act", lambda e, h=h, src=src, tmp=tmp: e.activation(out=tmp, in_=src, func=AF.Identity, scale=gy[:, h:h + 1],
                                                                         bias=nmr[:, h:h + 1]), [src, gy, nmr], [tmp])
                    o_ = gb[:, h * 256:(h + 1) * 256]
                    A("dve", lambda e, h=h, tmp=tmp, o_=o_: e.tensor_tensor(out=o_, in0=tmp, in1=sg[:, tb, h * 256:(h + 1) * 256], op=ALU.mult),
                      [tmp, sg[:, tb, h * 256:(h + 1) * 256]], [o_])

            def R3b(tb):
                gb = gated[tb % 2]
                pb = bank()
                pbv = pb[:].bitcast(BF16).rearrange("p (k c) -> p k c", k=8)
                for kc in range(8):
                    A("pe", lambda e, kc=kc, pbv=pbv: e.transpose(out=pbv[:, kc, :], in_=gb[:, kc * 128:(kc + 1) * 128], identity=ident),
                      [gb[:, kc * 128:(kc + 1) * 128], ident], [pbv[:, kc, :]])
                dd = gatedT[:, :, tb * 128:(tb + 1) * 128]
                A("dve", lambda e: e.tensor_copy(out=dd, in_=pbv), [pbv], [dd])

            def pooling():
                for g in range(4):
                    cur = uh[:, g, :]
                    src = cur
                    for lv in range(g + 1):
                        sh = 1 << lv
                        lo = 2 * sh - 1
                        dst = ptmp[lv % 2]
                        A("pool", lambda e, src=src, dst=dst, sh=sh, lo=lo: e.tensor_tensor(out=dst[:, lo:528], in0=src[:, lo:528],
                                                                                            in1=src[:, lo - sh:528 - sh], op=ALU.add),
                          [src[:, lo - sh:528]], [dst[:, lo:528]])
                        src = dst
                    wd = float(1 << (g + 1))
                    other = ptmp[(g + 1) % 2]
                    iw = invcnt[:, g, 15:16].broadcast_to([128, 512])
                    A("pool", lambda e, g=g, src=src, other=other, iw=iw: e.tensor_tensor(out=other[:, 16:528], in0=src[:, 16:528], in1=iw, op=ALU.mult),
                      [src[:, 16:528], invcnt[:, g, 15:16]], [other[:, 16:528]])
                    A("pool", lambda e, g=g, other=other: e.tensor_tensor(out=pbf[:, g, :], in0=other[:, 16:528], in1=uh[:, g, 16:528], op=ALU.subtract),
                      [other[:, 16:528], uh[:, g, 16:528]], [pbf[:, g, :]])
                    if t == 0:
                        fx = st_fx[:, 0:16]
                        A("pool", lambda e, g=g, src=src: e.tensor_tensor(out=fx, in0=src[:, 16:32], in1=invcnt[:, g, :], op=ALU.mult),
                          [src[:, 16:32], invcnt[:, g, :]], [fx])
                        A("pool", lambda e, g=g: e.tensor_tensor(out=pbf[:, g, 0:16], in0=fx, in1=uh[:, g, 16:32], op=ALU.subtract),
                          [fx, uh[:, g, 16:32]], [pbf[:, g, 0:16]])
                A("pool", lambda e: e.tensor_copy(out=uh[:, :, 0:16], in_=uh[:, :, 512:528]), [uh[:, :, 512:528]], [uh[:, :, 0:16]])

            tokmajor_chunk(g0 + 0, hT, ev_q)
            tokmajor_chunk(g0 + 1, hT, ev_k)
            for tb in range(NTB):
                diag_T(qrot[tb], diagq, qT[tb])
            tokmajor_chunk(g0 + 2, hT, ev_v(0))
            for tb in range(NTB):
                diag_T(krot[tb], diagk, kT[tb])
            tokmajor_chunk(g0 + 3, hT, ev_v(1))
            if hookA is not None:
                hookA()
            tokmajor_chunk(g0 + 4, hT, ev_g(0))
            R2(0)
            tokmajor_chunk(g0 + 5, hT, ev_g(1))
            R3a(0)
            R2(1)
            featmajor_chunk(g0 + 6, hT, ev_pu)
            pooling()
            R3a(1)
            R2(2)
            featmajor_chunk(g0 + 7, hT, ev_gate(0))
            R3b(0)
            R3a(2)
            R2(3)
            featmajor_chunk(g0 + 8, hT, ev_gate(1))
            R3b(1)
            R3a(3)
            featmajor_chunk(g0 + 9, hT, ev_gate(2))
            R3b(2)
            featmajor_chunk(g0 + 10, hT, ev_gate(3))
            R3b(3)

            S.mark('win_done%d' % t)
            for g in range(4):
                pb = bank()
                mm(pb[:], pwb[:, g, :], pbf[:, g, :], True, True)
                A("act", lambda e, g=g, pb=pb: e.activation(out=ypT[:, g, :], in_=pb[:], func=AF.Copy), [pb[:]], [ypT[:, g, :]])

            S.mark('poolbr_done%d' % t)
            wret0 = wchunk(g0 + 11)
            wpool = ring[(g0 + 12) % NRING][:, :].rearrange("p (k c) -> p k c", k=4)
            wret1 = wchunk(g0 + 13)
            pr = {}
            pp = {}

            def ret_blk(w, cb):
                pb = bank()
                pr[cb] = pb
                for kc in range(8):
                    mm(pb[:], w[:, kc, (cb % 4) * 128:(cb % 4 + 1) * 128], gatedT[:, kc, :], kc == 0, kc == 7)

            def pool_blk(cb):
                pb = bank()
                pp[cb] = pb
                for kc in range(4):
                    mm(pb[:], wpool[:, kc, cb * 128:(cb + 1) * 128], ypT[:, kc, :], kc == 0, kc == 3)

            def mix(cb):
                m1 = scr[:, (cb % 2) * 1024:(cb % 2) * 1024 + 512]
                m2 = scr[:, (cb % 2) * 1024 + 512:(cb % 2) * 1024 + 1024]
                A("dve", lambda e: e.tensor_tensor(out=m1, in0=pr[cb][:], in1=gat[:, cb, :], op=ALU.mult), [pr[cb][:], gat[:, cb, :]], [m1])
                A("dve", lambda e: e.tensor_tensor(out=m2, in0=pp[cb][:], in1=gat[:, 8 + cb, :], op=ALU.mult),
                  [pp[cb][:], gat[:, 8 + cb, :]], [m2])
                A("pool", lambda e: e.tensor_tensor(out=mixedT[:, cb, :], in0=m1, in1=m2, op=ALU.add), [m1, m2], [mixedT[:, cb, :]])

            for cb in range(4):
                ret_blk(wret0, cb)
                pool_blk(cb)
                mix(cb)
            pf(g0 + 11)
            for cb in range(4, 8):
                pool_blk(cb)
                if cb == 7:
                    pf(g0 + 12)
                ret_blk(wret1, cb)
                mix(cb)
            pf(g0 + 13)

            S.mark('merge_done%d' % t)
            if hookB is not None:
                hookB()
            wo = [wchunk(g0 + 14), wchunk(g0 + 15)]
            for tb in range(NTB):
                for i in range(2):
                    pb = bank()
                    for kc in range(8):
                        mm(pb[:], mixedT[:, kc, tb * 128:(tb + 1) * 128], wo[i][:, kc, :], kc == 0, kc == 7)
                    o_ = xsl[:, tb, i * 512:(i + 1) * 512]
                    A("dve", lambda e, o_=o_, pb=pb: e.tensor_tensor(out=o_, in0=pb[:], in1=o_, op=ALU.add), [pb[:], o_], [o_])
                if tb >= 2:
                    norm_b(h2T, [tb - 2])
                norm_a(xsl, [[tb]])
            pf(g0 + 14)
            pf(g0 + 15)

        def ffn_up(t):
            g0 = t * NCH
            xsl = xs[t % 2]

            def relu2(n, pb):
                rtmp = scr[:, (n % 2) * 512:(n % 2 + 1) * 512]
                A("act", lambda e: e.activation(out=rtmp, in_=pb[:], func=AF.Relu), [pb[:]], [rtmp])
                eng = "pool" if n % 2 == 0 else "dve"
                A(eng, lambda e: e.tensor_tensor(out=aT[:, n, :], in0=rtmp, in1=rtmp, op=ALU.mult), [rtmp], [aT[:, n, :]])

            w0, w1 = wchunk(g0 + 16), wchunk(g0 + 17)
            early = [(0, w0, cb) for cb in range(4)] + [(1, w1, cb) for cb in range(2)]
            banks = []
            for (j, w, cb) in early:
                pb = bank()
                banks.append(pb)
                for kc in range(8):
                    mm(pb[:, 0:256], w[:, kc, cb * 128:(cb + 1) * 128], h2T[:, kc, 0:256], kc == 0, kc == 7)
            norm_b(h2T, [2, 3])
            for (j, w, cb), pb in zip(early, banks):
                for kc in range(8):
                    mm(pb[:, 256:512], w[:, kc, cb * 128:(cb + 1) * 128], h2T[:, kc, 256:512], kc == 0, kc == 7)
                relu2(j * 4 + cb, pb)
                if (j, cb) == (0, 3):
                    pf(g0 + 16)
            for cb in range(2, 4):
                pb = bank()
                for kc in range(8):
                    mm(pb[:], w1[:, kc, cb * 128:(cb + 1) * 128], h2T[:, kc, :], kc == 0, kc == 7)
                relu2(4 + cb, pb)
            pf(g0 + 17)
            for j in range(2, 8):
                def ev_u(cb, pb, j=j):
                    relu2(j * 4 + cb, pb)
                featmajor_chunk(g0 + 16 + j, h2T, ev_u)
                if j == 3 and 1 <= t < NT - 1:
                    norm_a(xs[(t + 1) % 2])

        x1_normed = [False]

        def ffn_down(t):
            g0 = t * NCH
            xsl = xs[t % 2]
            for half in range(2):
                pbs = [bank() for _ in range(NTB)]
                for kg in range(4):
                    g = g0 + 24 + half * 4 + kg
                    w = wchunk(g)
                    for tb in range(NTB):
                        for kcl in range(8):
                            kc = kg * 8 + kcl
                            mm(pbs[tb][:], aT[:, kc, tb * 128:(tb + 1) * 128], w[:, kcl, :], kc == 0, kc == 31)
                    pf(g)
                    if t == 0 and NT > 1 and x1_loaded[0] and not x1_normed[0]:
                        x1_normed[0] = True
                        norm_a(xs[1])
                for tb in range(NTB):
                    o_ = xsl[:, tb, half * 512:(half + 1) * 512]
                    A("dve", lambda e, o_=o_, pb=pbs[tb]: e.tensor_tensor(out=o_, in0=pb[:], in1=o_, op=ALU.add), [pbs[tb][:], o_], [o_])

        def final(t):
            sl = t % 2
            xsl = xs[sl]
            fs = st_fin[:, 0:4]
            fv = st_fin[:, 4:8]
            fy = st_fin[:, 8:12]
            ft = st_fin[:, 12:16]
            fh = st_fin[:, 16:20]
            if t == NT - 1:
                for tb in range(NTB):
                    c = slice(tb, tb + 1)
                    A("act", lambda e, tb=tb, c=c: e.activation(out=junk[:], in_=xsl[:, tb, :], func=AF.Square, accum_out=fs[:, c]),
                      [xsl[:, tb, :]], [junk[:], fs[:, c]])
                    A("act", lambda e, c=c: e.activation(out=fv[:, c], in_=fs[:, c], func=AF.Sqrt, scale=1.0 / D, bias=epsc),
                      [fs[:, c], epsc], [fv[:, c]])
                    A("dve", lambda e, c=c: e.reciprocal(out=fy[:, c], in_=fv[:, c]), [fv[:, c]], [fy[:, c]])
                    A("dve", lambda e, tb=tb, c=c: e.scalar_tensor_tensor(out=xsl[:, tb, :], in0=xsl[:, tb, :], scalar=fy[:, c], in1=lnf,
                                                                         op0=ALU.mult, op1=ALU.mult), [xsl[:, tb, :], fy[:, c], lnf], [xsl[:, tb, :]])
                    r0 = t * TT + tb * 128
                    A("pool", lambda e, tb=tb, r0=r0: e.dma_start(out=out_d[r0:r0 + 128, :], in_=xsl[:, tb, :]),
                      [xsl[:, tb, :]], [("out", t, tb)], ch_outl[tb])
                return
            for tb in range(NTB):
                A("act", lambda e, tb=tb: e.activation(out=junk[:], in_=xsl[:, tb, :], func=AF.Square, accum_out=fs[:, tb:tb + 1]),
                  [xsl[:, tb, :]], [junk[:], fs[:, tb:tb + 1]])
            A("act", lambda e: e.activation(out=fv, in_=fs, func=AF.Sqrt, scale=1.0 / D, bias=epsc), [fs, epsc], [fv])
            A("dve", lambda e: e.reciprocal(out=fy, in_=fv), [fv], [fy])
            for tb in range(NTB):
                A("dve", lambda e, tb=tb: e.scalar_tensor_tensor(out=xsl[:, tb, :], in0=xsl[:, tb, :], scalar=fy[:, tb:tb + 1], in1=lnf,
                                                                 op0=ALU.mult, op1=ALU.mult), [xsl[:, tb, :], fy[:, tb:tb + 1], lnf], [xsl[:, tb, :]])
            A("pool", lambda e: e.dma_start(out=out_d[t * TT:(t + 1) * TT, :].rearrange("(c p) d -> p c d", p=128), in_=xsl[:]),
              [xsl[:]], [("out", t)], ch_out[sl])

        load_x(0)
        A("pool", lambda e: e.dma_start(out=cbt[:], in_=cb_d), [], [cbt[:]], ch_const[1])
        A("pool", lambda e: e.dma_start(out=cf[:, 0:512], in_=cf_d[:, 0:512]), [], [cf[:, 0:512]], ch_const[0])
        A("pool", lambda e: e.dma_start(out=pwf[:], in_=pw_d), [], [pwf[:]], ch_const[2])
        A("pool", lambda e: e.dma_start(out=cf[:, 512:1536], in_=cf_d[:, 512:1536]), [], [cf[:, 512:1536]], ch_const[3])
        A("dve", lambda e: e.tensor_copy(out=pwb[:].rearrange("p g d -> p (g d)"), in_=pwf[:]), [pwf[:]], [pwb[:]])
        prefetch_upto(0)
        norm_a(xs[0])
        norm_b(hT)
        prefetch_upto(D0 - 1)
        for t in range(NT):
            hookA = (lambda t=t: final(t - 1)) if t >= 2 else None
            hookB = (lambda t=t: load_x(t + 1)) if 1 <= t < NT - 1 else None
            mixer(t, hookA, hookB)
            S.mark('mixer_done%d' % t)
            ffn_up(t)
            S.mark('ffn_up_done%d' % t)
            if 1 <= t < NT - 1:
                norm_b(hT)
            ffn_down(t)
            S.mark('ffn_down_done%d' % t)
            if t == 0 and NT > 1:
                assert x1_normed[0]
                norm_b(hT)
                final(0)
            elif t == NT - 1:
                final(t)
        S.emit(nc, sems, final_chans=ch_out + ch_outl, limit=limit)
    return nc, S


def _host_consts():
    pos = np.arange(SEQ, dtype=np.float32)
    inv_freq = (10000.0 ** (-np.arange(64, dtype=np.float32) * 2.0 / 128.0)).astype(np.float32)
    ang = (pos[:, None] * inv_freq[None, :]).astype(np.float32)
    cosT = np.cos(ang.astype(np.float64)).astype(np.float32)
    sinT = np.sin(ang.astype(np.float64)).astype(np.float32)
    idx = np.arange(128, dtype=np.float64)
    gam = np.array(GAM, dtype=np.float64)
    cb = np.zeros((128, 1152), dtype=np.float32)
    cb[:, 0:128] = np.eye(128)
    for h in range(H):
        cb[:, 128 + h * 128:128 + (h + 1) * 128] = np.diag(gam[h] ** idx * DK ** -0.5)
        cb[:, 640 + h * 128:640 + (h + 1) * 128] = np.diag(gam[h] ** (-idx))
    cb = cb.astype(ml_dtypes.bfloat16)
    return cosT, sinT, cb


def _prep(x, ln_mix, w_in, b_gate, gn_gain, w_ret_up, pool_w, pool_scale, w_pool_up, w_o, ln_mlp,
           w_up, w_down, ln_final):
    x = np.asarray(x, dtype=np.float32)
    f32 = lambda a: np.asarray(a, dtype=np.float32)
    w_in, w_ret_up, w_pool_up, w_o, w_up, w_down = map(f32, (w_in, w_ret_up, w_pool_up, w_o, w_up, w_down))

    def rows8(w):
        return np.ascontiguousarray(w.reshape(8, 128, 512).transpose(1, 0, 2)).reshape(128, 4096)

    wall = np.empty((NCH, 128, 4096), dtype=np.float32)
    for j in range(11):
        wall[j] = rows8(w_in[0][:, j * 512:(j + 1) * 512])
    wall[11] = rows8(w_ret_up[0][:, 0:512])
    wall[12] = np.ascontiguousarray(w_pool_up[0].reshape(4, 128, 1024).transpose(1, 0, 2)).reshape(128, 4096)
    wall[13] = rows8(w_ret_up[0][:, 512:1024])
    wall[14] = rows8(w_o[0][:, 0:512])
    wall[15] = rows8(w_o[0][:, 512:1024])
    for j in range(8):
        wall[16 + j] = rows8(w_up[0][:, j * 512:(j + 1) * 512])
    for half in range(2):
        for kg in range(4):
            wall[24 + half * 4 + kg] = rows8(w_down[0][kg * 1024:(kg + 1) * 1024, half * 512:(half + 1) * 512])

    cosT, sinT, cb = _host_consts()
    idx = np.arange(128, dtype=np.float64)
    gam = np.array(GAM, dtype=np.float64)
    cf = np.zeros((128, 1536), dtype=np.float32)
    cf[:, 0:128] = (idx[None, :] >= idx[:, None]).astype(np.float32)
    cf[:, 128:132] = (gam[None, :] ** (128.0 - idx[:, None])).astype(np.float32)
    cf[:, 132:148] = f32(b_gate)[0].reshape(16, 128).T
    cf[:, 148:156] = f32(ln_mix)[0].reshape(8, 128).T
    cf[:, 156:164] = f32(gn_gain)[0].reshape(8, 128).T
    cf[:, 164:168] = f32(pool_scale)[0].reshape(4, 128).T
    cf[:, 168:176] = f32(ln_mlp)[0].reshape(8, 128).T
    for g in range(4):
        cf[:, 176 + g * 16:176 + (g + 1) * 16] = (1.0 / np.minimum(np.arange(16) + 1.0, 2.0 ** (g + 1)))[None, :]
    cf[:, 512:1536] = f32(ln_final)[None, :]
    pw = np.ascontiguousarray(f32(pool_w)[0].transpose(1, 0, 2)).reshape(128, 512)

    return dict(wall=wall, cosT=cosT, sinT=sinT, cf=cf, cb=cb, pw=pw)


def kernel(x, ln_mix, w_in, b_gate, gn_gain, w_ret_up, pool_w, pool_scale, w_pool_up, w_o, ln_mlp,
           w_up, w_down, ln_final):
    x = np.asarray(x, dtype=np.float32)
    shared = _prep(x, ln_mix, w_in, b_gate, gn_gain, w_ret_up, pool_w, pool_scale, w_pool_up, w_o, ln_mlp,
                   w_up, w_down, ln_final)
    nc, _ = build_program()
    in_maps = [dict(x=np.ascontiguousarray(x[b]), **shared) for b in range(NB)]
    res = run_bass_kernel_spmd(nc, in_maps, core_ids=list(range(NB)))
    return np.stack([np.asarray(r["out"], dtype=np.float32) for r in res.results], axis=0)
```

```python
import numpy as np
import ml_dtypes
from contextlib import ExitStack
import concourse.bass as bass
import concourse.mybir as mybir
from concourse.bass_utils import run_bass_kernel_spmd

F32 = mybir.dt.float32
BF16 = mybir.dt.bfloat16
I32 = mybir.dt.int32
AF = mybir.ActivationFunctionType
ALU = mybir.AluOpType

_ESZ = {F32: 4, BF16: 2, I32: 4}
_G = 256


def ap_keys(ap):
    if isinstance(ap, (tuple, str)):
        return [ap]
    t = ap.tensor
    row = 1
    for s in list(t.shape)[1:]:
        row *= int(s)
    esz = _ESZ[ap.dtype]
    off = int(ap.offset) % row
    hi = off
    for st, cnt in list(ap.ap)[1:]:
        hi += (int(cnt) - 1) * int(st)
    hi += 1
    name = t.name
    if name.startswith("ps"):
        return [(name, 0)]
    return [(name, b) for b in range(off * esz // _G, (hi * esz - 1) // _G + 1)]


class Chan:
    def __init__(self, sem):
        self.sem = sem
        self.count = 0


class Op:
    __slots__ = ("eng", "fn", "deps", "chan", "chanval", "semval", "needs_inc", "idx")


class Sched:
    ENGS = ("pe", "act", "dve", "pool", "sp")

    def __init__(self):
        self.ops = []
        self.lastw = {}
        self.readers = {}

    def add(self, eng, fn, reads=(), writes=(), chan=None):
        op = Op()
        op.eng = eng
        op.fn = fn
        op.chan = chan
        op.chanval = None
        op.semval = None
        op.needs_inc = False
        op.idx = len(self.ops)
        if chan is not None:
            chan.count += 16
            op.chanval = chan.count
        deps = {}
        rk = []
        for a in reads:
            rk.extend(ap_keys(a))
        wk = []
        for a in writes:
            wk.extend(ap_keys(a))
        psr = [k for k in rk if isinstance(k[0], str) and k[0].startswith("ps") and k[1] == 0 and len(k) == 2]
        if psr:
            rk = [k for k in rk if k not in psr]
            wk = wk + [k for k in psr if k not in wk]
        for k in rk:
            w = self.lastw.get(k)
            if w is not None:
                deps[w.idx] = w
        for k in wk:
            w = self.lastw.get(k)
            if w is not None:
                deps[w.idx] = w
            r = self.readers.get(k)
            if r:
                for o in r.values():
                    deps[o.idx] = o
        rkey = eng if chan is None else ("dma", id(chan))
        for k in rk:
            self.readers.setdefault(k, {})[rkey] = op
        for k in wk:
            self.lastw[k] = op
            self.readers[k] = {}
        deps.pop(op.idx, None)
        best = {}
        for d in deps.values():
            if d.eng == "pe" and eng == "pe" and d.chan is None and chan is None:
                continue
            k = d.eng if d.chan is None else ("dma", id(d.chan))
            if k not in best or best[k].idx < d.idx:
                best[k] = d
        op.deps = list(best.values())
        for d in op.deps:
            if d.chan is None:
                d.needs_inc = True
        self.ops.append(op)
        return op

    def mark(self, name):
        if not hasattr(self, "marks"):
            self.marks = []
        self.marks.append((name, len(self.ops)))

    def emit(self, nc, sems, final_chans=(), limit=None):
        if limit is not None:
            self.ops = self.ops[:limit]
            chmax = {}
            for op in self.ops:
                if op.chan is not None:
                    chmax[id(op.chan)] = (op.chan, op.chanval)
            final_chans = []
            for ch, v in chmax.values():
                c = Chan(ch.sem)
                c.count = v
                final_chans.append(c)
        cnt = {e: 0 for e in self.ENGS}
        for op in self.ops:
            if op.chan is None and op.needs_inc:
                cnt[op.eng] += 1
                op.semval = cnt[op.eng]
        per_eng = {e: [o for o in self.ops if o.eng == e] for e in self.ENGS}
        nwaits = {e: 0 for e in self.ENGS}

        def run(e, handle):
            waited = {}
            for op in per_eng[e]:
                need = {}
                for d in op.deps:
                    if d.chan is not None:
                        s, v = d.chan.sem, d.chanval
                    else:
                        s, v = sems[d.eng], d.semval
                    key = id(s)
                    if waited.get(key, 0) >= v:
                        continue
                    if key not in need or need[key][1] < v:
                        need[key] = (s, v)
                for key, (s, v) in need.items():
                    handle.wait_ge(s, v)
                    waited[key] = v
                    nwaits[e] += 1
                ins = op.fn(handle)
                if op.chan is not None:
                    ins.then_inc(op.chan.sem, 16)
                elif op.needs_inc:
                    ins.then_inc(sems[e], 1)
            if e == "sp":
                for ch in final_chans:
                    if ch.count:
                        handle.wait_ge(ch.sem, ch.count)

        with nc.Block() as block:
            @block.tensor
            def _(h):
                run("pe", h)

            @block.scalar
            def _(h):
                run("act", h)

            @block.vector
            def _(h):
                run("dve", h)

            @block.gpsimd
            def _(h):
                run("pool", h)

            @block.sync
            def _(h):
                run("sp", h)
        self.stats = {e: (len(per_eng[e]), cnt[e], nwaits[e]) for e in self.ENGS}


D = 1024
SEQ = 4096
NB = 8
H = 4
DK = 128
DV = 256
TT = 512
NTB = TT // 128
NT = SEQ // TT
NCH = 32
NRING = 4
EPS = 1e-6
GAM = [1.0 - 2.0 ** (-5.0 - h) for h in range(H)]
MAGIC = 1597463007.0


def build_program(SEQ=SEQ, limit=None):
    NT = SEQ // TT
    nc = bass.Bass("TRN2", target_bir_lowering=False)
    x_d = nc.dram_tensor("x", [SEQ, D], F32, kind="ExternalInput").ap()
    wall_d = nc.dram_tensor("wall", [NCH, 128, 4096], F32, kind="ExternalInput").ap()
    cos_d = nc.dram_tensor("cosT", [SEQ, 64], F32, kind="ExternalInput").ap()
    sin_d = nc.dram_tensor("sinT", [SEQ, 64], F32, kind="ExternalInput").ap()
    cf_d = nc.dram_tensor("cf", [128, 1536], F32, kind="ExternalInput").ap()
    cb_d = nc.dram_tensor("cb", [128, 1152], BF16, kind="ExternalInput").ap()
    pw_d = nc.dram_tensor("pw", [128, 512], F32, kind="ExternalInput").ap()
    out_d = nc.dram_tensor("out", [SEQ, D], F32, kind="ExternalOutput").ap()
    wscr_d = nc.dram_tensor("wscr", [NCH, 128, 4096], BF16, kind="Internal").ap()

    S = Sched()
    with ExitStack() as es:
        def sb(name, shape, dt):
            return es.enter_context(nc.sbuf_tensor(name, shape, dt))

        def sem(name):
            return es.enter_context(nc.semaphore(name))

        ring = [sb("ring%d" % i, [128, 4096], BF16) for i in range(NRING)]
        xs = [sb("xs%d" % i, [128, NTB, D], F32) for i in range(2)]
        big = sb("big", [128, 16384], BF16)
        hT = sb("hT", [128, 8, TT], BF16)
        gat = sb("gat", [128, 16, TT], BF16)
        scr = sb("scr", [128, 2048], F32)
        xn = [sb("xn%d" % i, [128, D], BF16) for i in range(NTB)]
        junk = sb("junk", [128, D], BF16)
        qrot = [sb("qrot%d" % i, [128, 512], BF16) for i in range(NTB)]
        krot = [sb("krot%d" % i, [128, 512], BF16) for i in range(NTB)]
        qT = [sb("qT%d" % i, [128, H, 128], BF16) for i in range(NTB)]
        kT = [sb("kT%d" % i, [128, H, 128], BF16) for i in range(NTB)]
        sT = [sb("sT%d" % i, [128, H, 128], BF16) for i in range(2)]
        St = sb("St", [128, H, DV], F32)
        Sbf = [sb("Sbf%d" % i, [128, H, DV], BF16) for i in range(2)]
        gated = [sb("gated%d" % i, [128, D], BF16) for i in range(2)]
        uh = sb("uh", [128, 4, 528], F32)
        ptmp = [sb("ptmp%d" % i, [128, 528], F32) for i in range(2)]
        pbf = sb("pbf", [128, 4, TT], BF16)
        ypT = sb("ypT", [128, 4, TT], BF16)
        cst = [sb("cst%d" % i, [128, 2, NTB, 64], F32) for i in range(2)]
        cf = sb("cfs", [128, 1536], F32)
        cbt = sb("cbs", [128, 1152], BF16)
        pwf = sb("pwf", [128, 512], F32)
        pwb = sb("pwb", [128, 4, 128], BF16)
        st_rms = sb("st_rms", [128, 64], F32)
        st_gn = sb("st_gn", [128, 64], F32)
        st_gn2 = sb("st_gn2", [128, 64], F32)
        st_fin = sb("st_fin", [128, 64], F32)
        st_fx = sb("st_fx", [128, 64], F32)
        st_eps = sb("st_eps", [128, 64], F32)
        ps = [es.enter_context(nc.psum_tensor("ps%d" % i, [128, 512], F32)) for i in range(8)]

        sems = {e: sem("s_" + e) for e in Sched.ENGS}
        ch_ring = [Chan(sem("c_ring%d" % i)) for i in range(NRING)]
        ch_ringc = [Chan(sem("c_ringc%d" % i)) for i in range(NRING)]
        ch_x = [Chan(sem("c_x%d" % i)) for i in range(2)]
        ch_cs = [Chan(sem("c_cs%d" % i)) for i in range(2)]
        ch_sn = [Chan(sem("c_sn%d" % i)) for i in range(2)]
        ch_out = [Chan(sem("c_out%d" % i)) for i in range(2)]
        ch_outl = [Chan(sem("c_outl%d" % i)) for i in range(NTB)]
        ch_stage = [Chan(sem("c_stage%d" % i)) for i in range(4)]
        ch_scr = [Chan(sem("c_scr%d" % i)) for i in range(NRING)]
        ch_const = [Chan(sem("c_const%d" % i)) for i in range(3)]

        aT = big[:, :].rearrange("p (j t) -> p j t", j=32)
        vv = big[:, 0:4096].rearrange("p (b c) -> p b c", b=NTB)
        vz = big[:, 4096:8192].rearrange("p (b c) -> p b c", b=NTB)
        sg = big[:, 8192:12288].rearrange("p (b c) -> p b c", b=NTB)
        gatedT = big[:, 12288:16384].rearrange("p (k t) -> p k t", k=8)
        stage = [big[:, i * 8192:(i + 1) * 8192].bitcast(F32) for i in range(2)]
        mixedT = hT
        h2T = gat[:, 0:8, :]
        cmask = cf[:, 0:128]
        vzs = cf[:, 128:132]
        bgT = cf[:, 132:148]
        g_mix = cf[:, 148:156]
        g_gn = cf[:, 156:164]
        g_pool = cf[:, 164:168]
        g_mlp = cf[:, 168:176]
        invcnt = cf[:, 176:240].rearrange("p (g j) -> p g j", g=4)
        lnf = cf[:, 512:1536]
        ident = cbt[:, 0:128]
        diagq = cbt[:, 128:640].rearrange("p (h c) -> p h c", h=H)
        diagk = cbt[:, 640:1152].rearrange("p (h c) -> p h c", h=H)
        ssq = st_rms[:, 0:4]
        rv = st_rms[:, 4:8]
        ry = st_rms[:, 8:12]
        rt = st_rms[:, 12:16]
        rh = st_rms[:, 16:20]
        bst = st_gn[:, 0:24].rearrange("p (h s) -> p h s", h=H)
        mv = st_gn[:, 24:32].rearrange("p (h s) -> p h s", h=H)
        nmr = st_gn2[:, 0:4]
        epsc = st_eps[:, 0:1]

        bank_ctr = [0]

        def bank():
            b = ps[bank_ctr[0] % 8]
            bank_ctr[0] += 1
            return b

        def A(eng, fn, reads, writes, chan=None):
            return S.add(eng, fn, reads=reads, writes=writes, chan=chan)

        def mm(out, lhsT, rhs, start, stop):
            A("pe", lambda e: e.matmul(out, lhsT=lhsT, rhs=rhs, start=start, stop=stop), [lhsT, rhs], [out])

        def rsqrt_chain(v, y, t, hh):
            A("dve", lambda e: e.tensor_single_scalar(out=t.bitcast(I32), in_=v.bitcast(I32), scalar=1,
                                                      op=ALU.arith_shift_right), [v], [t])
            A("dve", lambda e: e.tensor_scalar(out=y.bitcast(I32), in0=t.bitcast(I32), scalar1=-1.0, scalar2=MAGIC,
                                               op0=ALU.mult, op1=ALU.add), [t], [y])
            A("dve", lambda e: e.tensor_scalar(out=hh, in0=v, scalar1=-0.5, scalar2=None, op0=ALU.mult), [v], [hh])
            for _ in range(2):
                A("dve", lambda e: e.tensor_tensor(out=t, in0=y, in1=y, op=ALU.mult), [y], [t])
                A("dve", lambda e: e.scalar_tensor_tensor(out=t, in0=t, scalar=1.0, in1=hh, op0=ALU.mult, op1=ALU.mult),
                  [t, hh], [t])
                A("dve", lambda e: e.scalar_tensor_tensor(out=y, in0=t, scalar=1.5, in1=y, op0=ALU.add, op1=ALU.mult),
                  [t, y], [y])

        A("pool", lambda e: e.dma_start(out=cf[:], in_=cf_d), [], [cf[:]], ch_const[0])
        A("pool", lambda e: e.dma_start(out=cbt[:], in_=cb_d), [], [cbt[:]], ch_const[1])
        A("pool", lambda e: e.dma_start(out=pwf[:], in_=pw_d), [], [pwf[:]], ch_const[2])
        A("dve", lambda e: e.tensor_copy(out=pwb[:].rearrange("p g d -> p (g d)"), in_=pwf[:]), [pwf[:]], [pwb[:]])
        A("dve", lambda e: e.memset(St[:], 0.0), [], [St[:]])
        A("dve", lambda e: e.memset(Sbf[0][:], 0.0), [], [Sbf[0][:]])
        A("dve", lambda e: e.memset(uh[:], 0.0), [], [uh[:]])
        A("dve", lambda e: e.memset(epsc, EPS), [], [epsc])

        S.mark('consts_done')
        def chunk_gain(j):
            if j <= 10:
                return g_mix, 8
            if j in (11, 13):
                return g_gn, 8
            if j == 12:
                return g_pool, 4
            if 16 <= j <= 23:
                return g_mlp, 8
            return None, 1

        S.mark('prologue_done')
        issued = [0]
        TOTAL = NT * NCH

        stg = [xs[1][:, p, :] for p in range(4)]

        def store_chunk(j):
            sl = j % NRING
            A("sp", lambda e: e.dma_start(out=wscr_d[j], in_=ring[sl][:]), [ring[sl][:]], [("dram_wscr", j)], ch_scr[sl])

        def stage_chunk(j):
            sl = j % NRING
            dst = ring[sl]
            gain, nk = chunk_gain(j)
            if gain is None:
                A("pool", lambda e: e.dma_start(out=dst[:], in_=wall_d[j]), [], [dst[:]], ch_ringc[sl])
                if j > 0:
                    store_chunk(j - 1)
                if j == NCH - 1:
                    store_chunk(j)
                return
            for p in range(4):
                A("sp", lambda e, p=p: e.dma_start(out=stg[p], in_=wall_d[j][:, p * 1024:(p + 1) * 1024]), [], [stg[p]], ch_stage[p])
            for p in range(4):
                eng = "act" if p % 2 == 0 else "dve"
                if gain is None:
                    pieces = [(stg[p], dst[:, p * 1024:(p + 1) * 1024], None)]
                else:
                    w = 4096 // nk
                    per = nk // 4
                    pieces = []
                    for i in range(per):
                        kc = p * per + i
                        pieces.append((stg[p][:, i * w:(i + 1) * w], dst[:, kc * w:(kc + 1) * w], gain[:, kc:kc + 1]))
                for src, d_, gcol in pieces:
                    if gcol is None:
                        if eng == "act":
                            A("act", lambda e, src=src, d_=d_: e.activation(out=d_, in_=src, func=AF.Copy), [src], [d_])
                        else:
                            A("dve", lambda e, src=src, d_=d_: e.tensor_copy(out=d_, in_=src), [src], [d_])
                    elif eng == "act":
                        A("act", lambda e, src=src, d_=d_, gcol=gcol: e.activation(out=d_, in_=src, func=AF.Copy, scale=gcol),
                          [src, gcol], [d_])
                    else:
                        A("dve", lambda e, src=src, d_=d_, gcol=gcol: e.tensor_scalar(out=d_, in0=src, scalar1=gcol, scalar2=None,
                                                                                     op0=ALU.mult), [src, gcol], [d_])
            if j > 0:
                store_chunk(j - 1)
            if j == NCH - 1:
                store_chunk(j)

        def prefetch_upto(n):
            while issued[0] <= min(n, TOTAL - 1):
                g = issued[0]
                j = g % NCH
                sl = g % NRING
                if g < NCH:
                    stage_chunk(j)
                else:
                    A("sp", lambda e, j=j, sl=sl: e.dma_start(out=ring[sl][:], in_=wscr_d[j]), [("dram_wscr", j)], [ring[sl][:]], ch_ring[sl])
                issued[0] += 1

        D0 = 4

        x1_loaded = [False]

        def pf(g):
            prefetch_upto(g + (NRING if g + NRING >= NCH else D0))
            if NT > 1 and issued[0] >= 24 and not x1_loaded[0]:
                x1_loaded[0] = True
                load_x(1)

        def wchunk(g):
            return ring[g % NRING][:, :].rearrange("p (k c) -> p k c", k=8)

        def load_x(t):
            sl = t % 2
            A("pool", lambda e: e.dma_start(out=xs[sl][:], in_=x_d[t * TT:(t + 1) * TT, :].rearrange("(c p) d -> p c d", p=128)),
              [], [xs[sl][:]], ch_x[sl])
            A("pool", lambda e: e.dma_start(out=cst[sl][:, 0], in_=cos_d[t * TT:(t + 1) * TT, :].rearrange("(c p) f -> p c f", p=128)),
              [], [cst[sl][:, 0]], ch_cs[sl])
            A("pool", lambda e: e.dma_start(out=cst[sl][:, 1], in_=sin_d[t * TT:(t + 1) * TT, :].rearrange("(c p) f -> p c f", p=128)),
              [], [cst[sl][:, 1]], ch_sn[sl])

        def norm_a(xsl, groups=None):
            if groups is None:
                groups = [list(range(NTB))]
            for grp in groups:
                lo, hi = grp[0], grp[-1] + 1
                for tb in grp:
                    A("act", lambda e, tb=tb: e.activation(out=junk[:], in_=xsl[:, tb, :], func=AF.Square, accum_out=ssq[:, tb:tb + 1]),
                      [xsl[:, tb, :]], [junk[:], ssq[:, tb:tb + 1]])
                A("act", lambda e, lo=lo, hi=hi: e.activation(out=rv[:, lo:hi], in_=ssq[:, lo:hi], func=AF.Sqrt, scale=1.0 / D, bias=epsc),
                  [ssq[:, lo:hi], epsc], [rv[:, lo:hi]])
                A("dve", lambda e, lo=lo, hi=hi: e.reciprocal(out=ry[:, lo:hi], in_=rv[:, lo:hi]), [rv[:, lo:hi]], [ry[:, lo:hi]])
                for tb in grp:
                    xb = xn[tb]
                    A("dve", lambda e, tb=tb, xb=xb: e.tensor_scalar(out=xb[:], in0=xsl[:, tb, :], scalar1=ry[:, tb:tb + 1], scalar2=None,
                                                                   op0=ALU.mult), [xsl[:, tb, :], ry[:, tb:tb + 1]], [xb[:]])

        def norm_b(dstT, tbs=None):
            for tb in (range(NTB) if tbs is None else tbs):
                xb = xn[tb]
                pb = bank()
                pbv = pb[:].bitcast(BF16).rearrange("p (k c) -> p k c", k=8)
                for kc in range(8):
                    A("pe", lambda e, kc=kc, xb=xb, pbv=pbv: e.transpose(out=pbv[:, kc, :], in_=xb[:, kc * 128:(kc + 1) * 128], identity=ident),
                      [xb[:, kc * 128:(kc + 1) * 128], ident], [pbv[:, kc, :]])
                dd = dstT[:, :, tb * 128:(tb + 1) * 128]
                if tb % 2 == 0:
                    A("dve", lambda e, dd=dd, pbv=pbv: e.tensor_copy(out=dd, in_=pbv), [pbv], [dd])
                else:
                    A("act", lambda e, dd=dd, pbv=pbv: e.activation(out=dd, in_=pbv, func=AF.Copy), [pbv], [dd])

        def tokmajor_chunk(g, srcT, evac):
            w = wchunk(g)
            for tb in range(NTB):
                pb = bank()
                for kc in range(8):
                    mm(pb[:], srcT[:, kc, tb * 128:(tb + 1) * 128], w[:, kc, :], kc == 0, kc == 7)
                evac(tb, pb)
            pf(g)

        def featmajor_chunk(g, srcT, evac, nk=8):
            w = wchunk(g)
            for cb in range(4):
                pb = bank()
                for kc in range(nk):
                    mm(pb[:], w[:, kc, cb * 128:(cb + 1) * 128], srcT[:, kc, :], kc == 0, kc == nk - 1)
                evac(cb, pb)
            pf(g)

        def mixer(t, hookA=None, hookB=None):
            g0 = t * NCH
            xsl = xs[t % 2]
            cs = cst[t % 2]

            def rotary(pb, tb, dst):
                pv = pb[:].rearrange("p (h t f) -> p h t f", h=H, t=2)
                t1 = scr[:, (tb % 2) * 1024:(tb % 2) * 1024 + 512]
                t2 = scr[:, (tb % 2) * 1024 + 512:(tb % 2) * 1024 + 1024]
                t1v = t1.rearrange("p (h t f) -> p h t f", h=H, t=2)
                t2v = t2.rearrange("p (h t f) -> p h t f", h=H, t=2)
                dv = dst[:].rearrange("p (h t f) -> p h t f", h=H, t=2)
                cosb = cs[:, 0, tb, :].unsqueeze(1).unsqueeze(1).broadcast_to([128, H, 2, 64])
                sinb = cs[:, 1, tb, :].unsqueeze(1).broadcast_to([128, H, 64])
                A("dve", lambda e: e.tensor_tensor(out=t1v, in0=pv, in1=cosb, op=ALU.mult), [pb[:], cs[:, 0, tb, :]], [t1])
                A("dve", lambda e: e.tensor_tensor(out=t2v[:, :, 0, :], in0=pv[:, :, 1, :], in1=sinb, op=ALU.mult),
                  [pb[:], cs[:, 1, tb, :]], [t2])
                A("dve", lambda e: e.tensor_tensor(out=t2v[:, :, 1, :], in0=pv[:, :, 0, :], in1=sinb, op=ALU.mult),
                  [pb[:], cs[:, 1, tb, :]], [t2])
                A("pool", lambda e: e.tensor_tensor(out=dv[:, :, 0, :], in0=t1v[:, :, 0, :], in1=t2v[:, :, 0, :], op=ALU.subtract),
                  [t1, t2], [dst[:]])
                A("pool", lambda e: e.tensor_tensor(out=dv[:, :, 1, :], in0=t1v[:, :, 1, :], in1=t2v[:, :, 1, :], op=ALU.add),
                  [t1, t2], [dst[:]])

            def diag_T(src, dg, dst):
                pb = bank()
                pv = pb[:].rearrange("p (h c) -> p h c", h=H)
                for h in range(H):
                    mm(pv[:, h, :], src[:, h * 128:(h + 1) * 128], dg[:, h, :], True, True)
                A("dve", lambda e: e.tensor_copy(out=dst[:], in_=pv), [pb[:]], [dst[:]])

            def ev_q(tb, pb):
                rotary(pb, tb, qrot[tb])

            def ev_k(tb, pb):
                rotary(pb, tb, krot[tb])

            def ev_v(i):
                def f(tb, pb):
                    A("dve", lambda e: e.tensor_copy(out=vv[:, tb, i * 512:(i + 1) * 512], in_=pb[:]), [pb[:]],
                      [vv[:, tb, i * 512:(i + 1) * 512]])
                    for hl in range(2):
                        h = 2 * i + hl
                        o_ = vz[:, tb, h * 256:(h + 1) * 256]
                        A("act", lambda e, o_=o_, hl=hl, h=h: e.activation(out=o_, in_=pb[:, hl * 256:(hl + 1) * 256], func=AF.Copy,
                                                                         scale=vzs[:, h:h + 1]), [pb[:], vzs], [o_])
                return f

            def ev_g(i):
                def f(tb, pb):
                    o_ = sg[:, tb, i * 512:(i + 1) * 512]
                    A("act", lambda e: e.activation(out=o_, in_=pb[:], func=AF.Silu), [pb[:]], [o_])
                return f

            def ev_pu(cb, pb):
                o_ = uh[:, cb, 16:528]
                A("act", lambda e: e.activation(out=o_, in_=pb[:], func=AF.Copy), [pb[:]], [o_])

            def ev_gate(i):
                def f(cb, pb):
                    blk = i * 4 + cb
                    o_ = gat[:, blk, :]
                    A("act", lambda e: e.activation(out=o_, in_=pb[:], func=AF.Sigmoid, bias=bgT[:, blk:blk + 1]),
                      [pb[:], bgT], [o_])
                return f

            def R2(tb):
                pb = bank()
                pv = pb[:].rearrange("p (h c) -> p h c", h=H)
                for h in range(H):
                    mm(pv[:, h, :], kT[tb][:, h, :], qT[tb][:, h, :], True, True)
                sTb = sT[tb % 2]
                cmb = cmask.unsqueeze(1).broadcast_to([128, H, 128])
                A("dve", lambda e: e.tensor_tensor(out=sTb[:], in0=pv, in1=cmb, op=ALU.mult), [pb[:], cmask], [sTb[:]])
                for hp in range(2):
                    pk = bank()
                    for hl in range(2):
                        h = hp * 2 + hl
                        mm(pk[:, hl * 256:(hl + 1) * 256], krot[tb][:, h * 128:(h + 1) * 128], vz[:, tb, h * 256:(h + 1) * 256], True, True)
                    for hl in range(2):
                        h = hp * 2 + hl
                        A("dve", lambda e, h=h, hl=hl, pk=pk: e.scalar_tensor_tensor(out=St[:, h, :], in0=St[:, h, :], scalar=float(GAM[h] ** 128),
                                                                                    in1=pk[:, hl * 256:(hl + 1) * 256], op0=ALU.mult, op1=ALU.add),
                          [St[:, h, :], pk[:, hl * 256:(hl + 1) * 256]], [St[:, h, :]])
                nb = Sbf[(tb + 1) % 2]
                A("act", lambda e: e.activation(out=nb[:], in_=St[:], func=AF.Copy), [St[:]], [nb[:]])

            def R3a(tb):
                sTb = sT[tb % 2]
                sb_ = Sbf[tb % 2]
                pbs = []
                for hp in range(2):
                    po = bank()
                    pbs.append(po)
                    for hl in range(2):
                        h = hp * 2 + hl
                        o_ = po[:, hl * 256:(hl + 1) * 256]
                        mm(o_, sTb[:, h, :], vv[:, tb, h * 256:(h + 1) * 256], True, False)
                        mm(o_, qT[tb][:, h, :], sb_[:, h, :], False, True)
                for h in range(H):
                    src = pbs[h // 2][:, (h % 2) * 256:(h % 2 + 1) * 256]
                    A("dve", lambda e, h=h, src=src: e.bn_stats(out=bst[:, h, :], in_=src), [src], [bst[:, h, :]])
                    A("dve", lambda e, h=h: e.bn_aggr(out=mv[:, h, :], in_=bst[:, h, :]), [bst[:, h, :]], [mv[:, h, :]])
                gv = st_gn2[:, 4:8]
                gy = st_gn2[:, 8:12]
                gt = st_gn2[:, 12:16]
                gh = st_gn2[:, 16:20]
                A("dve", lambda e: e.tensor_scalar(out=gv, in0=mv[:, :, 1], scalar1=EPS, scalar2=None, op0=ALU.add), [mv], [gv])
                rsqrt_chain(gv, gy, gt, gh)
                A("dve", lambda e: e.scalar_tensor_tensor(out=nmr, in0=mv[:, :, 0], scalar=-1.0, in1=gy, op0=ALU.mult, op1=ALU.mult),
                  [mv, gy], [nmr])
                gb = gated[tb % 2]
                for h in range(H):
                    src = pbs[h // 2][:, (h % 2) * 256:(h % 2 + 1) * 256]
                    tmp = scr[:, 1024 + (h % 2) * 256:1024 + (h % 2 + 1) * 256]
                    A("act", lambda e, h=h, src=src, tmp=tmp: e.activation(out=tmp, in_=src, func=AF.Identity, scale=gy[:, h:h + 1],
                                                                         bias=nmr[:, h:h + 1]), [src, gy, nmr], [tmp])
                    o_ = gb[:, h * 256:(h + 1) * 256]
                    A("dve", lambda e, h=h, tmp=tmp, o_=o_: e.tensor_tensor(out=o_, in0=tmp, in1=sg[:, tb, h * 256:(h + 1) * 256], op=ALU.mult),
                      [tmp, sg[:, tb, h * 256:(h + 1) * 256]], [o_])

            def R3b(tb):
                gb = gated[tb % 2]
                pb = bank()
                pbv = pb[:].bitcast(BF16).rearrange("p (k c) -> p k c", k=8)
                for kc in range(8):
                    A("pe", lambda e, kc=kc, pbv=pbv: e.transpose(out=pbv[:, kc, :], in_=gb[:, kc * 128:(kc + 1) * 128], identity=ident),
                      [gb[:, kc * 128:(kc + 1) * 128], ident], [pbv[:, kc, :]])
                dd = gatedT[:, :, tb * 128:(tb + 1) * 128]
                A("dve", lambda e: e.tensor_copy(out=dd, in_=pbv), [pbv], [dd])

            def pooling():
                for g in range(4):
                    cur = uh[:, g, :]
                    src = cur
                    for lv in range(g + 1):
                        sh = 1 << lv
                        lo = 2 * sh - 1
                        dst = ptmp[lv % 2]
                        A("pool", lambda e, src=src, dst=dst, sh=sh, lo=lo: e.tensor_tensor(out=dst[:, lo:528], in0=src[:, lo:528],
                                                                                            in1=src[:, lo - sh:528 - sh], op=ALU.add),
                          [src[:, lo - sh:528]], [dst[:, lo:528]])
                        src = dst
                    wd = float(1 << (g + 1))
                    other = ptmp[(g + 1) % 2]
                    iw = invcnt[:, g, 15:16].broadcast_to([128, 512])
                    A("pool", lambda e, g=g, src=src, other=other, iw=iw: e.tensor_tensor(out=other[:, 16:528], in0=src[:, 16:528], in1=iw, op=ALU.mult),
                      [src[:, 16:528], invcnt[:, g, 15:16]], [other[:, 16:528]])
                    A("pool", lambda e, g=g, other=other: e.tensor_tensor(out=pbf[:, g, :], in0=other[:, 16:528], in1=uh[:, g, 16:528], op=ALU.subtract),
                      [other[:, 16:528], uh[:, g, 16:528]], [pbf[:, g, :]])
                    if t == 0:
                        fx = st_fx[:, 0:16]
                        A("pool", lambda e, g=g, src=src: e.tensor_tensor(out=fx, in0=src[:, 16:32], in1=invcnt[:, g, :], op=ALU.mult),
                          [src[:, 16:32], invcnt[:, g, :]], [fx])
                        A("pool", lambda e, g=g: e.tensor_tensor(out=pbf[:, g, 0:16], in0=fx, in1=uh[:, g, 16:32], op=ALU.subtract),
                          [fx, uh[:, g, 16:32]], [pbf[:, g, 0:16]])
                A("pool", lambda e: e.tensor_copy(out=uh[:, :, 0:16], in_=uh[:, :, 512:528]), [uh[:, :, 512:528]], [uh[:, :, 0:16]])

            tokmajor_chunk(g0 + 0, hT, ev_q)
            tokmajor_chunk(g0 + 1, hT, ev_k)
            for tb in range(NTB):
                diag_T(qrot[tb], diagq, qT[tb])
            tokmajor_chunk(g0 + 2, hT, ev_v(0))
            for tb in range(NTB):
                diag_T(krot[tb], diagk, kT[tb])
            tokmajor_chunk(g0 + 3, hT, ev_v(1))
            if hookA is not None:
                hookA()
            tokmajor_chunk(g0 + 4, hT, ev_g(0))
            R2(0)
            tokmajor_chunk(g0 + 5, hT, ev_g(1))
            R3a(0)
            R2(1)
            featmajor_chunk(g0 + 6, hT, ev_pu)
            pooling()
            R3a(1)
            R2(2)
            featmajor_chunk(g0 + 7, hT, ev_gate(0))
            R3b(0)
            R3a(2)
            R2(3)
            featmajor_chunk(g0 + 8, hT, ev_gate(1))
            R3b(1)
            R3a(3)
            featmajor_chunk(g0 + 9, hT, ev_gate(2))
            R3b(2)
            featmajor_chunk(g0 + 10, hT, ev_gate(3))
            R3b(3)

            S.mark('win_done%d' % t)
            for g in range(4):
                pb = bank()
                mm(pb[:], pwb[:, g, :], pbf[:, g, :], True, True)
                A("act", lambda e, g=g, pb=pb: e.activation(out=ypT[:, g, :], in_=pb[:], func=AF.Copy), [pb[:]], [ypT[:, g, :]])

            S.mark('poolbr_done%d' % t)
            wret0 = wchunk(g0 + 11)
            wpool = ring[(g0 + 12) % NRING][:, :].rearrange("p (k c) -> p k c", k=4)
            wret1 = wchunk(g0 + 13)
            pr = {}
            pp = {}

            def ret_blk(w, cb):
                pb = bank()
                pr[cb] = pb
                for kc in range(8):
                    mm(pb[:], w[:, kc, (cb % 4) * 128:(cb % 4 + 1) * 128], gatedT[:, kc, :], kc == 0, kc == 7)

            def pool_blk(cb):
                pb = bank()
                pp[cb] = pb
                for kc in range(4):
                    mm(pb[:], wpool[:, kc, cb * 128:(cb + 1) * 128], ypT[:, kc, :], kc == 0, kc == 3)

            def mix(cb):
                m1 = scr[:, (cb % 2) * 1024:(cb % 2) * 1024 + 512]
                m2 = scr[:, (cb % 2) * 1024 + 512:(cb % 2) * 1024 + 1024]
                A("dve", lambda e: e.tensor_tensor(out=m1, in0=pr[cb][:], in1=gat[:, cb, :], op=ALU.mult), [pr[cb][:], gat[:, cb, :]], [m1])
                A("dve", lambda e: e.tensor_tensor(out=m2, in0=pp[cb][:], in1=gat[:, 8 + cb, :], op=ALU.mult),
                  [pp[cb][:], gat[:, 8 + cb, :]], [m2])
                A("pool", lambda e: e.tensor_tensor(out=mixedT[:, cb, :], in0=m1, in1=m2, op=ALU.add), [m1, m2], [mixedT[:, cb, :]])

            for cb in range(4):
                ret_blk(wret0, cb)
                pool_blk(cb)
                mix(cb)
            pf(g0 + 11)
            for cb in range(4, 8):
                pool_blk(cb)
                if cb == 7:
                    pf(g0 + 12)
                ret_blk(wret1, cb)
                mix(cb)
            pf(g0 + 13)

            S.mark('merge_done%d' % t)
            if hookB is not None:
                hookB()
            wo = [wchunk(g0 + 14), wchunk(g0 + 15)]
            for tb in range(NTB):
                for i in range(2):
                    pb = bank()
                    for kc in range(8):
                        mm(pb[:], mixedT[:, kc, tb * 128:(tb + 1) * 128], wo[i][:, kc, :], kc == 0, kc == 7)
                    o_ = xsl[:, tb, i * 512:(i + 1) * 512]
                    A("dve", lambda e, o_=o_, pb=pb: e.tensor_tensor(out=o_, in0=pb[:], in1=o_, op=ALU.add), [pb[:], o_], [o_])
                if tb >= 2:
                    norm_b(h2T, [tb - 2])
                norm_a(xsl, [[tb]])
            pf(g0 + 14)
            pf(g0 + 15)

        def ffn_up(t):
            g0 = t * NCH
            xsl = xs[t % 2]

            def relu2(n, pb):
                rtmp = scr[:, (n % 2) * 512:(n % 2 + 1) * 512]
                A("act", lambda e: e.activation(out=rtmp, in_=pb[:], func=AF.Relu), [pb[:]], [rtmp])
                eng = "pool" if n % 2 == 0 else "dve"
                A(eng, lambda e: e.tensor_tensor(out=aT[:, n, :], in0=rtmp, in1=rtmp, op=ALU.mult), [rtmp], [aT[:, n, :]])

            w0, w1 = wchunk(g0 + 16), wchunk(g0 + 17)
            early = [(0, w0, cb) for cb in range(4)] + [(1, w1, cb) for cb in range(2)]
            banks = []
            for (j, w, cb) in early:
                pb = bank()
                banks.append(pb)
                for kc in range(8):
                    mm(pb[:, 0:256], w[:, kc, cb * 128:(cb + 1) * 128], h2T[:, kc, 0:256], kc == 0, kc == 7)
            norm_b(h2T, [2, 3])
            for (j, w, cb), pb in zip(early, banks):
                for kc in range(8):
                    mm(pb[:, 256:512], w[:, kc, cb * 128:(cb + 1) * 128], h2T[:, kc, 256:512], kc == 0, kc == 7)
                relu2(j * 4 + cb, pb)
                if (j, cb) == (0, 3):
                    pf(g0 + 16)
            for cb in range(2, 4):
                pb = bank()
                for kc in range(8):
                    mm(pb[:], w1[:, kc, cb * 128:(cb + 1) * 128], h2T[:, kc, :], kc == 0, kc == 7)
                relu2(4 + cb, pb)
            pf(g0 + 17)
            for j in range(2, 8):
                def ev_u(cb, pb, j=j):
                    relu2(j * 4 + cb, pb)
                featmajor_chunk(g0 + 16 + j, h2T, ev_u)
                if j == 3 and 1 <= t < NT - 1:
                    norm_a(xs[(t + 1) % 2])

        x1_normed = [False]

        def ffn_down(t):
            g0 = t * NCH
            xsl = xs[t % 2]
            for half in range(2):
                pbs = [bank() for _ in range(NTB)]
                for kg in range(4):
                    g = g0 + 24 + half * 4 + kg
                    w = wchunk(g)
                    for tb in range(NTB):
                        for kcl in range(8):
                            kc = kg * 8 + kcl
                            mm(pbs[tb][:], aT[:, kc, tb * 128:(tb + 1) * 128], w[:, kcl, :], kc == 0, kc == 31)
                    pf(g)
                    if t == 0 and NT > 1 and x1_loaded[0] and not x1_normed[0]:
                        x1_normed[0] = True
                        norm_a(xs[1])
                for tb in range(NTB):
                    o_ = xsl[:, tb, half * 512:(half + 1) * 512]
                    A("dve", lambda e, o_=o_, pb=pbs[tb]: e.tensor_tensor(out=o_, in0=pb[:], in1=o_, op=ALU.add), [pbs[tb][:], o_], [o_])

        def final(t):
            sl = t % 2
            xsl = xs[sl]
            fs = st_fin[:, 0:4]
            fv = st_fin[:, 4:8]
            fy = st_fin[:, 8:12]
            ft = st_fin[:, 12:16]
            fh = st_fin[:, 16:20]
            if t == NT - 1:
                for tb in range(NTB):
                    c = slice(tb, tb + 1)
                    A("act", lambda e, tb=tb, c=c: e.activation(out=junk[:], in_=xsl[:, tb, :], func=AF.Square, accum_out=fs[:, c]),
                      [xsl[:, tb, :]], [junk[:], fs[:, c]])
                    A("act", lambda e, c=c: e.activation(out=fv[:, c], in_=fs[:, c], func=AF.Sqrt, scale=1.0 / D, bias=epsc),
                      [fs[:, c], epsc], [fv[:, c]])
                    A("dve", lambda e, c=c: e.reciprocal(out=fy[:, c], in_=fv[:, c]), [fv[:, c]], [fy[:, c]])
                    A("dve", lambda e, tb=tb, c=c: e.scalar_tensor_tensor(out=xsl[:, tb, :], in0=xsl[:, tb, :], scalar=fy[:, c], in1=lnf,
                                                                         op0=ALU.mult, op1=ALU.mult), [xsl[:, tb, :], fy[:, c], lnf], [xsl[:, tb, :]])
                    r0 = t * TT + tb * 128
                    A("pool", lambda e, tb=tb, r0=r0: e.dma_start(out=out_d[r0:r0 + 128, :], in_=xsl[:, tb, :]),
                      [xsl[:, tb, :]], [("out", t, tb)], ch_outl[tb])
                return
            for tb in range(NTB):
                A("act", lambda e, tb=tb: e.activation(out=junk[:], in_=xsl[:, tb, :], func=AF.Square, accum_out=fs[:, tb:tb + 1]),
                  [xsl[:, tb, :]], [junk[:], fs[:, tb:tb + 1]])
            A("act", lambda e: e.activation(out=fv, in_=fs, func=AF.Sqrt, scale=1.0 / D, bias=epsc), [fs, epsc], [fv])
            A("dve", lambda e: e.reciprocal(out=fy, in_=fv), [fv], [fy])
            for tb in range(NTB):
                A("dve", lambda e, tb=tb: e.scalar_tensor_tensor(out=xsl[:, tb, :], in0=xsl[:, tb, :], scalar=fy[:, tb:tb + 1], in1=lnf,
                                                                 op0=ALU.mult, op1=ALU.mult), [xsl[:, tb, :], fy[:, tb:tb + 1], lnf], [xsl[:, tb, :]])
            A("pool", lambda e: e.dma_start(out=out_d[t * TT:(t + 1) * TT, :].rearrange("(c p) d -> p c d", p=128), in_=xsl[:]),
              [xsl[:]], [("out", t)], ch_out[sl])

        load_x(0)
        prefetch_upto(0)
        norm_a(xs[0])
        norm_b(hT)
        prefetch_upto(D0 - 1)
        for t in range(NT):
            hookA = (lambda t=t: final(t - 1)) if t >= 2 else None
            hookB = (lambda t=t: load_x(t + 1)) if 1 <= t < NT - 1 else None
            mixer(t, hookA, hookB)
            S.mark('mixer_done%d' % t)
            ffn_up(t)
            S.mark('ffn_up_done%d' % t)
            if 1 <= t < NT - 1:
                norm_b(hT)
            ffn_down(t)
            S.mark('ffn_down_done%d' % t)
            if t == 0 and NT > 1:
                assert x1_normed[0]
                norm_b(hT)
                final(0)
            elif t == NT - 1:
                final(t)
        S.emit(nc, sems, final_chans=ch_out + ch_outl, limit=limit)
    return nc, S


def _host_consts():
    pos = np.arange(SEQ, dtype=np.float32)
    inv_freq = (10000.0 ** (-np.arange(64, dtype=np.float32) * 2.0 / 128.0)).astype(np.float32)
    ang = (pos[:, None] * inv_freq[None, :]).astype(np.float32)
    cosT = np.cos(ang.astype(np.float64)).astype(np.float32)
    sinT = np.sin(ang.astype(np.float64)).astype(np.float32)
    idx = np.arange(128, dtype=np.float64)
    gam = np.array(GAM, dtype=np.float64)
    cb = np.zeros((128, 1152), dtype=np.float32)
    cb[:, 0:128] = np.eye(128)
    for h in range(H):
        cb[:, 128 + h * 128:128 + (h + 1) * 128] = np.diag(gam[h] ** idx * DK ** -0.5)
        cb[:, 640 + h * 128:640 + (h + 1) * 128] = np.diag(gam[h] ** (-idx))
    cb = cb.astype(ml_dtypes.bfloat16)
    return cosT, sinT, cb


def _prep(x, ln_mix, w_in, b_gate, gn_gain, w_ret_up, pool_w, pool_scale, w_pool_up, w_o, ln_mlp,
           w_up, w_down, ln_final):
    x = np.asarray(x, dtype=np.float32)
    f32 = lambda a: np.asarray(a, dtype=np.float32)
    w_in, w_ret_up, w_pool_up, w_o, w_up, w_down = map(f32, (w_in, w_ret_up, w_pool_up, w_o, w_up, w_down))

    def rows8(w):
        return np.ascontiguousarray(w.reshape(8, 128, 512).transpose(1, 0, 2)).reshape(128, 4096)

    wall = np.empty((NCH, 128, 4096), dtype=np.float32)
    for j in range(11):
        wall[j] = rows8(w_in[0][:, j * 512:(j + 1) * 512])
    wall[11] = rows8(w_ret_up[0][:, 0:512])
    wall[12] = np.ascontiguousarray(w_pool_up[0].reshape(4, 128, 1024).transpose(1, 0, 2)).reshape(128, 4096)
    wall[13] = rows8(w_ret_up[0][:, 512:1024])
    wall[14] = rows8(w_o[0][:, 0:512])
    wall[15] = rows8(w_o[0][:, 512:1024])
    for j in range(8):
        wall[16 + j] = rows8(w_up[0][:, j * 512:(j + 1) * 512])
    for half in range(2):
        for kg in range(4):
            wall[24 + half * 4 + kg] = rows8(w_down[0][kg * 1024:(kg + 1) * 1024, half * 512:(half + 1) * 512])

    cosT, sinT, cb = _host_consts()
    idx = np.arange(128, dtype=np.float64)
    gam = np.array(GAM, dtype=np.float64)
    cf = np.zeros((128, 1536), dtype=np.float32)
    cf[:, 0:128] = (idx[None, :] >= idx[:, None]).astype(np.float32)
    cf[:, 128:132] = (gam[None, :] ** (128.0 - idx[:, None])).astype(np.float32)
    cf[:, 132:148] = f32(b_gate)[0].reshape(16, 128).T
    cf[:, 148:156] = f32(ln_mix)[0].reshape(8, 128).T
    cf[:, 156:164] = f32(gn_gain)[0].reshape(8, 128).T
    cf[:, 164:168] = f32(pool_scale)[0].reshape(4, 128).T
    cf[:, 168:176] = f32(ln_mlp)[0].reshape(8, 128).T
    for g in range(4):
        cf[:, 176 + g * 16:176 + (g + 1) * 16] = (1.0 / np.minimum(np.arange(16) + 1.0, 2.0 ** (g + 1)))[None, :]
    cf[:, 512:1536] = f32(ln_final)[None, :]
    pw = np.ascontiguousarray(f32(pool_w)[0].transpose(1, 0, 2)).reshape(128, 512)

    return dict(wall=wall, cosT=cosT, sinT=sinT, cf=cf, cb=cb, pw=pw)


def kernel(x, ln_mix, w_in, b_gate, gn_gain, w_ret_up, pool_w, pool_scale, w_pool_up, w_o, ln_mlp,
           w_up, w_down, ln_final):
    x = np.asarray(x, dtype=np.float32)
    shared = _prep(x, ln_mix, w_in, b_gate, gn_gain, w_ret_up, pool_w, pool_scale, w_pool_up, w_o, ln_mlp,
                   w_up, w_down, ln_final)
    nc, _ = build_program()
    in_maps = [dict(x=np.ascontiguousarray(x[b]), **shared) for b in range(NB)]
    res = run_bass_kernel_spmd(nc, in_maps, core_ids=list(range(NB)))
    return np.stack([np.asarray(r["out"], dtype=np.float32) for r in res.results], axis=0)
```

```python
import numpy as np
import ml_dtypes
from contextlib import ExitStack
import concourse.bass as bass
import concourse.mybir as mybir
from concourse.bass_utils import run_bass_kernel_spmd

F32 = mybir.dt.float32
BF16 = mybir.dt.bfloat16
I32 = mybir.dt.int32
AF = mybir.ActivationFunctionType
ALU = mybir.AluOpType

_ESZ = {F32: 4, BF16: 2, I32: 4}
_G = 256


def ap_keys(ap):
    if isinstance(ap, (tuple, str)):
        return [ap]
    t = ap.tensor
    row = 1
    for s in list(t.shape)[1:]:
        row *= int(s)
    esz = _ESZ[ap.dtype]
    off = int(ap.offset) % row
    hi = off
    for st, cnt in list(ap.ap)[1:]:
        hi += (int(cnt) - 1) * int(st)
    hi += 1
    name = t.name
    if name.startswith("ps"):
        return [(name, 0)]
    return [(name, b) for b in range(off * esz // _G, (hi * esz - 1) // _G + 1)]


class Chan:
    def __init__(self, sem):
        self.sem = sem
        self.count = 0


class Op:
    __slots__ = ("eng", "fn", "deps", "chan", "chanval", "semval", "needs_inc", "idx")


class Sched:
    ENGS = ("pe", "act", "dve", "pool", "sp")

    def __init__(self):
        self.ops = []
        self.lastw = {}
        self.readers = {}

    def add(self, eng, fn, reads=(), writes=(), chan=None):
        op = Op()
        op.eng = eng
        op.fn = fn
        op.chan = chan
        op.chanval = None
        op.semval = None
        op.needs_inc = False
        op.idx = len(self.ops)
        if chan is not None:
            chan.count += 16
            op.chanval = chan.count
        deps = {}
        rk = []
        for a in reads:
            rk.extend(ap_keys(a))
        wk = []
        for a in writes:
            wk.extend(ap_keys(a))
        psr = [k for k in rk if isinstance(k[0], str) and k[0].startswith("ps") and k[1] == 0 and len(k) == 2]
        if psr:
            rk = [k for k in rk if k not in psr]
            wk = wk + [k for k in psr if k not in wk]
        for k in rk:
            w = self.lastw.get(k)
            if w is not None:
                deps[w.idx] = w
        for k in wk:
            w = self.lastw.get(k)
            if w is not None:
                deps[w.idx] = w
            r = self.readers.get(k)
            if r:
                for o in r.values():
                    deps[o.idx] = o
        rkey = eng if chan is None else ("dma", id(chan))
        for k in rk:
            self.readers.setdefault(k, {})[rkey] = op
        for k in wk:
            self.lastw[k] = op
            self.readers[k] = {}
        deps.pop(op.idx, None)
        best = {}
        for d in deps.values():
            if d.eng == "pe" and eng == "pe" and d.chan is None and chan is None:
                continue
            k = d.eng if d.chan is None else ("dma", id(d.chan))
            if k not in best or best[k].idx < d.idx:
                best[k] = d
        op.deps = list(best.values())
        for d in op.deps:
            if d.chan is None:
                d.needs_inc = True
        self.ops.append(op)
        return op

    def mark(self, name):
        if not hasattr(self, "marks"):
            self.marks = []
        self.marks.append((name, len(self.ops)))

    def emit(self, nc, sems, final_chans=(), limit=None):
        if limit is not None:
            self.ops = self.ops[:limit]
            chmax = {}
            for op in self.ops:
                if op.chan is not None:
                    chmax[id(op.chan)] = (op.chan, op.chanval)
            final_chans = []
            for ch, v in chmax.values():
                c = Chan(ch.sem)
                c.count = v
                final_chans.append(c)
        cnt = {e: 0 for e in self.ENGS}
        for op in self.ops:
            if op.chan is None and op.needs_inc:
                cnt[op.eng] += 1
                op.semval = cnt[op.eng]
        per_eng = {e: [o for o in self.ops if o.eng == e] for e in self.ENGS}
        nwaits = {e: 0 for e in self.ENGS}

        def run(e, handle):
            waited = {}
            for op in per_eng[e]:
                need = {}
                for d in op.deps:
                    if d.chan is not None:
                        s, v = d.chan.sem, d.chanval
                    else:
                        s, v = sems[d.eng], d.semval
                    key = id(s)
                    if waited.get(key, 0) >= v:
                        continue
                    if key not in need or need[key][1] < v:
                        need[key] = (s, v)
                for key, (s, v) in need.items():
                    handle.wait_ge(s, v)
                    waited[key] = v
                    nwaits[e] += 1
                ins = op.fn(handle)
                if op.chan is not None:
                    ins.then_inc(op.chan.sem, 16)
                elif op.needs_inc:
                    ins.then_inc(sems[e], 1)
            if e == "sp":
                for ch in final_chans:
                    if ch.count:
                        handle.wait_ge(ch.sem, ch.count)

        with nc.Block() as block:
            @block.tensor
            def _(h):
                run("pe", h)

            @block.scalar
            def _(h):
                run("act", h)

            @block.vector
            def _(h):
                run("dve", h)

            @block.gpsimd
            def _(h):
                run("pool", h)

            @block.sync
            def _(h):
                run("sp", h)
        self.stats = {e: (len(per_eng[e]), cnt[e], nwaits[e]) for e in self.ENGS}


D = 1024
SEQ = 4096
NB = 8
H = 4
DK = 128
DV = 256
TT = 512
NTB = TT // 128
NT = SEQ // TT
NCH = 32
NRING = 4
EPS = 1e-6
GAM = [1.0 - 2.0 ** (-5.0 - h) for h in range(H)]
MAGIC = 1597463007.0


def build_program(SEQ=SEQ, limit=None):
    NT = SEQ // TT
    nc = bass.Bass("TRN2", target_bir_lowering=False)
    x_d = nc.dram_tensor("x", [SEQ, D], F32, kind="ExternalInput").ap()
    wall_d = nc.dram_tensor("wall", [NCH, 128, 4096], F32, kind="ExternalInput").ap()
    cos_d = nc.dram_tensor("cosT", [SEQ, 64], F32, kind="ExternalInput").ap()
    sin_d = nc.dram_tensor("sinT", [SEQ, 64], F32, kind="ExternalInput").ap()
    cf_d = nc.dram_tensor("cf", [128, 1536], F32, kind="ExternalInput").ap()
    cb_d = nc.dram_tensor("cb", [128, 1152], BF16, kind="ExternalInput").ap()
    pw_d = nc.dram_tensor("pw", [128, 512], F32, kind="ExternalInput").ap()
    out_d = nc.dram_tensor("out", [SEQ, D], F32, kind="ExternalOutput").ap()
    wscr_d = nc.dram_tensor("wscr", [NCH, 128, 4096], BF16, kind="Internal").ap()

    S = Sched()
    with ExitStack() as es:
        def sb(name, shape, dt):
            return es.enter_context(nc.sbuf_tensor(name, shape, dt))

        def sem(name):
            return es.enter_context(nc.semaphore(name))

        ring = [sb("ring%d" % i, [128, 4096], BF16) for i in range(NRING)]
        xs = [sb("xs%d" % i, [128, NTB, D], F32) for i in range(2)]
        big = sb("big", [128, 16384], BF16)
        hT = sb("hT", [128, 8, TT], BF16)
        gat = sb("gat", [128, 16, TT], BF16)
        scr = sb("scr", [128, 2048], F32)
        xn = [sb("xn%d" % i, [128, D], BF16) for i in range(NTB)]
        junk = sb("junk", [128, D], BF16)
        qrot = [sb("qrot%d" % i, [128, 512], BF16) for i in range(NTB)]
        krot = [sb("krot%d" % i, [128, 512], BF16) for i in range(NTB)]
        qT = [sb("qT%d" % i, [128, H, 128], BF16) for i in range(NTB)]
        kT = [sb("kT%d" % i, [128, H, 128], BF16) for i in range(NTB)]
        sT = [sb("sT%d" % i, [128, H, 128], BF16) for i in range(2)]
        St = sb("St", [128, H, DV], F32)
        Sbf = [sb("Sbf%d" % i, [128, H, DV], BF16) for i in range(2)]
        gated = [sb("gated%d" % i, [128, D], BF16) for i in range(2)]
        uh = sb("uh", [128, 4, 528], F32)
        ptmp = [sb("ptmp%d" % i, [128, 528], F32) for i in range(2)]
        pbf = sb("pbf", [128, 4, TT], BF16)
        ypT = sb("ypT", [128, 4, TT], BF16)
        cst = [sb("cst%d" % i, [128, 2, NTB, 64], F32) for i in range(2)]
        cf = sb("cfs", [128, 1536], F32)
        cbt = sb("cbs", [128, 1152], BF16)
        pwf = sb("pwf", [128, 512], F32)
        pwb = sb("pwb", [128, 4, 128], BF16)
        st_rms = sb("st_rms", [128, 64], F32)
        st_gn = sb("st_gn", [128, 64], F32)
        st_gn2 = sb("st_gn2", [128, 64], F32)
        st_fin = sb("st_fin", [128, 64], F32)
        st_fx = sb("st_fx", [128, 64], F32)
        st_eps = sb("st_eps", [128, 64], F32)
        ps = [es.enter_context(nc.psum_tensor("ps%d" % i, [128, 512], F32)) for i in range(8)]

        sems = {e: sem("s_" + e) for e in Sched.ENGS}
        ch_ring = [Chan(sem("c_ring%d" % i)) for i in range(NRING)]
        ch_ringc = [Chan(sem("c_ringc%d" % i)) for i in range(NRING)]
        ch_x = [Chan(sem("c_x%d" % i)) for i in range(2)]
        ch_cs = [Chan(sem("c_cs%d" % i)) for i in range(2)]
        ch_sn = [Chan(sem("c_sn%d" % i)) for i in range(2)]
        ch_out = [Chan(sem("c_out%d" % i)) for i in range(2)]
        ch_outl = [Chan(sem("c_outl%d" % i)) for i in range(NTB)]
        ch_stage = [Chan(sem("c_stage%d" % i)) for i in range(4)]
        ch_scr = [Chan(sem("c_scr%d" % i)) for i in range(NRING)]
        ch_const = [Chan(sem("c_const%d" % i)) for i in range(3)]

        aT = big[:, :].rearrange("p (j t) -> p j t", j=32)
        vv = big[:, 0:4096].rearrange("p (b c) -> p b c", b=NTB)
        vz = big[:, 4096:8192].rearrange("p (b c) -> p b c", b=NTB)
        sg = big[:, 8192:12288].rearrange("p (b c) -> p b c", b=NTB)
        gatedT = big[:, 12288:16384].rearrange("p (k t) -> p k t", k=8)
        stage = [big[:, i * 8192:(i + 1) * 8192].bitcast(F32) for i in range(2)]
        mixedT = hT
        h2T = gat[:, 0:8, :]
        cmask = cf[:, 0:128]
        vzs = cf[:, 128:132]
        bgT = cf[:, 132:148]
        g_mix = cf[:, 148:156]
        g_gn = cf[:, 156:164]
        g_pool = cf[:, 164:168]
        g_mlp = cf[:, 168:176]
        invcnt = cf[:, 176:240].rearrange("p (g j) -> p g j", g=4)
        lnf = cf[:, 512:1536]
        ident = cbt[:, 0:128]
        diagq = cbt[:, 128:640].rearrange("p (h c) -> p h c", h=H)
        diagk = cbt[:, 640:1152].rearrange("p (h c) -> p h c", h=H)
        ssq = st_rms[:, 0:4]
        rv = st_rms[:, 4:8]
        ry = st_rms[:, 8:12]
        rt = st_rms[:, 12:16]
        rh = st_rms[:, 16:20]
        bst = st_gn[:, 0:24].rearrange("p (h s) -> p h s", h=H)
        mv = st_gn[:, 24:32].rearrange("p (h s) -> p h s", h=H)
        nmr = st_gn2[:, 0:4]
        epsc = st_eps[:, 0:1]

        bank_ctr = [0]

        def bank():
            b = ps[bank_ctr[0] % 8]
            bank_ctr[0] += 1
            return b

        def A(eng, fn, reads, writes, chan=None):
            return S.add(eng, fn, reads=reads, writes=writes, chan=chan)

        def mm(out, lhsT, rhs, start, stop):
            A("pe", lambda e: e.matmul(out, lhsT=lhsT, rhs=rhs, start=start, stop=stop), [lhsT, rhs], [out])

        def rsqrt_chain(v, y, t, hh):
            A("dve", lambda e: e.tensor_single_scalar(out=t.bitcast(I32), in_=v.bitcast(I32), scalar=1,
                                                      op=ALU.arith_shift_right), [v], [t])
            A("dve", lambda e: e.tensor_scalar(out=y.bitcast(I32), in0=t.bitcast(I32), scalar1=-1.0, scalar2=MAGIC,
                                               op0=ALU.mult, op1=ALU.add), [t], [y])
            A("dve", lambda e: e.tensor_scalar(out=hh, in0=v, scalar1=-0.5, scalar2=None, op0=ALU.mult), [v], [hh])
            for _ in range(2):
                A("dve", lambda e: e.tensor_tensor(out=t, in0=y, in1=y, op=ALU.mult), [y], [t])
                A("dve", lambda e: e.scalar_tensor_tensor(out=t, in0=t, scalar=1.0, in1=hh, op0=ALU.mult, op1=ALU.mult),
                  [t, hh], [t])
                A("dve", lambda e: e.scalar_tensor_tensor(out=y, in0=t, scalar=1.5, in1=y, op0=ALU.add, op1=ALU.mult),
                  [t, y], [y])

        A("pool", lambda e: e.dma_start(out=cf[:], in_=cf_d), [], [cf[:]], ch_const[0])
        A("pool", lambda e: e.dma_start(out=cbt[:], in_=cb_d), [], [cbt[:]], ch_const[1])
        A("pool", lambda e: e.dma_start(out=pwf[:], in_=pw_d), [], [pwf[:]], ch_const[2])
        A("dve", lambda e: e.tensor_copy(out=pwb[:].rearrange("p g d -> p (g d)"), in_=pwf[:]), [pwf[:]], [pwb[:]])
        A("dve", lambda e: e.memset(St[:], 0.0), [], [St[:]])
        A("dve", lambda e: e.memset(Sbf[0][:], 0.0), [], [Sbf[0][:]])
        A("dve", lambda e: e.memset(uh[:], 0.0), [], [uh[:]])
        A("dve", lambda e: e.memset(epsc, EPS), [], [epsc])

        S.mark('consts_done')
        def chunk_gain(j):
            if j <= 10:
                return g_mix, 8
            if j in (11, 13):
                return g_gn, 8
            if j == 12:
                return g_pool, 4
            if 16 <= j <= 23:
                return g_mlp, 8
            return None, 1

        S.mark('prologue_done')
        issued = [0]
        TOTAL = NT * NCH

        stg = [xs[1][:, p, :] for p in range(4)]

        def store_chunk(j):
            sl = j % NRING
            A("sp", lambda e: e.dma_start(out=wscr_d[j], in_=ring[sl][:]), [ring[sl][:]], [("dram_wscr", j)], ch_scr[sl])

        def stage_chunk(j):
            sl = j % NRING
            dst = ring[sl]
            gain, nk = chunk_gain(j)
            if gain is None:
                A("pool", lambda e: e.dma_start(out=dst[:], in_=wall_d[j]), [], [dst[:]], ch_ringc[sl])
                if j > 0:
                    store_chunk(j - 1)
                if j == NCH - 1:
                    store_chunk(j)
                return
            for p in range(4):
                A("sp", lambda e, p=p: e.dma_start(out=stg[p], in_=wall_d[j][:, p * 1024:(p + 1) * 1024]), [], [stg[p]], ch_stage[p])
            for p in range(4):
                eng = "act" if p % 2 == 0 else "dve"
                if gain is None:
                    pieces = [(stg[p], dst[:, p * 1024:(p + 1) * 1024], None)]
                else:
                    w = 4096 // nk
                    per = nk // 4
                    pieces = []
                    for i in range(per):
                        kc = p * per + i
                        pieces.append((stg[p][:, i * w:(i + 1) * w], dst[:, kc * w:(kc + 1) * w], gain[:, kc:kc + 1]))
                for src, d_, gcol in pieces:
                    if gcol is None:
                        if eng == "act":
                            A("act", lambda e, src=src, d_=d_: e.activation(out=d_, in_=src, func=AF.Copy), [src], [d_])
                        else:
                            A("dve", lambda e, src=src, d_=d_: e.tensor_copy(out=d_, in_=src), [src], [d_])
                    elif eng == "act":
                        A("act", lambda e, src=src, d_=d_, gcol=gcol: e.activation(out=d_, in_=src, func=AF.Copy, scale=gcol),
                          [src, gcol], [d_])
                    else:
                        A("dve", lambda e, src=src, d_=d_, gcol=gcol: e.tensor_scalar(out=d_, in0=src, scalar1=gcol, scalar2=None,
                                                                                     op0=ALU.mult), [src, gcol], [d_])
            if j > 0:
                store_chunk(j - 1)
            if j == NCH - 1:
                store_chunk(j)

        def prefetch_upto(n):
            while issued[0] <= min(n, TOTAL - 1):
                g = issued[0]
                j = g % NCH
                sl = g % NRING
                if g < NCH:
                    stage_chunk(j)
                else:
                    A("sp", lambda e, j=j, sl=sl: e.dma_start(out=ring[sl][:], in_=wscr_d[j]), [("dram_wscr", j)], [ring[sl][:]], ch_ring[sl])
                issued[0] += 1

        D0 = 4

        x1_loaded = [False]

        def pf(g):
            prefetch_upto(g + (NRING if g + NRING >= NCH else D0))
            if NT > 1 and issued[0] >= 24 and not x1_loaded[0]:
                x1_loaded[0] = True
                load_x(1)

        def wchunk(g):
            return ring[g % NRING][:, :].rearrange("p (k c) -> p k c", k=8)

        def load_x(t):
            sl = t % 2
            A("pool", lambda e: e.dma_start(out=xs[sl][:], in_=x_d[t * TT:(t + 1) * TT, :].rearrange("(c p) d -> p c d", p=128)),
              [], [xs[sl][:]], ch_x[sl])
            A("pool", lambda e: e.dma_start(out=cst[sl][:, 0], in_=cos_d[t * TT:(t + 1) * TT, :].rearrange("(c p) f -> p c f", p=128)),
              [], [cst[sl][:, 0]], ch_cs[sl])
            A("pool", lambda e: e.dma_start(out=cst[sl][:, 1], in_=sin_d[t * TT:(t + 1) * TT, :].rearrange("(c p) f -> p c f", p=128)),
              [], [cst[sl][:, 1]], ch_sn[sl])

        def norm_a(xsl, groups=None):
            if groups is None:
                groups = [list(range(NTB))]
            for grp in groups:
                lo, hi = grp[0], grp[-1] + 1
                for tb in grp:
                    A("act", lambda e, tb=tb: e.activation(out=junk[:], in_=xsl[:, tb, :], func=AF.Square, accum_out=ssq[:, tb:tb + 1]),
                      [xsl[:, tb, :]], [junk[:], ssq[:, tb:tb + 1]])
                A("act", lambda e, lo=lo, hi=hi: e.activation(out=rv[:, lo:hi], in_=ssq[:, lo:hi], func=AF.Sqrt, scale=1.0 / D, bias=epsc),
                  [ssq[:, lo:hi], epsc], [rv[:, lo:hi]])
                A("dve", lambda e, lo=lo, hi=hi: e.reciprocal(out=ry[:, lo:hi], in_=rv[:, lo:hi]), [rv[:, lo:hi]], [ry[:, lo:hi]])
                for tb in grp:
                    xb = xn[tb]
                    A("dve", lambda e, tb=tb, xb=xb: e.tensor_scalar(out=xb[:], in0=xsl[:, tb, :], scalar1=ry[:, tb:tb + 1], scalar2=None,
                                                                   op0=ALU.mult), [xsl[:, tb, :], ry[:, tb:tb + 1]], [xb[:]])

        def norm_b(dstT, tbs=None):
            for tb in (range(NTB) if tbs is None else tbs):
                xb = xn[tb]
                pb = bank()
                pbv = pb[:].bitcast(BF16).rearrange("p (k c) -> p k c", k=8)
                for kc in range(8):
                    A("pe", lambda e, kc=kc, xb=xb, pbv=pbv: e.transpose(out=pbv[:, kc, :], in_=xb[:, kc * 128:(kc + 1) * 128], identity=ident),
                      [xb[:, kc * 128:(kc + 1) * 128], ident], [pbv[:, kc, :]])
                dd = dstT[:, :, tb * 128:(tb + 1) * 128]
                if tb % 2 == 0:
                    A("dve", lambda e, dd=dd, pbv=pbv: e.tensor_copy(out=dd, in_=pbv), [pbv], [dd])
                else:
                    A("act", lambda e, dd=dd, pbv=pbv: e.activation(out=dd, in_=pbv, func=AF.Copy), [pbv], [dd])

        def tokmajor_chunk(g, srcT, evac):
            w = wchunk(g)
            for tb in range(NTB):
                pb = bank()
                for kc in range(8):
                    mm(pb[:], srcT[:, kc, tb * 128:(tb + 1) * 128], w[:, kc, :], kc == 0, kc == 7)
                evac(tb, pb)
            pf(g)

        def featmajor_chunk(g, srcT, evac, nk=8):
            w = wchunk(g)
            for cb in range(4):
                pb = bank()
                for kc in range(nk):
                    mm(pb[:], w[:, kc, cb * 128:(cb + 1) * 128], srcT[:, kc, :], kc == 0, kc == nk - 1)
                evac(cb, pb)
            pf(g)

        def mixer(t, hookA=None, hookB=None):
            g0 = t * NCH
            xsl = xs[t % 2]
            cs = cst[t % 2]

            def rotary(pb, tb, dst):
                pv = pb[:].rearrange("p (h t f) -> p h t f", h=H, t=2)
                t1 = scr[:, (tb % 2) * 1024:(tb % 2) * 1024 + 512]
                t2 = scr[:, (tb % 2) * 1024 + 512:(tb % 2) * 1024 + 1024]
                t1v = t1.rearrange("p (h t f) -> p h t f", h=H, t=2)
                t2v = t2.rearrange("p (h t f) -> p h t f", h=H, t=2)
                dv = dst[:].rearrange("p (h t f) -> p h t f", h=H, t=2)
                cosb = cs[:, 0, tb, :].unsqueeze(1).unsqueeze(1).broadcast_to([128, H, 2, 64])
                sinb = cs[:, 1, tb, :].unsqueeze(1).broadcast_to([128, H, 64])
                A("dve", lambda e: e.tensor_tensor(out=t1v, in0=pv, in1=cosb, op=ALU.mult), [pb[:], cs[:, 0, tb, :]], [t1])
                A("dve", lambda e: e.tensor_tensor(out=t2v[:, :, 0, :], in0=pv[:, :, 1, :], in1=sinb, op=ALU.mult),
                  [pb[:], cs[:, 1, tb, :]], [t2])
                A("dve", lambda e: e.tensor_tensor(out=t2v[:, :, 1, :], in0=pv[:, :, 0, :], in1=sinb, op=ALU.mult),
                  [pb[:], cs[:, 1, tb, :]], [t2])
                A("pool", lambda e: e.tensor_tensor(out=dv[:, :, 0, :], in0=t1v[:, :, 0, :], in1=t2v[:, :, 0, :], op=ALU.subtract),
                  [t1, t2], [dst[:]])
                A("pool", lambda e: e.tensor_tensor(out=dv[:, :, 1, :], in0=t1v[:, :, 1, :], in1=t2v[:, :, 1, :], op=ALU.add),
                  [t1, t2], [dst[:]])

            def diag_T(src, dg, dst):
                pb = bank()
                pv = pb[:].rearrange("p (h c) -> p h c", h=H)
                for h in range(H):
                    mm(pv[:, h, :], src[:, h * 128:(h + 1) * 128], dg[:, h, :], True, True)
                A("dve", lambda e: e.tensor_copy(out=dst[:], in_=pv), [pb[:]], [dst[:]])

            def ev_q(tb, pb):
                rotary(pb, tb, qrot[tb])

            def ev_k(tb, pb):
                rotary(pb, tb, krot[tb])

            def ev_v(i):
                def f(tb, pb):
                    A("dve", lambda e: e.tensor_copy(out=vv[:, tb, i * 512:(i + 1) * 512], in_=pb[:]), [pb[:]],
                      [vv[:, tb, i * 512:(i + 1) * 512]])
                    for hl in range(2):
                        h = 2 * i + hl
                        o_ = vz[:, tb, h * 256:(h + 1) * 256]
                        A("act", lambda e, o_=o_, hl=hl, h=h: e.activation(out=o_, in_=pb[:, hl * 256:(hl + 1) * 256], func=AF.Copy,
                                                                         scale=vzs[:, h:h + 1]), [pb[:], vzs], [o_])
                return f

            def ev_g(i):
                def f(tb, pb):
                    o_ = sg[:, tb, i * 512:(i + 1) * 512]
                    A("act", lambda e: e.activation(out=o_, in_=pb[:], func=AF.Silu), [pb[:]], [o_])
                return f

            def ev_pu(cb, pb):
                o_ = uh[:, cb, 16:528]
                A("act", lambda e: e.activation(out=o_, in_=pb[:], func=AF.Copy), [pb[:]], [o_])

            def ev_gate(i):
                def f(cb, pb):
                    blk = i * 4 + cb
                    o_ = gat[:, blk, :]
                    A("act", lambda e: e.activation(out=o_, in_=pb[:], func=AF.Sigmoid, bias=bgT[:, blk:blk + 1]),
                      [pb[:], bgT], [o_])
                return f

            def R2(tb):
                pb = bank()
                pv = pb[:].rearrange("p (h c) -> p h c", h=H)
                for h in range(H):
                    mm(pv[:, h, :], kT[tb][:, h, :], qT[tb][:, h, :], True, True)
                sTb = sT[tb % 2]
                cmb = cmask.unsqueeze(1).broadcast_to([128, H, 128])
                A("dve", lambda e: e.tensor_tensor(out=sTb[:], in0=pv, in1=cmb, op=ALU.mult), [pb[:], cmask], [sTb[:]])
                for hp in range(2):
                    pk = bank()
                    for hl in range(2):
                        h = hp * 2 + hl
                        mm(pk[:, hl * 256:(hl + 1) * 256], krot[tb][:, h * 128:(h + 1) * 128], vz[:, tb, h * 256:(h + 1) * 256], True, True)
                    for hl in range(2):
                        h = hp * 2 + hl
                        A("dve", lambda e, h=h, hl=hl, pk=pk: e.scalar_tensor_tensor(out=St[:, h, :], in0=St[:, h, :], scalar=float(GAM[h] ** 128),
                                                                                    in1=pk[:, hl * 256:(hl + 1) * 256], op0=ALU.mult, op1=ALU.add),
                          [St[:, h, :], pk[:, hl * 256:(hl + 1) * 256]], [St[:, h, :]])
                nb = Sbf[(tb + 1) % 2]
                A("act", lambda e: e.activation(out=nb[:], in_=St[:], func=AF.Copy), [St[:]], [nb[:]])

            def R3a(tb):
                sTb = sT[tb % 2]
                sb_ = Sbf[tb % 2]
                pbs = []
                for hp in range(2):
                    po = bank()
                    pbs.append(po)
                    for hl in range(2):
                        h = hp * 2 + hl
                        o_ = po[:, hl * 256:(hl + 1) * 256]
                        mm(o_, sTb[:, h, :], vv[:, tb, h * 256:(h + 1) * 256], True, False)
                        mm(o_, qT[tb][:, h, :], sb_[:, h, :], False, True)
                for h in range(H):
                    src = pbs[h // 2][:, (h % 2) * 256:(h % 2 + 1) * 256]
                    A("dve", lambda e, h=h, src=src: e.bn_stats(out=bst[:, h, :], in_=src), [src], [bst[:, h, :]])
                    A("dve", lambda e, h=h: e.bn_aggr(out=mv[:, h, :], in_=bst[:, h, :]), [bst[:, h, :]], [mv[:, h, :]])
                gv = st_gn2[:, 4:8]
                gy = st_gn2[:, 8:12]
                gt = st_gn2[:, 12:16]
                gh = st_gn2[:, 16:20]
                A("dve", lambda e: e.tensor_scalar(out=gv, in0=mv[:, :, 1], scalar1=EPS, scalar2=None, op0=ALU.add), [mv], [gv])
                rsqrt_chain(gv, gy, gt, gh)
                gb = gated[tb % 2]
                for h in range(H):
                    src = pbs[h // 2][:, (h % 2) * 256:(h % 2 + 1) * 256]
                    sgr = scr[:, 1024 + (h % 2) * 256:1024 + (h % 2 + 1) * 256].bitcast(BF16)[:, 0:256]
                    sgh = sg[:, tb, h * 256:(h + 1) * 256]
                    A("dve", lambda e, h=h, sgr=sgr, sgh=sgh: e.tensor_scalar(out=sgr, in0=sgh, scalar1=gy[:, h:h + 1], scalar2=None, op0=ALU.mult),
                      [sgh, gy], [sgr])
                    o_ = gb[:, h * 256:(h + 1) * 256]
                    A("dve", lambda e, h=h, src=src, sgr=sgr, o_=o_: e.scalar_tensor_tensor(out=o_, in0=src, scalar=mv[:, h, 0:1], in1=sgr,
                                                                                           op0=ALU.subtract, op1=ALU.mult),
                      [src, mv, sgr], [o_])

            def R3b(tb):
                gb = gated[tb % 2]
                pb = bank()
                pbv = pb[:].bitcast(BF16).rearrange("p (k c) -> p k c", k=8)
                for kc in range(8):
                    A("pe", lambda e, kc=kc, pbv=pbv: e.transpose(out=pbv[:, kc, :], in_=gb[:, kc * 128:(kc + 1) * 128], identity=ident),
                      [gb[:, kc * 128:(kc + 1) * 128], ident], [pbv[:, kc, :]])
                dd = gatedT[:, :, tb * 128:(tb + 1) * 128]
                A("dve", lambda e: e.tensor_copy(out=dd, in_=pbv), [pbv], [dd])

            def pooling():
                for g in range(4):
                    cur = uh[:, g, :]
                    src = cur
                    for lv in range(g + 1):
                        sh = 1 << lv
                        lo = 2 * sh - 1
                        dst = ptmp[lv % 2]
                        A("pool", lambda e, src=src, dst=dst, sh=sh, lo=lo: e.tensor_tensor(out=dst[:, lo:528], in0=src[:, lo:528],
                                                                                            in1=src[:, lo - sh:528 - sh], op=ALU.add),
                          [src[:, lo - sh:528]], [dst[:, lo:528]])
                        src = dst
                    wd = float(1 << (g + 1))
                    other = ptmp[(g + 1) % 2]
                    iw = invcnt[:, g, 15:16].broadcast_to([128, 512])
                    A("pool", lambda e, g=g, src=src, other=other, iw=iw: e.tensor_tensor(out=other[:, 16:528], in0=src[:, 16:528], in1=iw, op=ALU.mult),
                      [src[:, 16:528], invcnt[:, g, 15:16]], [other[:, 16:528]])
                    A("pool", lambda e, g=g, other=other: e.tensor_tensor(out=pbf[:, g, :], in0=other[:, 16:528], in1=uh[:, g, 16:528], op=ALU.subtract),
                      [other[:, 16:528], uh[:, g, 16:528]], [pbf[:, g, :]])
                    if t == 0:
                        fx = st_fx[:, 0:16]
                        A("pool", lambda e, g=g, src=src: e.tensor_tensor(out=fx, in0=src[:, 16:32], in1=invcnt[:, g, :], op=ALU.mult),
                          [src[:, 16:32], invcnt[:, g, :]], [fx])
                        A("pool", lambda e, g=g: e.tensor_tensor(out=pbf[:, g, 0:16], in0=fx, in1=uh[:, g, 16:32], op=ALU.subtract),
                          [fx, uh[:, g, 16:32]], [pbf[:, g, 0:16]])
                A("pool", lambda e: e.tensor_copy(out=uh[:, :, 0:16], in_=uh[:, :, 512:528]), [uh[:, :, 512:528]], [uh[:, :, 0:16]])

            tokmajor_chunk(g0 + 0, hT, ev_q)
            tokmajor_chunk(g0 + 1, hT, ev_k)
            for tb in range(NTB):
                diag_T(qrot[tb], diagq, qT[tb])
            tokmajor_chunk(g0 + 2, hT, ev_v(0))
            for tb in range(NTB):
                diag_T(krot[tb], diagk, kT[tb])
            tokmajor_chunk(g0 + 3, hT, ev_v(1))
            if hookA is not None:
                hookA()
            tokmajor_chunk(g0 + 4, hT, ev_g(0))
            R2(0)
            tokmajor_chunk(g0 + 5, hT, ev_g(1))
            R3a(0)
            R2(1)
            featmajor_chunk(g0 + 6, hT, ev_pu)
            pooling()
            R3a(1)
            R2(2)
            featmajor_chunk(g0 + 7, hT, ev_gate(0))
            R3b(0)
            R3a(2)
            R2(3)
            featmajor_chunk(g0 + 8, hT, ev_gate(1))
            R3b(1)
            R3a(3)
            featmajor_chunk(g0 + 9, hT, ev_gate(2))
            R3b(2)
            featmajor_chunk(g0 + 10, hT, ev_gate(3))
            R3b(3)

            S.mark('win_done%d' % t)
            for g in range(4):
                pb = bank()
                mm(pb[:], pwb[:, g, :], pbf[:, g, :], True, True)
                A("act", lambda e, g=g, pb=pb: e.activation(out=ypT[:, g, :], in_=pb[:], func=AF.Copy), [pb[:]], [ypT[:, g, :]])

            S.mark('poolbr_done%d' % t)
            wret0 = wchunk(g0 + 11)
            wpool = ring[(g0 + 12) % NRING][:, :].rearrange("p (k c) -> p k c", k=4)
            wret1 = wchunk(g0 + 13)
            pr = {}
            pp = {}

            def ret_blk(w, cb):
                pb = bank()
                pr[cb] = pb
                for kc in range(8):
                    mm(pb[:], w[:, kc, (cb % 4) * 128:(cb % 4 + 1) * 128], gatedT[:, kc, :], kc == 0, kc == 7)

            def pool_blk(cb):
                pb = bank()
                pp[cb] = pb
                for kc in range(4):
                    mm(pb[:], wpool[:, kc, cb * 128:(cb + 1) * 128], ypT[:, kc, :], kc == 0, kc == 3)

            def mix(cb):
                m1 = scr[:, (cb % 2) * 1024:(cb % 2) * 1024 + 512]
                m2 = scr[:, (cb % 2) * 1024 + 512:(cb % 2) * 1024 + 1024]
                A("dve", lambda e: e.tensor_tensor(out=m1, in0=pr[cb][:], in1=gat[:, cb, :], op=ALU.mult), [pr[cb][:], gat[:, cb, :]], [m1])
                A("dve", lambda e: e.tensor_tensor(out=m2, in0=pp[cb][:], in1=gat[:, 8 + cb, :], op=ALU.mult),
                  [pp[cb][:], gat[:, 8 + cb, :]], [m2])
                A("pool", lambda e: e.tensor_tensor(out=mixedT[:, cb, :], in0=m1, in1=m2, op=ALU.add), [m1, m2], [mixedT[:, cb, :]])

            for cb in range(4):
                ret_blk(wret0, cb)
                pool_blk(cb)
                mix(cb)
            pf(g0 + 11)
            for cb in range(4, 8):
                pool_blk(cb)
                if cb == 7:
                    pf(g0 + 12)
                ret_blk(wret1, cb)
                mix(cb)
            pf(g0 + 13)

            S.mark('merge_done%d' % t)
            if hookB is not None:
                hookB()
            wo = [wchunk(g0 + 14), wchunk(g0 + 15)]
            for tb in range(NTB):
                for i in range(2):
                    pb = bank()
                    for kc in range(8):
                        mm(pb[:], mixedT[:, kc, tb * 128:(tb + 1) * 128], wo[i][:, kc, :], kc == 0, kc == 7)
                    o_ = xsl[:, tb, i * 512:(i + 1) * 512]
                    A("dve", lambda e, o_=o_, pb=pb: e.tensor_tensor(out=o_, in0=pb[:], in1=o_, op=ALU.add), [pb[:], o_], [o_])
                if tb >= 2:
                    norm_b(h2T, [tb - 2])
                norm_a(xsl, [[tb]])
            pf(g0 + 14)
            pf(g0 + 15)

        def ffn_up(t):
            g0 = t * NCH
            xsl = xs[t % 2]

            def relu2(n, pb):
                rtmp = scr[:, (n % 2) * 512:(n % 2 + 1) * 512]
                A("act", lambda e: e.activation(out=rtmp, in_=pb[:], func=AF.Relu), [pb[:]], [rtmp])
                eng = "pool" if n % 2 == 0 else "dve"
                A(eng, lambda e: e.tensor_tensor(out=aT[:, n, :], in0=rtmp, in1=rtmp, op=ALU.mult), [rtmp], [aT[:, n, :]])

            w0, w1 = wchunk(g0 + 16), wchunk(g0 + 17)
            early = [(0, w0, cb) for cb in range(4)] + [(1, w1, cb) for cb in range(2)]
            banks = []
            for (j, w, cb) in early:
                pb = bank()
                banks.append(pb)
                for kc in range(8):
                    mm(pb[:, 0:256], w[:, kc, cb * 128:(cb + 1) * 128], h2T[:, kc, 0:256], kc == 0, kc == 7)
            norm_b(h2T, [2, 3])
            for (j, w, cb), pb in zip(early, banks):
                for kc in range(8):
                    mm(pb[:, 256:512], w[:, kc, cb * 128:(cb + 1) * 128], h2T[:, kc, 256:512], kc == 0, kc == 7)
                relu2(j * 4 + cb, pb)
                if (j, cb) == (0, 3):
                    pf(g0 + 16)
            for cb in range(2, 4):
                pb = bank()
                for kc in range(8):
                    mm(pb[:], w1[:, kc, cb * 128:(cb + 1) * 128], h2T[:, kc, :], kc == 0, kc == 7)
                relu2(4 + cb, pb)
            pf(g0 + 17)
            for j in range(2, 8):
                def ev_u(cb, pb, j=j):
                    relu2(j * 4 + cb, pb)
                featmajor_chunk(g0 + 16 + j, h2T, ev_u)
                if j == 3 and 1 <= t < NT - 1:
                    norm_a(xs[(t + 1) % 2])

        x1_normed = [False]

        def ffn_down(t):
            g0 = t * NCH
            xsl = xs[t % 2]
            for half in range(2):
                pbs = [bank() for _ in range(NTB)]
                for kg in range(4):
                    g = g0 + 24 + half * 4 + kg
                    w = wchunk(g)
                    for tb in range(NTB):
                        for kcl in range(8):
                            kc = kg * 8 + kcl
                            mm(pbs[tb][:], aT[:, kc, tb * 128:(tb + 1) * 128], w[:, kcl, :], kc == 0, kc == 31)
                    pf(g)
                    if t == 0 and NT > 1 and x1_loaded[0] and not x1_normed[0]:
                        x1_normed[0] = True
                        norm_a(xs[1])
                for tb in range(NTB):
                    o_ = xsl[:, tb, half * 512:(half + 1) * 512]
                    A("dve", lambda e, o_=o_, pb=pbs[tb]: e.tensor_tensor(out=o_, in0=pb[:], in1=o_, op=ALU.add), [pbs[tb][:], o_], [o_])

        def final(t):
            sl = t % 2
            xsl = xs[sl]
            fs = st_fin[:, 0:4]
            fv = st_fin[:, 4:8]
            fy = st_fin[:, 8:12]
            ft = st_fin[:, 12:16]
            fh = st_fin[:, 16:20]
            if t == NT - 1:
                for tb in range(NTB):
                    c = slice(tb, tb + 1)
                    A("act", lambda e, tb=tb, c=c: e.activation(out=junk[:], in_=xsl[:, tb, :], func=AF.Square, accum_out=fs[:, c]),
                      [xsl[:, tb, :]], [junk[:], fs[:, c]])
                    A("act", lambda e, c=c: e.activation(out=fv[:, c], in_=fs[:, c], func=AF.Sqrt, scale=1.0 / D, bias=epsc),
                      [fs[:, c], epsc], [fv[:, c]])
                    A("dve", lambda e, c=c: e.reciprocal(out=fy[:, c], in_=fv[:, c]), [fv[:, c]], [fy[:, c]])
                    A("dve", lambda e, tb=tb, c=c: e.scalar_tensor_tensor(out=xsl[:, tb, :], in0=xsl[:, tb, :], scalar=fy[:, c], in1=lnf,
                                                                         op0=ALU.mult, op1=ALU.mult), [xsl[:, tb, :], fy[:, c], lnf], [xsl[:, tb, :]])
                    r0 = t * TT + tb * 128
                    A("pool", lambda e, tb=tb, r0=r0: e.dma_start(out=out_d[r0:r0 + 128, :], in_=xsl[:, tb, :]),
                      [xsl[:, tb, :]], [("out", t, tb)], ch_outl[tb])
                return
            for tb in range(NTB):
                A("act", lambda e, tb=tb: e.activation(out=junk[:], in_=xsl[:, tb, :], func=AF.Square, accum_out=fs[:, tb:tb + 1]),
                  [xsl[:, tb, :]], [junk[:], fs[:, tb:tb + 1]])
            A("act", lambda e: e.activation(out=fv, in_=fs, func=AF.Sqrt, scale=1.0 / D, bias=epsc), [fs, epsc], [fv])
            A("dve", lambda e: e.reciprocal(out=fy, in_=fv), [fv], [fy])
            for tb in range(NTB):
                A("dve", lambda e, tb=tb: e.scalar_tensor_tensor(out=xsl[:, tb, :], in0=xsl[:, tb, :], scalar=fy[:, tb:tb + 1], in1=lnf,
                                                                 op0=ALU.mult, op1=ALU.mult), [xsl[:, tb, :], fy[:, tb:tb + 1], lnf], [xsl[:, tb, :]])
            A("pool", lambda e: e.dma_start(out=out_d[t * TT:(t + 1) * TT, :].rearrange("(c p) d -> p c d", p=128), in_=xsl[:]),
              [xsl[:]], [("out", t)], ch_out[sl])

        load_x(0)
        prefetch_upto(0)
        norm_a(xs[0])
        norm_b(hT)
        prefetch_upto(D0 - 1)
        for t in range(NT):
            hookA = (lambda t=t: final(t - 1)) if t >= 2 else None
            hookB = (lambda t=t: load_x(t + 1)) if 1 <= t < NT - 1 else None
            mixer(t, hookA, hookB)
            S.mark('mixer_done%d' % t)
            ffn_up(t)
            S.mark('ffn_up_done%d' % t)
            if 1 <= t < NT - 1:
                norm_b(hT)
            ffn_down(t)
            S.mark('ffn_down_done%d' % t)
            if t == 0 and NT > 1:
                assert x1_normed[0]
                norm_b(hT)
                final(0)
            elif t == NT - 1:
                final(t)
        S.emit(nc, sems, final_chans=ch_out + ch_outl, limit=limit)
    return nc, S


def _host_consts():
    pos = np.arange(SEQ, dtype=np.float32)
    inv_freq = (10000.0 ** (-np.arange(64, dtype=np.float32) * 2.0 / 128.0)).astype(np.float32)
    ang = (pos[:, None] * inv_freq[None, :]).astype(np.float32)
    cosT = np.cos(ang.astype(np.float64)).astype(np.float32)
    sinT = np.sin(ang.astype(np.float64)).astype(np.float32)
    idx = np.arange(128, dtype=np.float64)
    gam = np.array(GAM, dtype=np.float64)
    cb = np.zeros((128, 1152), dtype=np.float32)
    cb[:, 0:128] = np.eye(128)
    for h in range(H):
        cb[:, 128 + h * 128:128 + (h + 1) * 128] = np.diag(gam[h] ** idx * DK ** -0.5)
        cb[:, 640 + h * 128:640 + (h + 1) * 128] = np.diag(gam[h] ** (-idx))
    cb = cb.astype(ml_dtypes.bfloat16)
    return cosT, sinT, cb


def _prep(x, ln_mix, w_in, b_gate, gn_gain, w_ret_up, pool_w, pool_scale, w_pool_up, w_o, ln_mlp,
           w_up, w_down, ln_final):
    x = np.asarray(x, dtype=np.float32)
    f32 = lambda a: np.asarray(a, dtype=np.float32)
    w_in, w_ret_up, w_pool_up, w_o, w_up, w_down = map(f32, (w_in, w_ret_up, w_pool_up, w_o, w_up, w_down))

    def rows8(w):
        return np.ascontiguousarray(w.reshape(8, 128, 512).transpose(1, 0, 2)).reshape(128, 4096)

    wall = np.empty((NCH, 128, 4096), dtype=np.float32)
    for j in range(11):
        wall[j] = rows8(w_in[0][:, j * 512:(j + 1) * 512])
    wall[11] = rows8(w_ret_up[0][:, 0:512])
    wall[12] = np.ascontiguousarray(w_pool_up[0].reshape(4, 128, 1024).transpose(1, 0, 2)).reshape(128, 4096)
    wall[13] = rows8(w_ret_up[0][:, 512:1024])
    wall[14] = rows8(w_o[0][:, 0:512])
    wall[15] = rows8(w_o[0][:, 512:1024])
    for j in range(8):
        wall[16 + j] = rows8(w_up[0][:, j * 512:(j + 1) * 512])
    for half in range(2):
        for kg in range(4):
            wall[24 + half * 4 + kg] = rows8(w_down[0][kg * 1024:(kg + 1) * 1024, half * 512:(half + 1) * 512])

    cosT, sinT, cb = _host_consts()
    idx = np.arange(128, dtype=np.float64)
    gam = np.array(GAM, dtype=np.float64)
    cf = np.zeros((128, 1536), dtype=np.float32)
    cf[:, 0:128] = (idx[None, :] >= idx[:, None]).astype(np.float32)
    cf[:, 128:132] = (gam[None, :] ** (128.0 - idx[:, None])).astype(np.float32)
    cf[:, 132:148] = f32(b_gate)[0].reshape(16, 128).T
    cf[:, 148:156] = f32(ln_mix)[0].reshape(8, 128).T
    cf[:, 156:164] = f32(gn_gain)[0].reshape(8, 128).T
    cf[:, 164:168] = f32(pool_scale)[0].reshape(4, 128).T
    cf[:, 168:176] = f32(ln_mlp)[0].reshape(8, 128).T
    for g in range(4):
        cf[:, 176 + g * 16:176 + (g + 1) * 16] = (1.0 / np.minimum(np.arange(16) + 1.0, 2.0 ** (g + 1)))[None, :]
    cf[:, 512:1536] = f32(ln_final)[None, :]
    pw = np.ascontiguousarray(f32(pool_w)[0].transpose(1, 0, 2)).reshape(128, 512)

    return dict(wall=wall, cosT=cosT, sinT=sinT, cf=cf, cb=cb, pw=pw)


def kernel(x, ln_mix, w_in, b_gate, gn_gain, w_ret_up, pool_w, pool_scale, w_pool_up, w_o, ln_mlp,
           w_up, w_down, ln_final):
    x = np.asarray(x, dtype=np.float32)
    shared = _prep(x, ln_mix, w_in, b_gate, gn_gain, w_ret_up, pool_w, pool_scale, w_pool_up, w_o, ln_mlp,
                   w_up, w_down, ln_final)
    nc, _ = build_program()
    in_maps = [dict(x=np.ascontiguousarray(x[b]), **shared) for b in range(NB)]
    res = run_bass_kernel_spmd(nc, in_maps, core_ids=list(range(NB)))
    return np.stack([np.asarray(r["out"], dtype=np.float32) for r in res.results], axis=0)
```

```python
import numpy as np
import ml_dtypes
from contextlib import ExitStack
import concourse.bass as bass
import concourse.mybir as mybir
from concourse.bass_utils import run_bass_kernel_spmd

F32 = mybir.dt.float32
BF16 = mybir.dt.bfloat16
I32 = mybir.dt.int32
AF = mybir.ActivationFunctionType
ALU = mybir.AluOpType

_ESZ = {F32: 4, BF16: 2, I32: 4}
_G = 256


def ap_keys(ap):
    if isinstance(ap, (tuple, str)):
        return [ap]
    t = ap.tensor
    row = 1
    for s in list(t.shape)[1:]:
        row *= int(s)
    esz = _ESZ[ap.dtype]
    off = int(ap.offset) % row
    hi = off
    for st, cnt in list(ap.ap)[1:]:
        hi += (int(cnt) - 1) * int(st)
    hi += 1
    name = t.name
    if name.startswith("ps"):
        return [(name, 0)]
    return [(name, b) for b in range(off * esz // _G, (hi * esz - 1) // _G + 1)]


class Chan:
    def __init__(self, sem):
        self.sem = sem
        self.count = 0


class Op:
    __slots__ = ("eng", "fn", "deps", "chan", "chanval", "semval", "needs_inc", "idx")


class Sched:
    ENGS = ("pe", "act", "dve", "pool", "sp")

    def __init__(self):
        self.ops = []
        self.lastw = {}
        self.readers = {}

    def add(self, eng, fn, reads=(), writes=(), chan=None):
        op = Op()
        op.eng = eng
        op.fn = fn
        op.chan = chan
        op.chanval = None
        op.semval = None
        op.needs_inc = False
        op.idx = len(self.ops)
        if chan is not None:
            chan.count += 16
            op.chanval = chan.count
        deps = {}
        rk = []
        for a in reads:
            rk.extend(ap_keys(a))
        wk = []
        for a in writes:
            wk.extend(ap_keys(a))
        psr = [k for k in rk if isinstance(k[0], str) and k[0].startswith("ps") and k[1] == 0 and len(k) == 2]
        if psr:
            rk = [k for k in rk if k not in psr]
            wk = wk + [k for k in psr if k not in wk]
        for k in rk:
            w = self.lastw.get(k)
            if w is not None:
                deps[w.idx] = w
        for k in wk:
            w = self.lastw.get(k)
            if w is not None:
                deps[w.idx] = w
            r = self.readers.get(k)
            if r:
                for o in r.values():
                    deps[o.idx] = o
        rkey = eng if chan is None else ("dma", id(chan))
        for k in rk:
            self.readers.setdefault(k, {})[rkey] = op
        for k in wk:
            self.lastw[k] = op
            self.readers[k] = {}
        deps.pop(op.idx, None)
        best = {}
        for d in deps.values():
            if d.eng == "pe" and eng == "pe" and d.chan is None and chan is None:
                continue
            k = d.eng if d.chan is None else ("dma", id(d.chan))
            if k not in best or best[k].idx < d.idx:
                best[k] = d
        op.deps = list(best.values())
        for d in op.deps:
            if d.chan is None:
                d.needs_inc = True
        self.ops.append(op)
        return op

    def mark(self, name):
        if not hasattr(self, "marks"):
            self.marks = []
        self.marks.append((name, len(self.ops)))

    def emit(self, nc, sems, final_chans=(), limit=None):
        if limit is not None:
            self.ops = self.ops[:limit]
            chmax = {}
            for op in self.ops:
                if op.chan is not None:
                    chmax[id(op.chan)] = (op.chan, op.chanval)
            final_chans = []
            for ch, v in chmax.values():
                c = Chan(ch.sem)
                c.count = v
                final_chans.append(c)
        cnt = {e: 0 for e in self.ENGS}
        for op in self.ops:
            if op.chan is None and op.needs_inc:
                cnt[op.eng] += 1
                op.semval = cnt[op.eng]
        per_eng = {e: [o for o in self.ops if o.eng == e] for e in self.ENGS}
        nwaits = {e: 0 for e in self.ENGS}

        def run(e, handle):
            waited = {}
            for op in per_eng[e]:
                need = {}
                for d in op.deps:
                    if d.chan is not None:
                        s, v = d.chan.sem, d.chanval
                    else:
                        s, v = sems[d.eng], d.semval
                    key = id(s)
                    if waited.get(key, 0) >= v:
                        continue
                    if key not in need or need[key][1] < v:
                        need[key] = (s, v)
                for key, (s, v) in need.items():
                    handle.wait_ge(s, v)
                    waited[key] = v
                    nwaits[e] += 1
                ins = op.fn(handle)
                if op.chan is not None:
                    ins.then_inc(op.chan.sem, 16)
                elif op.needs_inc:
                    ins.then_inc(sems[e], 1)
            if e == "sp":
                for ch in final_chans:
                    if ch.count:
                        handle.wait_ge(ch.sem, ch.count)

        with nc.Block() as block:
            @block.tensor
            def _(h):
                run("pe", h)

            @block.scalar
            def _(h):
                run("act", h)

            @block.vector
            def _(h):
                run("dve", h)

            @block.gpsimd
            def _(h):
                run("pool", h)

            @block.sync
            def _(h):
                run("sp", h)
        self.stats = {e: (len(per_eng[e]), cnt[e], nwaits[e]) for e in self.ENGS}


D = 1024
SEQ = 4096
NB = 8
H = 4
DK = 128
DV = 256
TT = 512
NTB = TT // 128
NT = SEQ // TT
NCH = 32
NRING = 4
EPS = 1e-6
GAM = [1.0 - 2.0 ** (-5.0 - h) for h in range(H)]
MAGIC = 1597463007.0


def build_program(SEQ=SEQ, limit=None):
    NT = SEQ // TT
    nc = bass.Bass("TRN2", target_bir_lowering=False)
    x_d = nc.dram_tensor("x", [SEQ, D], F32, kind="ExternalInput").ap()
    wall_d = nc.dram_tensor("wall", [NCH, 128, 4096], F32, kind="ExternalInput").ap()
    cos_d = nc.dram_tensor("cosT", [SEQ, 64], F32, kind="ExternalInput").ap()
    sin_d = nc.dram_tensor("sinT", [SEQ, 64], F32, kind="ExternalInput").ap()
    cf_d = nc.dram_tensor("cf", [128, 1536], F32, kind="ExternalInput").ap()
    cb_d = nc.dram_tensor("cb", [128, 1152], BF16, kind="ExternalInput").ap()
    pw_d = nc.dram_tensor("pw", [128, 512], F32, kind="ExternalInput").ap()
    out_d = nc.dram_tensor("out", [SEQ, D], F32, kind="ExternalOutput").ap()
    wscr_d = nc.dram_tensor("wscr", [NCH, 128, 4096], BF16, kind="Internal").ap()

    S = Sched()
    with ExitStack() as es:
        def sb(name, shape, dt):
            return es.enter_context(nc.sbuf_tensor(name, shape, dt))

        def sem(name):
            return es.enter_context(nc.semaphore(name))

        ring = [sb("ring%d" % i, [128, 4096], BF16) for i in range(NRING)]
        xs = [sb("xs%d" % i, [128, NTB, D], F32) for i in range(2)]
        big = sb("big", [128, 16384], BF16)
        hT = sb("hT", [128, 8, TT], BF16)
        gat = sb("gat", [128, 16, TT], BF16)
        scr = sb("scr", [128, 2048], F32)
        xn = [sb("xn%d" % i, [128, D], BF16) for i in range(NTB)]
        junk = sb("junk", [128, D], BF16)
        qrot = [sb("qrot%d" % i, [128, 512], BF16) for i in range(NTB)]
        krot = [sb("krot%d" % i, [128, 512], BF16) for i in range(NTB)]
        qT = [sb("qT%d" % i, [128, H, 128], BF16) for i in range(NTB)]
        kT = [sb("kT%d" % i, [128, H, 128], BF16) for i in range(NTB)]
        sT = [sb("sT%d" % i, [128, H, 128], BF16) for i in range(2)]
        St = sb("St", [128, H, DV], F32)
        Sbf = [sb("Sbf%d" % i, [128, H, DV], BF16) for i in range(2)]
        gated = [sb("gated%d" % i, [128, D], BF16) for i in range(2)]
        uh = sb("uh", [128, 4, 528], F32)
        ptmp = [sb("ptmp%d" % i, [128, 528], F32) for i in range(2)]
        pbf = sb("pbf", [128, 4, TT], BF16)
        ypT = sb("ypT", [128, 4, TT], BF16)
        cst = [sb("cst%d" % i, [128, 2, NTB, 64], F32) for i in range(2)]
        cf = sb("cfs", [128, 1536], F32)
        cbt = sb("cbs", [128, 1152], BF16)
        pwf = sb("pwf", [128, 512], F32)
        pwb = sb("pwb", [128, 4, 128], BF16)
        st_rms = sb("st_rms", [128, 64], F32)
        st_gn = sb("st_gn", [128, 64], F32)
        st_gn2 = sb("st_gn2", [128, 64], F32)
        st_fin = sb("st_fin", [128, 64], F32)
        st_fx = sb("st_fx", [128, 64], F32)
        st_eps = sb("st_eps", [128, 64], F32)
        ps = [es.enter_context(nc.psum_tensor("ps%d" % i, [128, 512], F32)) for i in range(8)]

        sems = {e: sem("s_" + e) for e in Sched.ENGS}
        ch_ring = [Chan(sem("c_ring%d" % i)) for i in range(NRING)]
        ch_ringc = [Chan(sem("c_ringc%d" % i)) for i in range(NRING)]
        ch_x = [Chan(sem("c_x%d" % i)) for i in range(2)]
        ch_cs = [Chan(sem("c_cs%d" % i)) for i in range(2)]
        ch_sn = [Chan(sem("c_sn%d" % i)) for i in range(2)]
        ch_out = [Chan(sem("c_out%d" % i)) for i in range(2)]
        ch_outl = [Chan(sem("c_outl%d" % i)) for i in range(NTB)]
        ch_stage = [Chan(sem("c_stage%d" % i)) for i in range(4)]
        ch_scr = [Chan(sem("c_scr%d" % i)) for i in range(NRING)]
        ch_const = [Chan(sem("c_const%d" % i)) for i in range(3)]

        aT = big[:, :].rearrange("p (j t) -> p j t", j=32)
        vv = big[:, 0:4096].rearrange("p (b c) -> p b c", b=NTB)
        vz = big[:, 4096:8192].rearrange("p (b c) -> p b c", b=NTB)
        sg = big[:, 8192:12288].rearrange("p (b c) -> p b c", b=NTB)
        gatedT = big[:, 12288:16384].rearrange("p (k t) -> p k t", k=8)
        stage = [big[:, i * 8192:(i + 1) * 8192].bitcast(F32) for i in range(2)]
        mixedT = hT
        h2T = gat[:, 0:8, :]
        cmask = cf[:, 0:128]
        vzs = cf[:, 128:132]
        bgT = cf[:, 132:148]
        g_mix = cf[:, 148:156]
        g_gn = cf[:, 156:164]
        g_pool = cf[:, 164:168]
        g_mlp = cf[:, 168:176]
        invcnt = cf[:, 176:240].rearrange("p (g j) -> p g j", g=4)
        lnf = cf[:, 512:1536]
        ident = cbt[:, 0:128]
        diagq = cbt[:, 128:640].rearrange("p (h c) -> p h c", h=H)
        diagk = cbt[:, 640:1152].rearrange("p (h c) -> p h c", h=H)
        ssq = st_rms[:, 0:4]
        rv = st_rms[:, 4:8]
        ry = st_rms[:, 8:12]
        rt = st_rms[:, 12:16]
        rh = st_rms[:, 16:20]
        bst = st_gn[:, 0:24].rearrange("p (h s) -> p h s", h=H)
        mv = st_gn[:, 24:32].rearrange("p (h s) -> p h s", h=H)
        nmr = st_gn2[:, 0:4]
        epsc = st_eps[:, 0:1]

        bank_ctr = [0]

        def bank():
            b = ps[bank_ctr[0] % 8]
            bank_ctr[0] += 1
            return b

        def A(eng, fn, reads, writes, chan=None):
            return S.add(eng, fn, reads=reads, writes=writes, chan=chan)

        def mm(out, lhsT, rhs, start, stop):
            A("pe", lambda e: e.matmul(out, lhsT=lhsT, rhs=rhs, start=start, stop=stop), [lhsT, rhs], [out])

        def rsqrt_chain(v, y, t, hh):
            A("dve", lambda e: e.tensor_single_scalar(out=t.bitcast(I32), in_=v.bitcast(I32), scalar=1,
                                                      op=ALU.arith_shift_right), [v], [t])
            A("dve", lambda e: e.tensor_scalar(out=y.bitcast(I32), in0=t.bitcast(I32), scalar1=-1.0, scalar2=MAGIC,
                                               op0=ALU.mult, op1=ALU.add), [t], [y])
            A("dve", lambda e: e.tensor_scalar(out=hh, in0=v, scalar1=-0.5, scalar2=None, op0=ALU.mult), [v], [hh])
            for _ in range(2):
                A("dve", lambda e: e.tensor_tensor(out=t, in0=y, in1=y, op=ALU.mult), [y], [t])
                A("dve", lambda e: e.scalar_tensor_tensor(out=t, in0=t, scalar=1.0, in1=hh, op0=ALU.mult, op1=ALU.mult),
                  [t, hh], [t])
                A("dve", lambda e: e.scalar_tensor_tensor(out=y, in0=t, scalar=1.5, in1=y, op0=ALU.add, op1=ALU.mult),
                  [t, y], [y])

        A("pool", lambda e: e.dma_start(out=cf[:], in_=cf_d), [], [cf[:]], ch_const[0])
        A("pool", lambda e: e.dma_start(out=cbt[:], in_=cb_d), [], [cbt[:]], ch_const[1])
        A("pool", lambda e: e.dma_start(out=pwf[:], in_=pw_d), [], [pwf[:]], ch_const[2])
        A("dve", lambda e: e.tensor_copy(out=pwb[:].rearrange("p g d -> p (g d)"), in_=pwf[:]), [pwf[:]], [pwb[:]])
        A("dve", lambda e: e.memset(St[:], 0.0), [], [St[:]])
        A("dve", lambda e: e.memset(Sbf[0][:], 0.0), [], [Sbf[0][:]])
        A("dve", lambda e: e.memset(uh[:], 0.0), [], [uh[:]])
        A("dve", lambda e: e.memset(epsc, EPS), [], [epsc])

        S.mark('consts_done')
        def chunk_gain(j):
            if j <= 10:
                return g_mix, 8
            if j in (11, 13):
                return g_gn, 8
            if j == 12:
                return g_pool, 4
            if 16 <= j <= 23:
                return g_mlp, 8
            return None, 1

        S.mark('prologue_done')
        issued = [0]
        TOTAL = NT * NCH

        stg = [xs[1][:, p, :] for p in range(4)]

        def store_chunk(j):
            sl = j % NRING
            A("sp", lambda e: e.dma_start(out=wscr_d[j], in_=ring[sl][:]), [ring[sl][:]], [("dram_wscr", j)], ch_scr[sl])

        def stage_chunk(j):
            sl = j % NRING
            dst = ring[sl]
            gain, nk = chunk_gain(j)
            if gain is None:
                A("pool", lambda e: e.dma_start(out=dst[:], in_=wall_d[j]), [], [dst[:]], ch_ringc[sl])
                if j > 0:
                    store_chunk(j - 1)
                if j == NCH - 1:
                    store_chunk(j)
                return
            for p in range(4):
                A("sp", lambda e, p=p: e.dma_start(out=stg[p], in_=wall_d[j][:, p * 1024:(p + 1) * 1024]), [], [stg[p]], ch_stage[p])
            for p in range(4):
                eng = "act" if p % 2 == 0 else "dve"
                if gain is None:
                    pieces = [(stg[p], dst[:, p * 1024:(p + 1) * 1024], None)]
                else:
                    w = 4096 // nk
                    per = nk // 4
                    pieces = []
                    for i in range(per):
                        kc = p * per + i
                        pieces.append((stg[p][:, i * w:(i + 1) * w], dst[:, kc * w:(kc + 1) * w], gain[:, kc:kc + 1]))
                for src, d_, gcol in pieces:
                    if gcol is None:
                        if eng == "act":
                            A("act", lambda e, src=src, d_=d_: e.activation(out=d_, in_=src, func=AF.Copy), [src], [d_])
                        else:
                            A("dve", lambda e, src=src, d_=d_: e.tensor_copy(out=d_, in_=src), [src], [d_])
                    elif eng == "act":
                        A("act", lambda e, src=src, d_=d_, gcol=gcol: e.activation(out=d_, in_=src, func=AF.Copy, scale=gcol),
                          [src, gcol], [d_])
                    else:
                        A("dve", lambda e, src=src, d_=d_, gcol=gcol: e.tensor_scalar(out=d_, in0=src, scalar1=gcol, scalar2=None,
                                                                                     op0=ALU.mult), [src, gcol], [d_])
            if j > 0:
                store_chunk(j - 1)
            if j == NCH - 1:
                store_chunk(j)

        def prefetch_upto(n):
            while issued[0] <= min(n, TOTAL - 1):
                g = issued[0]
                j = g % NCH
                sl = g % NRING
                if g < NCH:
                    stage_chunk(j)
                else:
                    A("sp", lambda e, j=j, sl=sl: e.dma_start(out=ring[sl][:], in_=wscr_d[j]), [("dram_wscr", j)], [ring[sl][:]], ch_ring[sl])
                issued[0] += 1

        D0 = 4

        x1_loaded = [False]

        def pf(g):
            prefetch_upto(g + (NRING if g + NRING >= NCH else D0))
            if NT > 1 and issued[0] >= 24 and not x1_loaded[0]:
                x1_loaded[0] = True
                load_x(1)

        def wchunk(g):
            return ring[g % NRING][:, :].rearrange("p (k c) -> p k c", k=8)

        def load_x(t):
            sl = t % 2
            A("pool", lambda e: e.dma_start(out=xs[sl][:], in_=x_d[t * TT:(t + 1) * TT, :].rearrange("(c p) d -> p c d", p=128)),
              [], [xs[sl][:]], ch_x[sl])
            A("pool", lambda e: e.dma_start(out=cst[sl][:, 0], in_=cos_d[t * TT:(t + 1) * TT, :].rearrange("(c p) f -> p c f", p=128)),
              [], [cst[sl][:, 0]], ch_cs[sl])
            A("pool", lambda e: e.dma_start(out=cst[sl][:, 1], in_=sin_d[t * TT:(t + 1) * TT, :].rearrange("(c p) f -> p c f", p=128)),
              [], [cst[sl][:, 1]], ch_sn[sl])

        def norm_a(xsl, groups=None):
            if groups is None:
                groups = [list(range(NTB))]
            for grp in groups:
                lo, hi = grp[0], grp[-1] + 1
                for tb in grp:
                    A("act", lambda e, tb=tb: e.activation(out=junk[:], in_=xsl[:, tb, :], func=AF.Square, accum_out=ssq[:, tb:tb + 1]),
                      [xsl[:, tb, :]], [junk[:], ssq[:, tb:tb + 1]])
                A("act", lambda e, lo=lo, hi=hi: e.activation(out=rv[:, lo:hi], in_=ssq[:, lo:hi], func=AF.Sqrt, scale=1.0 / D, bias=epsc),
                  [ssq[:, lo:hi], epsc], [rv[:, lo:hi]])
                A("dve", lambda e, lo=lo, hi=hi: e.reciprocal(out=ry[:, lo:hi], in_=rv[:, lo:hi]), [rv[:, lo:hi]], [ry[:, lo:hi]])
                for tb in grp:
                    xb = xn[tb]
                    A("dve", lambda e, tb=tb, xb=xb: e.tensor_scalar(out=xb[:], in0=xsl[:, tb, :], scalar1=ry[:, tb:tb + 1], scalar2=None,
                                                                   op0=ALU.mult), [xsl[:, tb, :], ry[:, tb:tb + 1]], [xb[:]])

        def norm_b(dstT, tbs=None):
            for tb in (range(NTB) if tbs is None else tbs):
                xb = xn[tb]
                pb = bank()
                pbv = pb[:].bitcast(BF16).rearrange("p (k c) -> p k c", k=8)
                for kc in range(8):
                    A("pe", lambda e, kc=kc, xb=xb, pbv=pbv: e.transpose(out=pbv[:, kc, :], in_=xb[:, kc * 128:(kc + 1) * 128], identity=ident),
                      [xb[:, kc * 128:(kc + 1) * 128], ident], [pbv[:, kc, :]])
                dd = dstT[:, :, tb * 128:(tb + 1) * 128]
                if tb % 2 == 0:
                    A("dve", lambda e, dd=dd, pbv=pbv: e.tensor_copy(out=dd, in_=pbv), [pbv], [dd])
                else:
                    A("act", lambda e, dd=dd, pbv=pbv: e.activation(out=dd, in_=pbv, func=AF.Copy), [pbv], [dd])

        def tokmajor_chunk(g, srcT, evac):
            w = wchunk(g)
            for tb in range(NTB):
                pb = bank()
                for kc in range(8):
                    mm(pb[:], srcT[:, kc, tb * 128:(tb + 1) * 128], w[:, kc, :], kc == 0, kc == 7)
                evac(tb, pb)
            pf(g)

        def featmajor_chunk(g, srcT, evac, nk=8):
            w = wchunk(g)
            for cb in range(4):
                pb = bank()
                for kc in range(nk):
                    mm(pb[:], w[:, kc, cb * 128:(cb + 1) * 128], srcT[:, kc, :], kc == 0, kc == nk - 1)
                evac(cb, pb)
            pf(g)

        def mixer(t, hookA=None, hookB=None):
            g0 = t * NCH
            xsl = xs[t % 2]
            cs = cst[t % 2]

            def rotary(pb, tb, dst):
                pv = pb[:].rearrange("p (h t f) -> p h t f", h=H, t=2)
                t1 = scr[:, (tb % 2) * 1024:(tb % 2) * 1024 + 512]
                t2 = scr[:, (tb % 2) * 1024 + 512:(tb % 2) * 1024 + 1024]
                t1v = t1.rearrange("p (h t f) -> p h t f", h=H, t=2)
                t2v = t2.rearrange("p (h t f) -> p h t f", h=H, t=2)
                dv = dst[:].rearrange("p (h t f) -> p h t f", h=H, t=2)
                cosb = cs[:, 0, tb, :].unsqueeze(1).unsqueeze(1).broadcast_to([128, H, 2, 64])
                sinb = cs[:, 1, tb, :].unsqueeze(1).broadcast_to([128, H, 64])
                A("dve", lambda e: e.tensor_tensor(out=t1v, in0=pv, in1=cosb, op=ALU.mult), [pb[:], cs[:, 0, tb, :]], [t1])
                A("dve", lambda e: e.tensor_tensor(out=t2v[:, :, 0, :], in0=pv[:, :, 1, :], in1=sinb, op=ALU.mult),
                  [pb[:], cs[:, 1, tb, :]], [t2])
                A("dve", lambda e: e.tensor_tensor(out=t2v[:, :, 1, :], in0=pv[:, :, 0, :], in1=sinb, op=ALU.mult),
                  [pb[:], cs[:, 1, tb, :]], [t2])
                A("pool", lambda e: e.tensor_tensor(out=dv[:, :, 0, :], in0=t1v[:, :, 0, :], in1=t2v[:, :, 0, :], op=ALU.subtract),
                  [t1, t2], [dst[:]])
                A("pool", lambda e: e.tensor_tensor(out=dv[:, :, 1, :], in0=t1v[:, :, 1, :], in1=t2v[:, :, 1, :], op=ALU.add),
                  [t1, t2], [dst[:]])

            def diag_T(src, dg, dst):
                pb = bank()
                pv = pb[:].rearrange("p (h c) -> p h c", h=H)
                for h in range(H):
                    mm(pv[:, h, :], src[:, h * 128:(h + 1) * 128], dg[:, h, :], True, True)
                A("dve", lambda e: e.tensor_copy(out=dst[:], in_=pv), [pb[:]], [dst[:]])

            def ev_q(tb, pb):
                rotary(pb, tb, qrot[tb])

            def ev_k(tb, pb):
                rotary(pb, tb, krot[tb])

            def ev_v(i):
                def f(tb, pb):
                    A("dve", lambda e: e.tensor_copy(out=vv[:, tb, i * 512:(i + 1) * 512], in_=pb[:]), [pb[:]],
                      [vv[:, tb, i * 512:(i + 1) * 512]])
                    for hl in range(2):
                        h = 2 * i + hl
                        o_ = vz[:, tb, h * 256:(h + 1) * 256]
                        A("act", lambda e, o_=o_, hl=hl, h=h: e.activation(out=o_, in_=pb[:, hl * 256:(hl + 1) * 256], func=AF.Copy,
                                                                         scale=vzs[:, h:h + 1]), [pb[:], vzs], [o_])
                return f

            def ev_g(i):
                def f(tb, pb):
                    o_ = sg[:, tb, i * 512:(i + 1) * 512]
                    A("act", lambda e: e.activation(out=o_, in_=pb[:], func=AF.Silu), [pb[:]], [o_])
                return f

            def ev_pu(cb, pb):
                o_ = uh[:, cb, 16:528]
                A("act", lambda e: e.activation(out=o_, in_=pb[:], func=AF.Copy), [pb[:]], [o_])

            def ev_gate(i):
                def f(cb, pb):
                    blk = i * 4 + cb
                    o_ = gat[:, blk, :]
                    A("act", lambda e: e.activation(out=o_, in_=pb[:], func=AF.Sigmoid, bias=bgT[:, blk:blk + 1]),
                      [pb[:], bgT], [o_])
                return f

            def R2(tb):
                pb = bank()
                pv = pb[:].rearrange("p (h c) -> p h c", h=H)
                for h in range(H):
                    mm(pv[:, h, :], kT[tb][:, h, :], qT[tb][:, h, :], True, True)
                sTb = sT[tb % 2]
                cmb = cmask.unsqueeze(1).broadcast_to([128, H, 128])
                A("dve", lambda e: e.tensor_tensor(out=sTb[:], in0=pv, in1=cmb, op=ALU.mult), [pb[:], cmask], [sTb[:]])
                for hp in range(2):
                    pk = bank()
                    for hl in range(2):
                        h = hp * 2 + hl
                        mm(pk[:, hl * 256:(hl + 1) * 256], krot[tb][:, h * 128:(h + 1) * 128], vz[:, tb, h * 256:(h + 1) * 256], True, True)
                    for hl in range(2):
                        h = hp * 2 + hl
                        A("dve", lambda e, h=h, hl=hl, pk=pk: e.scalar_tensor_tensor(out=St[:, h, :], in0=St[:, h, :], scalar=float(GAM[h] ** 128),
                                                                                    in1=pk[:, hl * 256:(hl + 1) * 256], op0=ALU.mult, op1=ALU.add),
                          [St[:, h, :], pk[:, hl * 256:(hl + 1) * 256]], [St[:, h, :]])
                nb = Sbf[(tb + 1) % 2]
                A("act", lambda e: e.activation(out=nb[:], in_=St[:], func=AF.Copy), [St[:]], [nb[:]])

            def R3a(tb):
                sTb = sT[tb % 2]
                sb_ = Sbf[tb % 2]
                pbs = []
                for hp in range(2):
                    po = bank()
                    pbs.append(po)
                    for hl in range(2):
                        h = hp * 2 + hl
                        o_ = po[:, hl * 256:(hl + 1) * 256]
                        mm(o_, sTb[:, h, :], vv[:, tb, h * 256:(h + 1) * 256], True, False)
                        mm(o_, qT[tb][:, h, :], sb_[:, h, :], False, True)
                for h in range(H):
                    src = pbs[h // 2][:, (h % 2) * 256:(h % 2 + 1) * 256]
                    A("dve", lambda e, h=h, src=src: e.bn_stats(out=bst[:, h, :], in_=src), [src], [bst[:, h, :]])
                    A("dve", lambda e, h=h: e.bn_aggr(out=mv[:, h, :], in_=bst[:, h, :]), [bst[:, h, :]], [mv[:, h, :]])
                gv = st_gn2[:, 4:8]
                gy = st_gn2[:, 8:12]
                gt = st_gn2[:, 12:16]
                gh = st_gn2[:, 16:20]
                A("dve", lambda e: e.tensor_scalar(out=gv, in0=mv[:, :, 1], scalar1=EPS, scalar2=None, op0=ALU.add), [mv], [gv])
                rsqrt_chain(gv, gy, gt, gh)
                A("dve", lambda e: e.scalar_tensor_tensor(out=nmr, in0=mv[:, :, 0], scalar=-1.0, in1=gy, op0=ALU.mult, op1=ALU.mult),
                  [mv, gy], [nmr])
                gb = gated[tb % 2]
                for h in range(H):
                    src = pbs[h // 2][:, (h % 2) * 256:(h % 2 + 1) * 256]
                    tmp = scr[:, 1024 + (h % 2) * 256:1024 + (h % 2 + 1) * 256]
                    A("act", lambda e, h=h, src=src, tmp=tmp: e.activation(out=tmp, in_=src, func=AF.Identity, scale=gy[:, h:h + 1],
                                                                         bias=nmr[:, h:h + 1]), [src, gy, nmr], [tmp])
                    o_ = gb[:, h * 256:(h + 1) * 256]
                    A("dve", lambda e, h=h, tmp=tmp, o_=o_: e.tensor_tensor(out=o_, in0=tmp, in1=sg[:, tb, h * 256:(h + 1) * 256], op=ALU.mult),
                      [tmp, sg[:, tb, h * 256:(h + 1) * 256]], [o_])

            def R3b(tb):
                gb = gated[tb % 2]
                pb = bank()
                pbv = pb[:].bitcast(BF16).rearrange("p (k c) -> p k c", k=8)
                for kc in range(8):
                    A("pe", lambda e, kc=kc, pbv=pbv: e.transpose(out=pbv[:, kc, :], in_=gb[:, kc * 128:(kc + 1) * 128], identity=ident),
                      [gb[:, kc * 128:(kc + 1) * 128], ident], [pbv[:, kc, :]])
                dd = gatedT[:, :, tb * 128:(tb + 1) * 128]
                A("dve", lambda e: e.tensor_copy(out=dd, in_=pbv), [pbv], [dd])

            def pooling():
                for g in range(4):
                    cur = uh[:, g, :]
                    src = cur
                    for lv in range(g + 1):
                        sh = 1 << lv
                        lo = 2 * sh - 1
                        dst = ptmp[lv % 2]
                        A("pool", lambda e, src=src, dst=dst, sh=sh, lo=lo: e.tensor_tensor(out=dst[:, lo:528], in0=src[:, lo:528],
                                                                                            in1=src[:, lo - sh:528 - sh], op=ALU.add),
                          [src[:, lo - sh:528]], [dst[:, lo:528]])
                        src = dst
                    wd = float(1 << (g + 1))
                    other = ptmp[(g + 1) % 2]
                    iw = invcnt[:, g, 15:16].broadcast_to([128, 512])
                    A("pool", lambda e, g=g, src=src, other=other, iw=iw: e.tensor_tensor(out=other[:, 16:528], in0=src[:, 16:528], in1=iw, op=ALU.mult),
                      [src[:, 16:528], invcnt[:, g, 15:16]], [other[:, 16:528]])
                    A("pool", lambda e, g=g, other=other: e.tensor_tensor(out=pbf[:, g, :], in0=other[:, 16:528], in1=uh[:, g, 16:528], op=ALU.subtract),
                      [other[:, 16:528], uh[:, g, 16:528]], [pbf[:, g, :]])
                    if t == 0:
                        fx = st_fx[:, 0:16]
                        A("pool", lambda e, g=g, src=src: e.tensor_tensor(out=fx, in0=src[:, 16:32], in1=invcnt[:, g, :], op=ALU.mult),
                          [src[:, 16:32], invcnt[:, g, :]], [fx])
                        A("pool", lambda e, g=g: e.tensor_tensor(out=pbf[:, g, 0:16], in0=fx, in1=uh[:, g, 16:32], op=ALU.subtract),
                          [fx, uh[:, g, 16:32]], [pbf[:, g, 0:16]])
                A("pool", lambda e: e.tensor_copy(out=uh[:, :, 0:16], in_=uh[:, :, 512:528]), [uh[:, :, 512:528]], [uh[:, :, 0:16]])

            tokmajor_chunk(g0 + 0, hT, ev_q)
            tokmajor_chunk(g0 + 1, hT, ev_k)
            for tb in range(NTB):
                diag_T(qrot[tb], diagq, qT[tb])
            tokmajor_chunk(g0 + 2, hT, ev_v(0))
            for tb in range(NTB):
                diag_T(krot[tb], diagk, kT[tb])
            tokmajor_chunk(g0 + 3, hT, ev_v(1))
            if hookA is not None:
                hookA()
            tokmajor_chunk(g0 + 4, hT, ev_g(0))
            R2(0)
            tokmajor_chunk(g0 + 5, hT, ev_g(1))
            R3a(0)
            R2(1)
            featmajor_chunk(g0 + 6, hT, ev_pu)
            pooling()
            R3a(1)
            R2(2)
            featmajor_chunk(g0 + 7, hT, ev_gate(0))
            R3b(0)
            R3a(2)
            R2(3)
            featmajor_chunk(g0 + 8, hT, ev_gate(1))
            R3b(1)
            R3a(3)
            featmajor_chunk(g0 + 9, hT, ev_gate(2))
            R3b(2)
            featmajor_chunk(g0 + 10, hT, ev_gate(3))
            R3b(3)

            S.mark('win_done%d' % t)
            for g in range(4):
                pb = bank()
                mm(pb[:], pwb[:, g, :], pbf[:, g, :], True, True)
                A("act", lambda e, g=g, pb=pb: e.activation(out=ypT[:, g, :], in_=pb[:], func=AF.Copy), [pb[:]], [ypT[:, g, :]])

            S.mark('poolbr_done%d' % t)
            wret0 = wchunk(g0 + 11)
            wpool = ring[(g0 + 12) % NRING][:, :].rearrange("p (k c) -> p k c", k=4)
            wret1 = wchunk(g0 + 13)
            pr = {}
            pp = {}

            def ret_blk(w, cb):
                pb = bank()
                pr[cb] = pb
                for kc in range(8):
                    mm(pb[:], w[:, kc, (cb % 4) * 128:(cb % 4 + 1) * 128], gatedT[:, kc, :], kc == 0, kc == 7)

            def pool_blk(cb):
                pb = bank()
                pp[cb] = pb
                for kc in range(4):
                    mm(pb[:], wpool[:, kc, cb * 128:(cb + 1) * 128], ypT[:, kc, :], kc == 0, kc == 3)

            def mix(cb):
                m1 = scr[:, (cb % 2) * 1024:(cb % 2) * 1024 + 512]
                m2 = scr[:, (cb % 2) * 1024 + 512:(cb % 2) * 1024 + 1024]
                A("dve", lambda e: e.tensor_tensor(out=m1, in0=pr[cb][:], in1=gat[:, cb, :], op=ALU.mult), [pr[cb][:], gat[:, cb, :]], [m1])
                A("dve", lambda e: e.tensor_tensor(out=m2, in0=pp[cb][:], in1=gat[:, 8 + cb, :], op=ALU.mult),
                  [pp[cb][:], gat[:, 8 + cb, :]], [m2])
                A("pool", lambda e: e.tensor_tensor(out=mixedT[:, cb, :], in0=m1, in1=m2, op=ALU.add), [m1, m2], [mixedT[:, cb, :]])

            for cb in range(4):
                ret_blk(wret0, cb)
                pool_blk(cb)
                mix(cb)
            pf(g0 + 11)
            for cb in range(4, 8):
                pool_blk(cb)
                if cb == 7:
                    pf(g0 + 12)
                ret_blk(wret1, cb)
                mix(cb)
            pf(g0 + 13)

            S.mark('merge_done%d' % t)
            if hookB is not None:
                hookB()
            wo = [wchunk(g0 + 14), wchunk(g0 + 15)]
            for tb in range(NTB):
                for i in range(2):
                    pb = bank()
                    for kc in range(8):
                        mm(pb[:], mixedT[:, kc, tb * 128:(tb + 1) * 128], wo[i][:, kc, :], kc == 0, kc == 7)
                    o_ = xsl[:, tb, i * 512:(i + 1) * 512]
                    A("dve", lambda e, o_=o_, pb=pb: e.tensor_tensor(out=o_, in0=pb[:], in1=o_, op=ALU.add), [pb[:], o_], [o_])
                if tb >= 2:
                    norm_b(h2T, [tb - 2])
                norm_a(xsl, [[tb]])
            pf(g0 + 14)
            pf(g0 + 15)

        def ffn_up(t):
            g0 = t * NCH
            xsl = xs[t % 2]

            def relu2(n, pb):
                rtmp = scr[:, (n % 2) * 512:(n % 2 + 1) * 512]
                A("act", lambda e: e.activation(out=rtmp, in_=pb[:], func=AF.Relu), [pb[:]], [rtmp])
                eng = "pool" if n % 2 == 0 else "dve"
                A(eng, lambda e: e.tensor_tensor(out=aT[:, n, :], in0=rtmp, in1=rtmp, op=ALU.mult), [rtmp], [aT[:, n, :]])

            w0, w1 = wchunk(g0 + 16), wchunk(g0 + 17)
            early = [(0, w0, cb) for cb in range(4)] + [(1, w1, cb) for cb in range(2)]
            banks = []
            for (j, w, cb) in early:
                pb = bank()
                banks.append(pb)
                for kc in range(8):
                    mm(pb[:, 0:256], w[:, kc, cb * 128:(cb + 1) * 128], h2T[:, kc, 0:256], kc == 0, kc == 7)
            norm_b(h2T, [2, 3])
            for (j, w, cb), pb in zip(early, banks):
                for kc in range(8):
                    mm(pb[:, 256:512], w[:, kc, cb * 128:(cb + 1) * 128], h2T[:, kc, 256:512], kc == 0, kc == 7)
                relu2(j * 4 + cb, pb)
                if (j, cb) == (0, 3):
                    pf(g0 + 16)
            for cb in range(2, 4):
                pb = bank()
                for kc in range(8):
                    mm(pb[:], w1[:, kc, cb * 128:(cb + 1) * 128], h2T[:, kc, :], kc == 0, kc == 7)
                relu2(4 + cb, pb)
            pf(g0 + 17)
            for j in range(2, 8):
                def ev_u(cb, pb, j=j):
                    relu2(j * 4 + cb, pb)
                featmajor_chunk(g0 + 16 + j, h2T, ev_u)
                if j == 3 and 1 <= t < NT - 1:
                    norm_a(xs[(t + 1) % 2])

        x1_normed = [False]

        def ffn_down(t):
            g0 = t * NCH
            xsl = xs[t % 2]
            for half in range(2):
                pbs = [bank() for _ in range(NTB)]
                for kg in range(4):
                    g = g0 + 24 + half * 4 + kg
                    w = wchunk(g)
                    for tb in range(NTB):
                        for kcl in range(8):
                            kc = kg * 8 + kcl
                            mm(pbs[tb][:], aT[:, kc, tb * 128:(tb + 1) * 128], w[:, kcl, :], kc == 0, kc == 31)
                    pf(g)
                    if t == 0 and NT > 1 and x1_loaded[0] and not x1_normed[0]:
                        x1_normed[0] = True
                        norm_a(xs[1])
                for tb in range(NTB):
                    o_ = xsl[:, tb, half * 512:(half + 1) * 512]
                    A("dve", lambda e, o_=o_, pb=pbs[tb]: e.tensor_tensor(out=o_, in0=pb[:], in1=o_, op=ALU.add), [pbs[tb][:], o_], [o_])
                    if t == NT - 1 and half == 1:
                        final(t, only_tb=tb)

        def final(t, only_tb=None):
            sl = t % 2
            xsl = xs[sl]
            fs = st_fin[:, 0:4]
            fv = st_fin[:, 4:8]
            fy = st_fin[:, 8:12]
            ft = st_fin[:, 12:16]
            fh = st_fin[:, 16:20]
            if t == NT - 1:
                for tb in ([only_tb] if only_tb is not None else range(NTB)):
                    c = slice(tb, tb + 1)
                    A("act", lambda e, tb=tb, c=c: e.activation(out=junk[:], in_=xsl[:, tb, :], func=AF.Square, accum_out=fs[:, c]),
                      [xsl[:, tb, :]], [junk[:], fs[:, c]])
                    A("act", lambda e, c=c: e.activation(out=fv[:, c], in_=fs[:, c], func=AF.Sqrt, scale=1.0 / D, bias=epsc),
                      [fs[:, c], epsc], [fv[:, c]])
                    A("dve", lambda e, c=c: e.reciprocal(out=fy[:, c], in_=fv[:, c]), [fv[:, c]], [fy[:, c]])
                    A("dve", lambda e, tb=tb, c=c: e.scalar_tensor_tensor(out=xsl[:, tb, :], in0=xsl[:, tb, :], scalar=fy[:, c], in1=lnf,
                                                                         op0=ALU.mult, op1=ALU.mult), [xsl[:, tb, :], fy[:, c], lnf], [xsl[:, tb, :]])
                    r0 = t * TT + tb * 128
                    A("pool", lambda e, tb=tb, r0=r0: e.dma_start(out=out_d[r0:r0 + 128, :], in_=xsl[:, tb, :]),
                      [xsl[:, tb, :]], [("out", t, tb)], ch_outl[tb])
                return
            for tb in range(NTB):
                A("act", lambda e, tb=tb: e.activation(out=junk[:], in_=xsl[:, tb, :], func=AF.Square, accum_out=fs[:, tb:tb + 1]),
                  [xsl[:, tb, :]], [junk[:], fs[:, tb:tb + 1]])
            A("act", lambda e: e.activation(out=fv, in_=fs, func=AF.Sqrt, scale=1.0 / D, bias=epsc), [fs, epsc], [fv])
            A("dve", lambda e: e.reciprocal(out=fy, in_=fv), [fv], [fy])
            for tb in range(NTB):
                A("dve", lambda e, tb=tb: e.scalar_tensor_tensor(out=xsl[:, tb, :], in0=xsl[:, tb, :], scalar=fy[:, tb:tb + 1], in1=lnf,
                                                                 op0=ALU.mult, op1=ALU.mult), [xsl[:, tb, :], fy[:, tb:tb + 1], lnf], [xsl[:, tb, :]])
            A("pool", lambda e: e.dma_start(out=out_d[t * TT:(t + 1) * TT, :].rearrange("(c p) d -> p c d", p=128), in_=xsl[:]),
              [xsl[:]], [("out", t)], ch_out[sl])

        load_x(0)
        prefetch_upto(0)
        norm_a(xs[0])
        norm_b(hT)
        prefetch_upto(D0 - 1)
        for t in range(NT):
            hookA = (lambda t=t: final(t - 1)) if t >= 2 else None
            hookB = (lambda t=t: load_x(t + 1)) if 1 <= t < NT - 1 else None
            mixer(t, hookA, hookB)
            S.mark('mixer_done%d' % t)
            ffn_up(t)
            S.mark('ffn_up_done%d' % t)
            if 1 <= t < NT - 1:
                norm_b(hT)
            ffn_down(t)
            S.mark('ffn_down_done%d' % t)
            if t == 0 and NT > 1:
                assert x1_normed[0]
                norm_b(hT)
                final(0)
            elif t == NT - 1:
                pass
        S.emit(nc, sems, final_chans=ch_out + ch_outl, limit=limit)
    return nc, S


def _host_consts():
    pos = np.arange(SEQ, dtype=np.float32)
    inv_freq = (10000.0 ** (-np.arange(64, dtype=np.float32) * 2.0 / 128.0)).astype(np.float32)
    ang = (pos[:, None] * inv_freq[None, :]).astype(np.float32)
    cosT = np.cos(ang.astype(np.float64)).astype(np.float32)
    sinT = np.sin(ang.astype(np.float64)).astype(np.float32)
    idx = np.arange(128, dtype=np.float64)
    gam = np.array(GAM, dtype=np.float64)
    cb = np.zeros((128, 1152), dtype=np.float32)
    cb[:, 0:128] = np.eye(128)
    for h in range(H):
        cb[:, 128 + h * 128:128 + (h + 1) * 128] = np.diag(gam[h] ** idx * DK ** -0.5)
        cb[:, 640 + h * 128:640 + (h + 1) * 128] = np.diag(gam[h] ** (-idx))
    cb = cb.astype(ml_dtypes.bfloat16)
    return cosT, sinT, cb


def _prep(x, ln_mix, w_in, b_gate, gn_gain, w_ret_up, pool_w, pool_scale, w_pool_up, w_o, ln_mlp,
           w_up, w_down, ln_final):
    x = np.asarray(x, dtype=np.float32)
    f32 = lambda a: np.asarray(a, dtype=np.float32)
    w_in, w_ret_up, w_pool_up, w_o, w_up, w_down = map(f32, (w_in, w_ret_up, w_pool_up, w_o, w_up, w_down))

    def rows8(w):
        return np.ascontiguousarray(w.reshape(8, 128, 512).transpose(1, 0, 2)).reshape(128, 4096)

    wall = np.empty((NCH, 128, 4096), dtype=np.float32)
    for j in range(11):
        wall[j] = rows8(w_in[0][:, j * 512:(j + 1) * 512])
    wall[11] = rows8(w_ret_up[0][:, 0:512])
    wall[12] = np.ascontiguousarray(w_pool_up[0].reshape(4, 128, 1024).transpose(1, 0, 2)).reshape(128, 4096)
    wall[13] = rows8(w_ret_up[0][:, 512:1024])
    wall[14] = rows8(w_o[0][:, 0:512])
    wall[15] = rows8(w_o[0][:, 512:1024])
    for j in range(8):
        wall[16 + j] = rows8(w_up[0][:, j * 512:(j + 1) * 512])
    for half in range(2):
        for kg in range(4):
            wall[24 + half * 4 + kg] = rows8(w_down[0][kg * 1024:(kg + 1) * 1024, half * 512:(half + 1) * 512])

    cosT, sinT, cb = _host_consts()
    idx = np.arange(128, dtype=np.float64)
    gam = np.array(GAM, dtype=np.float64)
    cf = np.zeros((128, 1536), dtype=np.float32)
    cf[:, 0:128] = (idx[None, :] >= idx[:, None]).astype(np.float32)
    cf[:, 128:132] = (gam[None, :] ** (128.0 - idx[:, None])).astype(np.float32)
    cf[:, 132:148] = f32(b_gate)[0].reshape(16, 128).T
    cf[:, 148:156] = f32(ln_mix)[0].reshape(8, 128).T
    cf[:, 156:164] = f32(gn_gain)[0].reshape(8, 128).T
    cf[:, 164:168] = f32(pool_scale)[0].reshape(4, 128).T
    cf[:, 168:176] = f32(ln_mlp)[0].reshape(8, 128).T
    for g in range(4):
        cf[:, 176 + g * 16:176 + (g + 1) * 16] = (1.0 / np.minimum(np.arange(16) + 1.0, 2.0 ** (g + 1)))[None, :]
    cf[:, 512:1536] = f32(ln_final)[None, :]
    pw = np.ascontiguousarray(f32(pool_w)[0].transpose(1, 0, 2)).reshape(128, 512)

    return dict(wall=wall, cosT=cosT, sinT=sinT, cf=cf, cb=cb, pw=pw)


def kernel(x, ln_mix, w_in, b_gate, gn_gain, w_ret_up, pool_w, pool_scale, w_pool_up, w_o, ln_mlp,
           w_up, w_down, ln_final):
    x = np.asarray(x, dtype=np.float32)
    shared = _prep(x, ln_mix, w_in, b_gate, gn_gain, w_ret_up, pool_w, pool_scale, w_pool_up, w_o, ln_mlp,
                   w_up, w_down, ln_final)
    nc, _ = build_program()
    in_maps = [dict(x=np.ascontiguousarray(x[b]), **shared) for b in range(NB)]
    res = run_bass_kernel_spmd(nc, in_maps, core_ids=list(range(NB)))
    return np.stack([np.asarray(r["out"], dtype=np.float32) for r in res.results], axis=0)
```
